# Optimizing a Trainium2 kernel written in Bass

```python
import math
import jax, jax.numpy as jnp
from jax import lax
import numpy as np

D_MODEL = 1024
BATCH = 32
SEQ = 2048
DEPTH = 1

EPS = 1e-6
D_FF = 2816
ATTN_HEADS = 8
ATTN_QK_DIM = 64
ATTN_V_DIM = 2 * ATTN_QK_DIM
ATTN_WIDTH = ATTN_HEADS * ATTN_V_DIM
QK_COLS = ATTN_HEADS * 2 * ATTN_QK_DIM
ROT_DIM = ATTN_QK_DIM // 4
ROPE_THETA = 500000.0
Q_BLOCK = 128
SSD_HEADS = 16
SSD_HEAD_DIM = 64
SSD_INNER = SSD_HEADS * SSD_HEAD_DIM
SSD_GROUPS = 2
SSD_STATE = 128
SSD_CONV = 5
SSD_CHUNK = 128
CONV_CH = SSD_INNER + 2 * SSD_GROUPS * SSD_STATE
D_MIX = ATTN_WIDTH + SSD_INNER
IN_SPLITS = (
    QK_COLS,
    2 * QK_COLS,
    2 * QK_COLS + ATTN_WIDTH,
    2 * QK_COLS + ATTN_WIDTH + SSD_INNER,
    2 * QK_COLS + ATTN_WIDTH + SSD_INNER + CONV_CH,
)
D_IN = 2 * QK_COLS + ATTN_WIDTH + SSD_INNER + CONV_CH + 2 * SSD_HEADS

kernel_name = "hybrid_diffattn_ssd_macaron_block"


def rmsnorm(x, g):
    xf = x.astype(jnp.float32)
    y = xf * lax.rsqrt(jnp.mean(xf * xf, axis=-1, keepdims=True) + EPS)
    return (y * g.astype(jnp.float32)).astype(x.dtype)


def swiglu(x, w_gate, w_up, w_down):
    return (jax.nn.silu(x @ w_gate) * (x @ w_up)) @ w_down


def rotary_tables(seq, dtype):
    pos = jnp.arange(seq, dtype=jnp.float32)
    inv_freq = jnp.power(jnp.float32(ROPE_THETA),
                         -jnp.arange(0, ROT_DIM, 2, dtype=jnp.float32) / ROT_DIM)
    ang = pos[:, None] * inv_freq[None, :]
    return jnp.cos(ang).astype(dtype), jnp.sin(ang).astype(dtype)


def apply_partial_rotary(t, cos, sin):
    half = ROT_DIM // 2
    c = cos[:, None, None, :]
    s = sin[:, None, None, :]
    x1 = t[..., :half]
    x2 = t[..., half:ROT_DIM]
    return jnp.concatenate([x1 * c - x2 * s, x2 * c + x1 * s, t[..., ROT_DIM:]], axis=-1)


def diff_attention(q, k, v, lam):
    b, s = q.shape[0], q.shape[1]
    nb = s // Q_BLOCK
    qb = q.reshape(b, nb, Q_BLOCK, ATTN_HEADS, 2, ATTN_QK_DIM).swapaxes(0, 1)
    scale = ATTN_QK_DIM ** -0.5

    def block(qi):
        sc = jnp.einsum("bqhcd,bkhcd->bhcqk", qi, k).astype(jnp.float32) * scale
        p = jax.nn.softmax(sc, axis=-1)
        p_diff = p[:, :, 0] - lam * p[:, :, 1]
        return jnp.einsum("bhqk,bkhe->bqhe", p_diff.astype(v.dtype), v)

    out = lax.map(block, qb)
    return out.swapaxes(0, 1).reshape(b, s, ATTN_HEADS, ATTN_V_DIM)


def centred_depthwise_conv(u, w, bias):
    out = lax.conv_general_dilated(
        u, w[:, None, :], window_strides=(1,),
        padding=[(SSD_CONV // 2, SSD_CONV // 2)],
        dimension_numbers=("NWC", "WIO", "NWC"),
        feature_group_count=u.shape[-1])
    return out + bias


def ssd_chunked(xh, dt, a, bm, cm):
    b, l, h, p = xh.shape
    g, n = bm.shape[2], bm.shape[3]
    r = h // g
    c = l // SSD_CHUNK
    L = SSD_CHUNK
    xdt = (xh * dt[..., None]).reshape(b, c, L, g, r, p)
    da = (dt * a).reshape(b, c, L, h).transpose(0, 1, 3, 2)
    cs = jnp.cumsum(da, axis=-1)
    bc = bm.reshape(b, c, L, g, n)
    cc = cm.reshape(b, c, L, g, n)
    mask = jnp.tril(jnp.ones((L, L), dtype=bool))
    diff = cs[..., :, None] - cs[..., None, :]
    decay = jnp.where(mask, jnp.exp(jnp.where(mask, diff, 0.0)), 0.0)
    decay = decay.reshape(b, c, g, r, L, L)
    cb = jnp.einsum("bclgn,bcsgn->bcgls", cc, bc)
    y_diag = jnp.einsum("bcgrls,bcsgrp->bclgrp", cb[:, :, :, None] * decay, xdt)
    decay_states = jnp.exp(cs[..., -1:] - cs).reshape(b, c, g, r, L)
    states = jnp.einsum("bclgn,bcgrl,bclgrp->bcgrpn", bc, decay_states, xdt)
    chunk_decay = jnp.exp(cs[..., -1]).reshape(b, c, g, r)

    def step(prev, inp):
        s_c, d_c = inp
        return prev * d_c[..., None, None] + s_c, prev

    init = jnp.zeros((b, g, r, p, n), dtype=states.dtype)
    _, prev_states = lax.scan(step, init,
                              (states.swapaxes(0, 1), chunk_decay.swapaxes(0, 1)))
    prev_states = prev_states.swapaxes(0, 1)
    state_decay_out = jnp.exp(cs).reshape(b, c, g, r, L)
    y_off = jnp.einsum("bclgn,bcgrpn,bcgrl->bclgrp", cc, prev_states, state_decay_out)
    return (y_diag + y_off).reshape(b, l, h, p)


def hybrid_mixer(hn, layer_idx, p):
    b, s, _ = hn.shape
    u = hn @ p["w_in"]
    q, k, v, z, xbc, dt_raw = jnp.split(u, IN_SPLITS, axis=-1)

    q = q.reshape(b, s, ATTN_HEADS, 2, ATTN_QK_DIM)
    k = k.reshape(b, s, ATTN_HEADS, 2, ATTN_QK_DIM)
    v = v.reshape(b, s, ATTN_HEADS, ATTN_V_DIM)
    cos, sin = rotary_tables(s, q.dtype)
    q = apply_partial_rotary(q, cos, sin)
    k = apply_partial_rotary(k, cos, sin)
    lam_init = 0.8 - 0.6 * math.exp(-0.3 * layer_idx)
    f32 = jnp.float32
    lam = (jnp.exp(jnp.sum(p["lambda_q1"].astype(f32) * p["lambda_k1"].astype(f32)))
           - jnp.exp(jnp.sum(p["lambda_q2"].astype(f32) * p["lambda_k2"].astype(f32)))
           + lam_init)
    attn = diff_attention(q, k, v, lam)
    attn = rmsnorm(attn, p["attn_subln_g"]) * (1.0 - lam_init)
    attn = attn.reshape(b, s, ATTN_WIDTH)

    xbc = jax.nn.silu(centred_depthwise_conv(xbc, p["conv_w"], p["conv_b"]))
    xs, bm, cm = jnp.split(xbc, (SSD_INNER, SSD_INNER + SSD_GROUPS * SSD_STATE), axis=-1)
    xs = xs.reshape(b, s, SSD_HEADS, SSD_HEAD_DIM)
    bm = bm.reshape(b, s, SSD_GROUPS, SSD_STATE)
    cm = cm.reshape(b, s, SSD_GROUPS, SSD_STATE)
    dt_f, dt_b = jnp.split(dt_raw.astype(f32), 2, axis=-1)
    dt_f = jax.nn.softplus(dt_f + p["dt_bias_fwd"].astype(f32))
    dt_b = jax.nn.softplus(dt_b + p["dt_bias_bwd"].astype(f32))
    a_f = -jnp.exp(p["a_log_fwd"].astype(f32))
    a_b = -jnp.exp(p["a_log_bwd"].astype(f32))
    y_f = ssd_chunked(xs, dt_f, a_f, bm, cm)
    rev = lambda t: jnp.flip(t, axis=1)
    y_b = rev(ssd_chunked(rev(xs), rev(dt_b), a_b, rev(bm), rev(cm)))
    y = y_f + y_b + xs * p["d_skip"][:, None]
    y = y.reshape(b, s, SSD_INNER).astype(hn.dtype)
    y = rmsnorm(y * jax.nn.silu(z), p["ssd_norm_g"])

    mixed = jnp.concatenate([attn, y], axis=-1)
    return mixed @ p["w_out"]


def hybrid_layer(h, layer_idx, p):
    f = swiglu(rmsnorm(h, p["ffn1_pre_g"]), p["ffn1_w_gate"], p["ffn1_w_up"], p["ffn1_w_down"])
    h = h + 0.5 * rmsnorm(f, p["ffn1_post_g"])
    m = hybrid_mixer(rmsnorm(h, p["mix_pre_g"]), layer_idx, p)
    h = h + rmsnorm(m, p["mix_post_g"])
    f = swiglu(rmsnorm(h, p["ffn2_pre_g"]), p["ffn2_w_gate"], p["ffn2_w_up"], p["ffn2_w_down"])
    h = h + 0.5 * rmsnorm(f, p["ffn2_post_g"])
    return rmsnorm(h, p["final_g"])


def setup_inputs(seed: int = 0) -> dict:
    key = jax.random.key(seed)
    ks = iter(jax.random.split(key, 40))
    f32 = jnp.float32

    def w(shape, fan_in):
        return jax.random.normal(next(ks), shape, f32) * (fan_in ** -0.5)

    def gain(shape):
        return 1.0 + 0.02 * jax.random.normal(next(ks), shape, f32)

    def small(shape, scale):
        return scale * jax.random.normal(next(ks), shape, f32)

    def a_log():
        return jnp.log(jax.random.uniform(next(ks), (DEPTH, SSD_HEADS), f32, 1.0, 16.0))

    def dt_bias():
        dt = jnp.exp(jax.random.uniform(next(ks), (DEPTH, SSD_HEADS), f32,
                                        math.log(1e-3), math.log(1e-1)))
        return dt + jnp.log(-jnp.expm1(-dt))

    Dp = DEPTH
    x = jax.random.normal(next(ks), (BATCH, SEQ, D_MODEL), f32)
    return {
        "x": x,
        "ffn1_pre_g": gain((Dp, D_MODEL)),
        "ffn1_w_gate": w((Dp, D_MODEL, D_FF), D_MODEL),
        "ffn1_w_up": w((Dp, D_MODEL, D_FF), D_MODEL),
        "ffn1_w_down": w((Dp, D_FF, D_MODEL), D_FF),
        "ffn1_post_g": gain((Dp, D_MODEL)),
        "mix_pre_g": gain((Dp, D_MODEL)),
        "w_in": w((Dp, D_MODEL, D_IN), D_MODEL),
        "lambda_q1": small((Dp, ATTN_QK_DIM), 0.1),
        "lambda_k1": small((Dp, ATTN_QK_DIM), 0.1),
        "lambda_q2": small((Dp, ATTN_QK_DIM), 0.1),
        "lambda_k2": small((Dp, ATTN_QK_DIM), 0.1),
        "attn_subln_g": gain((Dp, ATTN_V_DIM)),
        "conv_w": w((Dp, SSD_CONV, CONV_CH), SSD_CONV),
        "conv_b": small((Dp, CONV_CH), 0.02),
        "a_log_fwd": a_log(),
        "a_log_bwd": a_log(),
        "dt_bias_fwd": dt_bias(),
        "dt_bias_bwd": dt_bias(),
        "d_skip": gain((Dp, SSD_HEADS)),
        "ssd_norm_g": gain((Dp, SSD_INNER)),
        "w_out": w((Dp, D_MIX, D_MODEL), D_MIX),
        "mix_post_g": gain((Dp, D_MODEL)),
        "ffn2_pre_g": gain((Dp, D_MODEL)),
        "ffn2_w_gate": w((Dp, D_MODEL, D_FF), D_MODEL),
        "ffn2_w_up": w((Dp, D_MODEL, D_FF), D_MODEL),
        "ffn2_w_down": w((Dp, D_FF, D_MODEL), D_FF),
        "ffn2_post_g": gain((Dp, D_MODEL)),
        "final_g": gain((Dp, D_MODEL)),
    }


def reference(x, ffn1_pre_g, ffn1_w_gate, ffn1_w_up, ffn1_w_down, ffn1_post_g,
              mix_pre_g, w_in, lambda_q1, lambda_k1, lambda_q2, lambda_k2,
              attn_subln_g, conv_w, conv_b, a_log_fwd, a_log_bwd, dt_bias_fwd,
              dt_bias_bwd, d_skip, ssd_norm_g, w_out, mix_post_g, ffn2_pre_g,
              ffn2_w_gate, ffn2_w_up, ffn2_w_down, ffn2_post_g, final_g):
    h = x
    for i in range(DEPTH):
        p = {
            "ffn1_pre_g": ffn1_pre_g[i], "ffn1_w_gate": ffn1_w_gate[i],
            "ffn1_w_up": ffn1_w_up[i], "ffn1_w_down": ffn1_w_down[i],
            "ffn1_post_g": ffn1_post_g[i], "mix_pre_g": mix_pre_g[i],
            "w_in": w_in[i], "lambda_q1": lambda_q1[i], "lambda_k1": lambda_k1[i],
            "lambda_q2": lambda_q2[i], "lambda_k2": lambda_k2[i],
            "attn_subln_g": attn_subln_g[i], "conv_w": conv_w[i], "conv_b": conv_b[i],
            "a_log_fwd": a_log_fwd[i], "a_log_bwd": a_log_bwd[i],
            "dt_bias_fwd": dt_bias_fwd[i], "dt_bias_bwd": dt_bias_bwd[i],
            "d_skip": d_skip[i], "ssd_norm_g": ssd_norm_g[i], "w_out": w_out[i],
            "mix_post_g": mix_post_g[i], "ffn2_pre_g": ffn2_pre_g[i],
            "ffn2_w_gate": ffn2_w_gate[i], "ffn2_w_up": ffn2_w_up[i],
            "ffn2_w_down": ffn2_w_down[i], "ffn2_post_g": ffn2_post_g[i],
            "final_g": final_g[i],
        }
        h = hybrid_layer(h, i, p)
    return h
```

```python
import numpy as np
import concourse.bass as bass
import concourse.mybir as mybir
from concourse.bass_utils import run_bass_kernel_spmd

F32 = mybir.dt.float32
BF16 = mybir.dt.bfloat16
AF = mybir.ActivationFunctionType
ALU = mybir.AluOpType
AX = mybir.AxisListType

D = 1024
S = 2048
DFF = 2816
NJ = DFF // 128
DIN = 5664
EPS = 1e-6
NCORES = 8
G = 512
NG = S // G

SAME_ENGINE_SYNC = True


class _Op:
    __slots__ = ("eng", "fn", "deps", "dma", "dmacnt", "sig", "seq", "dmadeps", "strict")


class Prog:
    def __init__(self, nc):
        self.nc = nc
        self.ops = []
        self.last_w = {}
        self.readers = {}
        self.dma_cnt = {}

    def op(self, eng, fn, r=(), w=(), dma=None, strict=False):
        idx = len(self.ops)
        o = _Op()
        o.strict = strict
        o.eng = eng
        o.fn = fn
        o.dma = dma
        deps = set()
        for k in r:
            d = self.last_w.get(k)
            if d is not None:
                deps.add(d)
        for k in w:
            d = self.last_w.get(k)
            if d is not None:
                deps.add(d)
            for x in self.readers.get(k, ()):
                deps.add(x)
        deps.discard(idx)
        o.deps = []
        o.dmadeps = []
        for d in deps:
            od = self.ops[d]
            if od.dma is not None:
                o.dmadeps.append((od.dma, od.dmacnt))
            else:
                o.deps.append(d)
        if dma is not None:
            self.dma_cnt[dma] = self.dma_cnt.get(dma, 0) + 16
            o.dmacnt = self.dma_cnt[dma]
        else:
            o.dmacnt = 0
        for k in w:
            self.last_w[k] = idx
            self.readers[k] = []
        for k in r:
            lst = self.readers.setdefault(k, [])
            if dma is None:
                lst[:] = [x for x in lst if not (self.ops[x].eng == eng and self.ops[x].dma is None)]
            lst.append(idx)
        o.sig = False
        o.seq = 0
        self.ops.append(o)
        return idx

    def emit(self, stack):
        nc = self.nc
        ops = self.ops
        engs = {"pe": nc.tensor, "act": nc.scalar, "dve": nc.vector, "pool": nc.gpsimd, "sp": nc.sync}
        for o in ops:
            for d in o.deps:
                od = ops[d]
                if od.eng == o.eng and not o.strict:
                    if od.eng == "pe" or od.eng == "sp" or not SAME_ENGINE_SYNC:
                        continue
                od.sig = True
        cnt = {e: 0 for e in engs}
        for o in ops:
            if o.dma is None and o.sig:
                cnt[o.eng] += 1
                o.seq = cnt[o.eng]
        print("SEMCNT", cnt, "nops", len(ops), "dma", {k: v // 16 for k, v in self.dma_cnt.items() if v > 16 * 100})
        esem = {e: stack.enter_context(nc.semaphore("s_" + e)) for e in engs}
        dsem = {k: stack.enter_context(nc.semaphore("d_" + str(k))) for k in self.dma_cnt}
        waited = {e: {} for e in engs}
        for o in ops:
            E = engs[o.eng]
            wt = waited[o.eng]
            need = {}
            for d in o.deps:
                od = ops[d]
                if not od.sig:
                    continue
                if od.eng == o.eng and not o.strict and (od.eng in ("pe", "sp") or not SAME_ENGINE_SYNC):
                    continue
                if od.seq > need.get(od.eng, 0):
                    need[od.eng] = od.seq
            for e2, v in need.items():
                if wt.get(e2, 0) < v:
                    E.wait_ge(esem[e2], v)
                    wt[e2] = v
            for (k, c) in o.dmadeps:
                kk = ("dma", k)
                if wt.get(kk, 0) < c:
                    E.wait_ge(dsem[k], c)
                    wt[kk] = c
            ins = o.fn(E)
            if o.dma is not None:
                ins.then_inc(dsem[o.dma], 16)
            elif o.sig:
                ins.then_inc(esem[o.eng], 1)
        for k, c in self.dma_cnt.items():
            nc.sync.wait_ge(dsem[k], c)
        for e in engs:
            if e != "sp" and cnt[e] > 0:
                nc.sync.wait_ge(esem[e], cnt[e])


class Arena:
    def __init__(self, t32, words):
        self.t = t32
        self.words = words
        self.top = 0

    def alloc(self, shape, dtype):
        n = 1
        for s in shape:
            n *= s
        nbytes = n * (4 if dtype == F32 else 2)
        w = (nbytes + 3) // 4
        w = (w + 7) // 8 * 8
        off = self.top
        self.top += w
        assert self.top <= self.words, ("arena overflow", self.top, self.words)
        ap = self.t[:, off:off + w]
        if dtype != F32:
            ap = ap.bitcast(dtype)[:, 0:n]
        else:
            ap = ap[:, 0:n]
        if len(shape) == 2:
            return ap.rearrange("p (a b) -> p a b", a=shape[0])
        if len(shape) == 3:
            return ap.rearrange("p (a b c) -> p a b c", a=shape[0], b=shape[1])
        return ap

    def alloc_top(self, shape, dtype):
        n = 1
        for s in shape:
            n *= s
        nbytes = n * (4 if dtype == F32 else 2)
        w = (nbytes + 3) // 4
        w = (w + 7) // 8 * 8
        self.words -= w
        assert self.top <= self.words, ("arena overflow(top)", self.top, self.words)
        off = self.words
        ap = self.t[:, off:off + w]
        ap = ap.bitcast(dtype)[:, 0:n] if dtype != F32 else ap[:, 0:n]
        if len(shape) == 2:
            return ap.rearrange("p (a b) -> p a b", a=shape[0])
        if len(shape) == 3:
            return ap.rearrange("p (a b c) -> p a b c", a=shape[0], b=shape[1])
        return ap

    def mark(self):
        return self.top

    def release(self, m):
        self.top = m


NCONST = 128 * 4 + 2 * 16 * 8


def _const_pack():
    c = np.zeros((128, NCONST), np.float32)
    i = np.arange(128)
    c[:, 0:128] = np.eye(128, dtype=np.float32)
    c[:, 128:256] = (i[:, None] <= i[None, :]).astype(np.float32)
    c[:, 256:384] = (i[:, None] >= i[None, :]).astype(np.float32)
    c[:, 384:512] = 1.0
    pos = np.arange(S, dtype=np.float32)
    inv = np.power(np.float32(500000.0), -np.arange(0, 16, 2, dtype=np.float32) / np.float32(16)).astype(np.float32)
    ang = (pos[:, None] * inv[None, :]).astype(np.float32)
    cs = np.cos(ang).astype(np.float32).reshape(16, 128, 8).transpose(1, 0, 2).reshape(128, 128)
    sn = np.sin(ang).astype(np.float32).reshape(16, 128, 8).transpose(1, 0, 2).reshape(128, 128)
    c[:, 512:640] = cs
    c[:, 640:768] = sn
    return c


class Ctx:
    pass


def build_program(nseq, dbg=None, stages=("f1", "attn", "ssd", "o", "f2")):
    from contextlib import ExitStack
    nc = bass.Bass("TRN2", target_bir_lowering=False)
    T = nseq * S
    dr = lambda n, sh, dt=F32, kind="ExternalInput": nc.dram_tensor(n, sh, dt, kind=kind).ap()
    x = dr("x", [T, D])
    out = dr("out", [T, D], kind="ExternalOutput")
    consts = dr("consts", [128, NCONST])
    wnames = {"ffn1_w_gate": [D, DFF], "ffn1_w_up": [D, DFF], "ffn1_w_down": [DFF, D],
              "ffn2_w_gate": [D, DFF], "ffn2_w_up": [D, DFF], "ffn2_w_down": [DFF, D],
              "w_in": [D, DIN], "w_out": [2048, D]}
    wf = {n: dr(n, sh) for n, sh in wnames.items()}
    wb = {n: dr(n + "_bf", sh, BF16, kind="Internal") for n, sh in wnames.items()}
    vnames = {"ffn1_pre_g": 1024, "ffn1_post_g": 1024, "mix_pre_g": 1024, "mix_post_g": 1024,
              "ffn2_pre_g": 1024, "ffn2_post_g": 1024, "final_g": 1024, "ssd_norm_g": 1024,
              "attn_subln_g": 128, "lambda_q1": 64, "lambda_k1": 64, "lambda_q2": 64, "lambda_k2": 64,
              "a_log_fwd": 16, "a_log_bwd": 16, "dt_bias_fwd": 16, "dt_bias_bwd": 16,
              "d_skip": 16}
    vf = {n: dr(n, [1, k]) for n, k in vnames.items()}
    conv_w = dr("conv_w", [128, 60])
    conv_b = dr("conv_b", [128, 12])
    h1d = dr("h1_spill", [T, D], F32, kind="Internal")
    dbg_out = {}
    if dbg:
        for n, sh in dbg.items():
            if isinstance(sh, tuple):
                dbg_out[n] = dr("dbg_" + n, sh[0], sh[1], kind="ExternalOutput")
            else:
                dbg_out[n] = dr("dbg_" + n, sh, kind="ExternalOutput")

    with ExitStack() as st:
        AW = 53000
        arena_t = st.enter_context(nc.sbuf_tensor("arena", [128, AW], F32))
        ps = st.enter_context(nc.psum_tensor("ps", [128, 8, 512], F32))
        A = Arena(arena_t, AW)
        p = Prog(nc)
        C = Ctx()
        C.nc, C.p, C.A, C.ps = nc, p, A, ps
        uid = [0]

        def U(s):
            uid[0] += 1
            return "%s#%d" % (s, uid[0])

        live_regions = set()
        _orig_op = p.op

        def op(eng, fn, r=(), w=(), dma=None):
            for k in w:
                live_regions.add(k)
            for k in r:
                live_regions.add(k)
            return _orig_op(eng, fn, r=r, w=w, dma=dma)
        p.op = op

        def full_barrier():
            regs = list(live_regions)
            for e in ("pe", "act", "dve", "pool", "sp"):
                _orig_op(e, (lambda E: E.nop()), r=(), w=regs, strict=True)

        cst = A.alloc([NCONST], F32)
        p.op("sp", lambda e: e.dma_start(out=cst, in_=consts), w=["cst"], dma="cst")
        ident = A.alloc([128], BF16)
        p.op("dve", lambda e: e.tensor_copy(ident, cst[:, 0:128]), r=["cst"], w=["ident"])
        Uf32 = cst[:, 128:256]
        Ub32 = cst[:, 256:384]
        ones32 = cst[:, 384:512]
        maskf = A.alloc([128], BF16)
        maskb = A.alloc([128], BF16)
        p.op("dve", lambda e: e.tensor_copy(maskf, Uf32), r=["cst"], w=["maskf"])
        p.op("dve", lambda e: e.tensor_copy(maskb, Ub32), r=["cst"], w=["maskb"])
        cosT = cst[:, 512:640].rearrange("p (t f) -> p t f", t=16)
        sinT = cst[:, 640:768].rearrange("p (t f) -> p t f", t=16)
        negh = A.alloc([16], F32)
        p.op("pool", lambda e: e.memset(negh, -0.5), w=["negh"])

        def bcast_load(name, n, key=None):
            t = A.alloc([n], F32)
            k = key or ("v_" + name)
            p.op("sp", lambda e: e.dma_start(out=t, in_=vf[name].partition_broadcast(128)), w=[k], dma=k)
            return t, k

        gsub, k_gsub = bcast_load("attn_subln_g", 128)
        p.op("dve", lambda e: e.tensor_scalar(gsub, gsub, 1.0 - (0.8 - 0.6), None, op0=ALU.mult), r=[k_gsub], w=[k_gsub])
        lq1, k1 = bcast_load("lambda_q1", 64)
        lk1, k2 = bcast_load("lambda_k1", 64)
        lq2, k3 = bcast_load("lambda_q2", 64)
        lk2, k4 = bcast_load("lambda_k2", 64)
        lamt = A.alloc([8], F32)
        ljunk = A.alloc([64], F32)
        p.op("dve", lambda e: e.tensor_tensor(ljunk, lq1, lk1, op=ALU.mult), r=[k1, k2], w=["ljunk"])
        p.op("dve", lambda e: e.reduce_sum(lamt[:, 0:1], ljunk, axis=AX.X), r=["ljunk"], w=["lamt"])
        p.op("dve", lambda e: e.tensor_tensor(ljunk, lq2, lk2, op=ALU.mult), r=[k3, k4, "lamt"], w=["ljunk"])
        p.op("dve", lambda e: e.reduce_sum(lamt[:, 1:2], ljunk, axis=AX.X), r=["ljunk"], w=["lamt"])
        p.op("act", lambda e: e.activation(lamt[:, 0:2], lamt[:, 0:2], AF.Exp), r=["lamt"], w=["lamt"])
        p.op("dve", lambda e: e.tensor_tensor(lamt[:, 2:3], lamt[:, 0:1], lamt[:, 1:2], op=ALU.subtract), r=["lamt"], w=["lamt"])
        p.op("dve", lambda e: e.tensor_scalar(lamt[:, 2:3], lamt[:, 2:3], 0.8 - 0.6, None, op0=ALU.add), r=["lamt"], w=["lamt"])
        p.op("dve", lambda e: e.tensor_scalar(lamt[:, 3:4], lamt[:, 2:3], -1.0, None, op0=ALU.mult), r=["lamt"], w=["lamt"])
        alog = A.alloc([32], F32)
        p.op("sp", lambda e: e.dma_start(out=alog[:, 0:16], in_=vf["a_log_fwd"].partition_broadcast(128)), w=["alog"], dma="alog")
        p.op("sp", lambda e: e.dma_start(out=alog[:, 16:32], in_=vf["a_log_bwd"].partition_broadcast(128)), w=["alog"], dma="alog")
        aneg = A.alloc([32], F32)
        p.op("act", lambda e: e.activation(aneg, alog, AF.Exp), r=["alog"], w=["aneg"])
        p.op("dve", lambda e: e.tensor_scalar(aneg, aneg, -1.0, None, op0=ALU.mult), r=["aneg"], w=["aneg"])
        dtb = A.alloc([32], F32)
        p.op("sp", lambda e: e.dma_start(out=dtb[:, 0:16], in_=vf["dt_bias_fwd"].partition_broadcast(128)), w=["dtb"], dma="dtb")
        p.op("sp", lambda e: e.dma_start(out=dtb[:, 16:32], in_=vf["dt_bias_bwd"].partition_broadcast(128)), w=["dtb"], dma="dtb")
        dsk, k_dsk = bcast_load("d_skip", 16)
        cw = A.alloc([12, 5], F32)
        cb = A.alloc([12], F32)
        p.op("sp", lambda e: e.dma_start(out=cw.rearrange("p c k -> p (c k)"), in_=conv_w), w=["cw"], dma="cw")
        p.op("sp", lambda e: e.dma_start(out=cb, in_=conv_b), w=["cb"], dma="cb")
        cbh = A.alloc([12], F32)
        p.op("dve", lambda e: e.tensor_scalar(cbh, cb, 0.5, None, op0=ALU.mult), r=["cb"], w=["cbh"])

        for n, sh in wnames.items():
            rows = sh[0]
            nsplit = 4
            rs = rows // nsplit
            for i in range(nsplit):
                p.op("pool", lambda e, n=n, i=i, rs=rs: e.dma_start(out=wb[n][i * rs:(i + 1) * rs, :], in_=wf[n][i * rs:(i + 1) * rs, :]),
                     w=["wb_" + n], dma="wb_" + n)

        C.ident, C.negh = ident, negh
        junk1 = A.alloc([1024], BF16)
        base_mark = A.mark()

        def rms_rstd(ssq_ap, rstd_ap, n, kr, kw, width):
            p.op("dve", lambda e: e.tensor_scalar(rstd_ap, ssq_ap, 1.0 / width, EPS, op0=ALU.mult, op1=ALU.add), r=[kr], w=[kw])
            p.op("pool", lambda e: e.tensor_tensor(rstd_ap, rstd_ap, negh[:, 0:n], op=ALU.pow), r=[kw, "negh"], w=[kw])

        tr_ctr = [0]

        def norm_transpose(src, src_key, g_b, g_key, dstT, dst_key, dst_cols, bufs):
            i = tr_ctr[0] % 2
            tr_ctr[0] += 1
            junk, ssq, rstd, xn = bufs["junk"][i], bufs["ssq"][i], bufs["rstd"][i], bufs["xn"][i]
            kj, ks, kr, kx = "ntj%d" % i, "nts%d" % i, "ntr%d" % i, "ntx%d" % i
            bank = 0 + i
            kb = "ps%d" % bank
            p.op("act", lambda e: e.activation(junk, src, AF.Square, accum_out=ssq[:, 0:1]), r=[src_key], w=[ks])
            rms_rstd(ssq[:, 0:1], rstd[:, 0:1], 1, ks, kr, 1024.0)
            p.op("dve", lambda e: e.scalar_tensor_tensor(xn, src, rstd[:, 0:1], g_b, op0=ALU.mult, op1=ALU.mult), r=[src_key, kr, g_key], w=[kx])
            psT = ps[:, bank, :].bitcast(BF16).rearrange("p (c t) -> p c t", c=8)
            for c in range(8):
                p.op("pe", lambda e, c=c: e.transpose(psT[:, c, :], xn[:, c * 128:(c + 1) * 128], ident), r=[kx, "ident"], w=[kb])
            p.op("act", lambda e: e.activation(dstT[:, :, dst_cols], psT, AF.Copy), r=[kb], w=[dst_key])

        def alloc_norm_bufs():
            return {"junk": [junk1, junk1],
                    "ssq": [A.alloc([8], F32) for _ in range(2)],
                    "rstd": [A.alloc([8], F32) for _ in range(2)],
                    "xn": [A.alloc([1024], BF16) for _ in range(2)]}

        def ffn_phase(which, seq, get_src, epilogue):
            m = A.mark()
            wg, wu, wd = wb[which + "_w_gate"], wb[which + "_w_up"], wb[which + "_w_down"]
            gpre, kpre = bcast_load(which + "_pre_g", 1024, key="gpre")
            gpost, kpost = bcast_load(which + "_post_g", 1024, key="gpost")
            p.op("dve", lambda e: e.tensor_scalar(gpost, gpost, 0.5, None, op0=ALU.mult), r=[kpost], w=[kpost])
            nb = alloc_norm_bufs()
            hnT = [A.alloc([8, G], BF16) for _ in range(2)]
            actT = A.alloc([NJ, G], BF16)
            Wd = A.alloc([NJ, 1024], BF16)
            Wgu = [A.alloc([2, 8, 256], BF16) for _ in range(2)]
            th = [A.alloc([512], F32) for _ in range(2)]
            aa = [A.alloc([512], F32) for _ in range(2)]
            t1x = A.alloc([1024], F32)
            t1 = [t1x, t1x]
            fj = [junk1, junk1]
            fs = [A.alloc([8], F32) for _ in range(2)]
            fr = [A.alloc([8], F32) for _ in range(2)]
            NJB = 11
            wctr = 0
            srcs = {}
            srcs[0] = get_src(0)
            for t in range(4):
                norm_transpose(srcs[0][0][:, t, :], srcs[0][1], gpre, kpre, hnT[0], "hnT0", slice(t * 128, (t + 1) * 128), nb)
            for g in range(NG):
                src, skey = srcs[g]
                hT = hnT[g % 2]
                khT = "hnT%d" % (g % 2)
                for hh in range(2):
                    p.op("sp", lambda e, hh=hh: e.dma_start(out=Wd[:, hh * 11:(hh + 1) * 11, :], in_=wd.rearrange("(j p) n -> p j n", p=128)[:, hh * 11:(hh + 1) * 11, :]),
                         r=["wb_" + which + "_w_down"], w=["Wd"], dma="Wd")
                for jb in range(NJB):
                    ncols = 256
                    slot = wctr % 2
                    wctr += 1
                    W = Wgu[slot]
                    kW = "Wgu%d" % slot
                    for wi, wsrc in enumerate((wg, wu)):
                        p.op("sp", lambda e, wi=wi, wsrc=wsrc, jb=jb, ncols=ncols, W=W: e.dma_start(
                            out=W[:, wi, :, 0:ncols], in_=wsrc.rearrange("(c p) n -> p c n", p=128)[:, :, jb * 256: jb * 256 + ncols]),
                            r=["wb_" + which + ("_w_gate" if wi == 0 else "_w_up")], w=[kW], dma=kW)
                    if g + 1 < NG and jb == 1:
                        srcs[g + 1] = get_src(g + 1)
                    if g + 1 < NG and jb in (2, 4, 6, 8):
                        tn = (jb - 2) // 2
                        nsrc, nkey = srcs[g + 1]
                        norm_transpose(nsrc[:, tn, :], nkey, gpre, kpre, hnT[(g + 1) % 2], "hnT%d" % ((g + 1) % 2), slice(tn * 128, (tn + 1) * 128), nb)
                    for jj in range(ncols // 128):
                        j = jb * 2 + jj
                        pb = 2 + 2 * (j % 2)
                        for wi in range(2):
                            for c in range(8):
                                p.op("pe", lambda e, wi=wi, c=c, jj=jj, pb=pb, W=W, hT=hT: e.matmul(ps[:, pb + wi, :], W[:, wi, c, jj * 128:(jj + 1) * 128], hT[:, c, :], start=(c == 0), stop=(c == 7)),
                                     r=[kW, khT], w=["ps%d" % (pb + wi)])
                        i2 = j % 2
                        p.op("act", lambda e, pb=pb, i2=i2: e.activation(th[i2], ps[:, pb, :], AF.Tanh, scale=0.5), r=["ps%d" % pb], w=["th%d" % i2])
                        p.op("dve", lambda e, pb=pb, i2=i2: e.scalar_tensor_tensor(aa[i2], th[i2], 1.0, ps[:, pb, :], op0=ALU.add, op1=ALU.mult), r=["th%d" % i2, "ps%d" % pb], w=["aa%d" % i2])
                        p.op("dve", lambda e, pb=pb, i2=i2, j=j: e.scalar_tensor_tensor(actT[:, j, :], aa[i2], 0.5, ps[:, pb + 1, :], op0=ALU.mult, op1=ALU.mult), r=["aa%d" % i2, "ps%d" % (pb + 1)], w=["actT"])
                pend = None
                for t in range(4):
                    pb = 2 + 2 * (t % 3)
                    i2 = t % 2
                    for n in range(2):
                        for j in range(NJ):
                            p.op("pe", lambda e, n=n, j=j, t=t, pb=pb: e.matmul(ps[:, pb + n, :], actT[:, j, t * 128:(t + 1) * 128], Wd[:, j, n * 512:(n + 1) * 512], start=(j == 0), stop=(j == NJ - 1)),
                                 r=["actT", "Wd"], w=["ps%d" % (pb + n)])
                    fps = ps[:, pb:pb + 2, :].rearrange("p a b -> p (a b)")
                    kps = ["ps%d" % pb, "ps%d" % (pb + 1)]
                    p.op("act", lambda e, fps=fps, i2=i2: e.activation(fj[i2], fps, AF.Square, accum_out=fs[i2][:, 0:1]), r=kps, w=["fs%d" % i2])
                    rms_rstd(fs[i2][:, 0:1], fr[i2][:, 0:1], 1, "fs%d" % i2, "fr%d" % i2, 1024.0)
                    p.op("dve", lambda e, fps=fps, i2=i2: e.scalar_tensor_tensor(t1[i2], fps, fr[i2][:, 0:1], gpost, op0=ALU.mult, op1=ALU.mult), r=kps + ["fr%d" % i2, kpost], w=["t1"])
                    p.op("pool", lambda e, i2=i2, t=t, src=src: e.tensor_tensor(src[:, t, :], src[:, t, :], t1[i2], op=ALU.add), r=["t1", skey], w=[skey])
                    if pend is not None:
                        epilogue(*pend)
                    pend = (g, t, src[:, t, :], skey)
                epilogue(*pend)
            full_barrier()
            A.release(m)

        def dump(name, ap, keys):
            if name in dbg_out:
                p.op("sp", lambda e: e.dma_start(out=dbg_out[name], in_=ap), r=list(keys), w=["dbgo_" + name], dma="dbg")
        C.dump = dump
        C.ffn_phase = ffn_phase
        C.norm_transpose = norm_transpose
        C.alloc_norm_bufs = alloc_norm_bufs
        C.bcast_load = bcast_load
        C.full_barrier = full_barrier
        C.rms_rstd = rms_rstd
        C.U = U
        C.x, C.out, C.h1d, C.wb, C.vf, C.dbg_out = x, out, h1d, wb, vf, dbg_out
        C.cst, C.cosT, C.sinT, C.maskf, C.maskb, C.Uf32, C.Ub32, C.ones32 = cst, cosT, sinT, maskf, maskb, Uf32, Ub32, ones32
        C.gsub, C.k_gsub, C.lamt, C.aneg, C.dtb, C.dsk, C.k_dsk, C.cw, C.cb, C.cbh = gsub, k_gsub, lamt, aneg, dtb, dsk, k_dsk, cw, cb, cbh
        C.stages = stages

        for seq in range(nseq):
            run_sequence(C, seq)

        p.emit(st)
    return nc


def run_sequence(C, seq):
    p, A, ps = C.p, C.A, C.ps
    x, out, h1d, wb = C.x, C.out, C.h1d, C.wb
    U = C.U
    stages = C.stages
    tok0 = seq * S
    words_save = A.words
    seq_mark = A.mark()
    hnTm = A.alloc_top([8, S], BF16)
    kmix = "mixedT"
    khm = "hnTm"

    if "f1" in stages:
        m = A.mark()
        xt = [A.alloc([4, 1024], F32) for _ in range(2)]
        gmix, kgmix = C.bcast_load("mix_pre_g", 1024, key="gmix")
        nb2 = C.alloc_norm_bufs()

        def get_src(g):
            slot = g % 2
            k = "xt%d" % slot
            p.op("sp", lambda e: e.dma_start(out=xt[slot], in_=x[tok0 + g * G: tok0 + (g + 1) * G, :].rearrange("(t p) d -> p t d", p=128)), w=[k], dma=k)
            return xt[slot], k

        def epi(g, t, res, rkey):
            r0 = tok0 + g * G + t * 128
            p.op("sp", lambda e: e.dma_start(out=h1d[r0:r0 + 128, :], in_=res), r=[rkey], w=["h1d"], dma="h1st%d" % (g % 2))
            C.norm_transpose(res, rkey, gmix, kgmix, hnTm, khm, slice(g * G + t * 128, g * G + (t + 1) * 128), nb2)

        C.ffn_phase("ffn1", seq, get_src, epi)
        A.release(m)
    C.seq_mark = seq_mark
    C.words_save = words_save
    attnT = A.alloc([16, 8, 128], BF16)
    mixedT = attnT
    if "hn" in C.dbg_out and seq == 0:
        m = A.mark()
        tmp = A.alloc([8 * S // 4], F32)
        for q in range(4):
            p.op("dve", lambda e, q=q: e.tensor_copy(tmp, hnTm.rearrange("p c t -> p (c t)")[:, q * 4096:(q + 1) * 4096]), r=[khm], w=["dbgtmp"])
            p.op("sp", lambda e, q=q: e.dma_start(out=C.dbg_out["hn"][:, q * 4096:(q + 1) * 4096], in_=tmp), r=["dbgtmp"], w=["dbgo"], dma="dbg")
        C.full_barrier()
        A.release(m)

    if "attn" in stages:
        attention_phase(C, seq, mixedT, kmix, hnTm, khm)
    ssdT = A.alloc([16, 8, 128], BF16)
    C.attnT, C.ssdT = attnT, ssdT
    if "ssd" in stages:
        ssd_phase(C, seq, ssdT, kmix, hnTm, khm)
    for nm, tt in (("attnT", attnT), ("ssdT", ssdT)):
        if nm in C.dbg_out and seq == 0:
            m = A.mark()
            tmp = A.alloc([4096], F32)
            mf = tt.rearrange("p a b c -> p (a b c)")
            for q in range(4):
                p.op("dve", lambda e, q=q, mf=mf: e.tensor_copy(tmp, mf[:, q * 4096:(q + 1) * 4096]), r=[kmix + "x"] + [kmix + "a%d" % t for t in range(16)], w=["dbgtmp"])
                p.op("sp", lambda e, q=q, nm=nm: e.dma_start(out=C.dbg_out[nm][:, q * 4096:(q + 1) * 4096], in_=tmp), r=["dbgtmp"], w=["dbgo"], dma="dbg")
            C.full_barrier()
            A.release(m)
    A.words = words_save
    if "o" in stages:
        oproj_ffn2_phase(C, seq, mixedT, kmix)
    C.full_barrier()
    A.release(seq_mark)


def attention_phase(C, seq, mixedT, kmix, hnTm, khm):
    p, A, ps = C.p, C.A, C.ps
    ident = C.ident
    win = C.wb["w_in"].rearrange("(c p) n -> p c n", p=128)
    m = A.mark()
    QT = A.alloc([8, S], BF16)
    KT = A.alloc([8, S], BF16)
    Vp = A.alloc([16, 8, 130], BF16)
    m_in = A.mark()
    Wqkv = A.alloc([8, 1024], BF16)
    p.op("pool", lambda e: e.memset(Vp.rearrange("p a b c -> p (a b c)"), 1.0), w=["Vp"])
    qtok = [A.alloc([16, 64], BF16) for _ in range(2)]
    rta = [A.alloc([16, 8], F32) for _ in range(4)]
    rtb = [A.alloc([16, 8], F32) for _ in range(4)]
    for i in range(3):
        p.op("sp", lambda e, i=i: e.dma_start(out=Wqkv, in_=win[:, :, i * 1024:(i + 1) * 1024]), r=["wb_w_in"], w=["Wqkv"], dma="Wqkv")
        for t in range(16):
            cs_ = C.cosT[:, t, :].unsqueeze(1).to_broadcast([128, 16, 8])
            sn_ = C.sinT[:, t, :].unsqueeze(1).to_broadcast([128, 16, 8])
            b0 = 2 + 2 * (t % 3)
            for n in range(2):
                for c in range(8):
                    p.op("pe", lambda e, i=i, n=n, c=c, b0=b0, t=t: e.matmul(ps[:, b0 + n, :], hnTm[:, c, t * 128:(t + 1) * 128], Wqkv[:, c, n * 512:(n + 1) * 512], start=(c == 0), stop=(c == 7)),
                         r=[khm, "Wqkv"], w=["ps%d" % (b0 + n)])
            kps = ["ps%d" % b0, "ps%d" % (b0 + 1)]
            pv = ps[:, b0:b0 + 2, :].rearrange("p a b -> p (a b)")
            if i == 2:
                p.op("act", lambda e, pv=pv, t=t: e.activation(Vp[:, t, :, 0:128], pv.rearrange("p (h d) -> p h d", d=128), AF.Copy), r=kps, w=["Vp"])
                continue
            qv = pv.rearrange("p (h d) -> p h d", d=64)
            qt_ = qtok[i]
            kq = "qtok%d" % i
            ra, rb = rta[2 * i], rta[2 * i + 1]
            rc, rd = rtb[2 * i], rtb[2 * i + 1]
            p.op("dve", lambda e, qv=qv, ra=ra, cs_=cs_: e.tensor_tensor(ra, qv[:, :, 0:8], cs_, op=ALU.mult), r=kps + ["cst"], w=["ra%d" % i])
            p.op("dve", lambda e, qv=qv, rb=rb, sn_=sn_: e.tensor_tensor(rb, qv[:, :, 8:16], sn_, op=ALU.mult), r=kps + ["cst"], w=["rb%d" % i])
            p.op("dve", lambda e, qv=qv, rc=rc, cs_=cs_: e.tensor_tensor(rc, qv[:, :, 8:16], cs_, op=ALU.mult), r=kps + ["cst"], w=["rc%d" % i])
            p.op("dve", lambda e, qv=qv, rd=rd, sn_=sn_: e.tensor_tensor(rd, qv[:, :, 0:8], sn_, op=ALU.mult), r=kps + ["cst"], w=["rd%d" % i])
            p.op("pool", lambda e, qt_=qt_, ra=ra, rb=rb: e.tensor_tensor(qt_[:, :, 0:8], ra, rb, op=ALU.subtract), r=["ra%d" % i, "rb%d" % i], w=[kq])
            p.op("pool", lambda e, qt_=qt_, rc=rc, rd=rd: e.tensor_tensor(qt_[:, :, 8:16], rc, rd, op=ALU.add), r=["rc%d" % i, "rd%d" % i], w=[kq])
            p.op("act", lambda e, qt_=qt_, qv=qv: e.activation(qt_[:, :, 16:64], qv[:, :, 16:64], AF.Copy), r=kps, w=[kq])
            tb = t % 2
            psT = ps[:, tb, :].bitcast(BF16).rearrange("p (c t) -> p c t", c=8)
            qf = qt_.rearrange("p a b -> p (a b)")
            for h in range(8):
                p.op("pe", lambda e, h=h, psT=psT, qf=qf: e.transpose(psT[:, h, :], qf[:, h * 128:(h + 1) * 128], ident), r=[kq, "ident"], w=["ps%d" % tb])
            dst = QT if i == 0 else KT
            p.op("act", lambda e, dst=dst, psT=psT, t=t: e.activation(dst[:, :, t * 128:(t + 1) * 128], psT, AF.Copy), r=["ps%d" % tb], w=["QT" if i == 0 else "KT"])
    C.full_barrier()
    A.release(m_in)
    Qpad = [A.alloc([2, S], BF16) for _ in range(2)]
    for i_ in range(2):
        p.op("pool", lambda e, i_=i_: e.memset(Qpad[i_].rearrange("p a b -> p (a b)"), 0.0), w=["Qpad%d" % i_])
    NSB = 3
    SB = (0, 1, 6)
    PF = 2
    E = [A.alloc([512], BF16) for _ in range(NSB)]
    rr = [A.alloc([8], F32) for _ in range(2)]
    o1 = [A.alloc([128], F32) for _ in range(2)]
    o2 = [A.alloc([128], F32) for _ in range(2)]
    ss = [A.alloc([8], F32) for _ in range(2)]
    rs = [A.alloc([8], F32) for _ in range(2)]
    at = [A.alloc([128], BF16) for _ in range(2)]
    junk = A.alloc([128], BF16)
    import os as _os
    _nh = int(_os.environ.get("ATTN_HEADS_DBG", "8"))
    steps = [(h, qg, kt) for h in range(_nh) for qg in range(8) for kt in range(16)]

    def load_qpad(h):
        qp = Qpad[h % 2]
        kqp = "Qpad%d" % (h % 2)
        p.op("act", lambda e, qp=qp, h=h: e.activation(qp[0:64, 0, :], QT[0:64, h, :], AF.Copy), r=["QT"], w=[kqp])
        p.op("pool", lambda e, qp=qp, h=h: e.tensor_copy(qp[64:128, 1, :], QT[64:128, h, :]), r=["QT"], w=[kqp])

    def issue_qk(i):
        h, qg, kt = steps[i]
        sb = SB[i % NSB]
        qp = Qpad[h % 2]
        kqp = "Qpad%d" % (h % 2)
        for cm in range(2):
            p.op("pe", lambda e, sb=sb, cm=cm, h=h, kt=kt, qg=qg, qp=qp: e.matmul(ps[:, sb, cm * 256:(cm + 1) * 256], KT[:, h, kt * 128:(kt + 1) * 128], qp[:, cm, qg * 256:(qg + 1) * 256], start=True, stop=True),
                 r=["KT", kqp], w=["ps%d" % sb])

    if _nh > 0:
        load_qpad(0)
        if _nh > 1:
            load_qpad(1)
    for i in range(min(PF, len(steps))):
        issue_qk(i)
    ectr = 0
    for i, (h, qg, kt) in enumerate(steps):
        accs = (2, 3) if ((h * 8 + qg) % 2 == 0) else (4, 5)
        sl = i % NSB
        sb = SB[sl]
        p.op("act", lambda e, sb=sb, sl=sl: e.activation(E[sl], ps[:, sb, :], AF.Exp, scale=0.125), r=["ps%d" % sb], w=["E%d" % sl])
        if i + PF < len(steps):
            issue_qk(i + PF)
        for cm in range(2):
            for qt in range(2):
                p.op("pe", lambda e, sl=sl, cm=cm, qt=qt, kt=kt, h=h, accs=accs: e.matmul(ps[:, accs[cm], qt * 130:(qt + 1) * 130], E[sl][:, cm * 256 + qt * 128: cm * 256 + (qt + 1) * 128], Vp[:, kt, h, 0:130], start=(kt == 0 and qt == 0), stop=(kt == 15), skip_group_check=True),
                     r=["E%d" % sl, "Vp"], w=["ps%d" % accs[cm]])
        if kt == 15 and qg == 7 and h + 2 < _nh:
            load_qpad(h + 2)
        if kt == 15:
            for qt in range(2):
                t = qg * 2 + qt
                i2 = ectr % 2
                ectr += 1
                ka = ["ps%d" % accs[0], "ps%d" % accs[1]]
                O1 = ps[:, accs[0], qt * 130: qt * 130 + 128]
                O2 = ps[:, accs[1], qt * 130: qt * 130 + 128]
                s1 = ps[:, accs[0], qt * 130 + 128: qt * 130 + 129]
                s2 = ps[:, accs[1], qt * 130 + 128: qt * 130 + 129]
                r_ = rr[i2]
                p.op("dve", lambda e, r_=r_, s1=s1: e.reciprocal(r_[:, 0:1], s1), r=ka, w=["rr%d" % i2])
                p.op("dve", lambda e, r_=r_, s2=s2: e.reciprocal(r_[:, 1:2], s2), r=ka, w=["rr%d" % i2])
                p.op("dve", lambda e, r_=r_: e.tensor_tensor(r_[:, 1:2], r_[:, 1:2], C.lamt[:, 3:4], op=ALU.mult), r=["rr%d" % i2, "lamt"], w=["rr%d" % i2])
                p.op("dve", lambda e, r_=r_, O1=O1, i2=i2: e.tensor_scalar(o1[i2], O1, r_[:, 0:1], None, op0=ALU.mult), r=ka + ["rr%d" % i2], w=["o1%d" % i2])
                p.op("dve", lambda e, r_=r_, O2=O2, i2=i2: e.scalar_tensor_tensor(o2[i2], O2, r_[:, 1:2], o1[i2], op0=ALU.mult, op1=ALU.add), r=ka + ["rr%d" % i2, "o1%d" % i2], w=["o2%d" % i2])
                p.op("act", lambda e, i2=i2: e.activation(junk, o2[i2], AF.Square, accum_out=ss[i2][:, 0:1]), r=["o2%d" % i2], w=["ss%d" % i2])
                C.rms_rstd(ss[i2][:, 0:1], rs[i2][:, 0:1], 1, "ss%d" % i2, "rs%d" % i2, 128.0)
                p.op("dve", lambda e, i2=i2: e.scalar_tensor_tensor(at[i2], o2[i2], rs[i2][:, 0:1], C.gsub, op0=ALU.mult, op1=ALU.mult), r=["o2%d" % i2, "rs%d" % i2, C.k_gsub], w=["at%d" % i2])
                psT = ps[:, 7, i2 * 64:(i2 + 1) * 64].bitcast(BF16)
                p.op("pe", lambda e, psT=psT, i2=i2: e.transpose(psT, at[i2], ident), r=["at%d" % i2, "ident"], w=["ps7_%d" % i2])
                p.op("act", lambda e, psT=psT, t=t, h=h: e.activation(mixedT[:, t, h, :], psT, AF.Copy), r=["ps7_%d" % i2], w=[kmix + "a%d" % t])
    C.full_barrier()
    A.release(m)


def ssd_phase(C, seq, mixedT, kmix, hnTm, khm):
    p, A, ps = C.p, C.A, C.ps
    ident = C.ident
    win = C.wb["w_in"].rearrange("(c p) n -> p c n", p=128)
    m0 = A.mark()
    zs = A.alloc([16, 1024], BF16)
    BCT = A.alloc([4, S], BF16)
    dtt = A.alloc([16, 32], F32)
    dat = A.alloc([16, 32], F32)
    m1 = A.mark()
    Wz = A.alloc([8, 1024], BF16)
    p.op("sp", lambda e: e.dma_start(out=Wz, in_=win[:, :, 3072:4096]), r=["wb_w_in"], w=["Wz"], dma="Wz")
    th = [A.alloc([1024], F32) for _ in range(2)]
    for t in range(16):
        b0 = 2 + 2 * (t % 2)
        i2 = t % 2
        for n in range(2):
            for c in range(8):
                p.op("pe", lambda e, n=n, c=c, b0=b0, t=t: e.matmul(ps[:, b0 + n, :], hnTm[:, c, t * 128:(t + 1) * 128], Wz[:, c, n * 512:(n + 1) * 512], start=(c == 0), stop=(c == 7)),
                     r=[khm, "Wz"], w=["ps%d" % (b0 + n)])
        kps = ["ps%d" % b0, "ps%d" % (b0 + 1)]
        pv = ps[:, b0:b0 + 2, :].rearrange("p a b -> p (a b)")
        p.op("act", lambda e, pv=pv, i2=i2: e.activation(th[i2], pv, AF.Tanh, scale=0.5), r=kps, w=["zth%d" % i2])
        p.op("dve", lambda e, pv=pv, i2=i2, t=t: e.scalar_tensor_tensor(zs[:, t, :], th[i2], 1.0, pv, op0=ALU.add, op1=ALU.mult), r=kps + ["zth%d" % i2], w=["zs"])
    C.full_barrier()
    A.release(m1)
    import os as _os
    _lvl = int(_os.environ.get("SSD_DBG", "9"))
    if _lvl <= 0:
        C.full_barrier()
        A.release(m0)
        return
    Wxs = [A.alloc([8, 128], BF16) for _ in range(2)]
    Wdt = A.alloc([8, 32], BF16)
    p.op("sp", lambda e: e.dma_start(out=Wdt, in_=win[:, :, 5632:5664]), r=["wb_w_in"], w=["Wdt"], dma="Wdt")
    dps = ps[:, 6, :].rearrange("p (t k) -> p t k", k=32)
    for t in range(16):
        for c in range(8):
            p.op("pe", lambda e, c=c, t=t: e.matmul(dps[:, t, :], hnTm[:, c, t * 128:(t + 1) * 128], Wdt[:, c, :], start=(c == 0), stop=(c == 7)), r=[khm, "Wdt"], w=["ps6"])
    p.op("dve", lambda e: e.tensor_tensor(dtt, dps, C.dtb.unsqueeze(1).to_broadcast([128, 16, 32]), op=ALU.add), r=["ps6", "dtb"], w=["dtt"])
    dflat = dtt.rearrange("p a b -> p (a b)")
    _dtl = int(_os.environ.get("SSD_DT", "9"))
    if _dtl >= 2:
        p.op("act", lambda e, dflat=dflat: e.activation(dflat, dflat, AF.Exp), r=["dtt"], w=["dtt"])
    if _dtl >= 3:
        p.op("act", lambda e, dflat=dflat: e.activation(dflat, dflat, AF.Ln, bias=C.ones32[:, 0:1]), r=["dtt", "cst"], w=["dtt"])
    if _dtl >= 4:
        p.op("dve", lambda e: e.tensor_tensor(dat, dtt, C.aneg.unsqueeze(1).to_broadcast([128, 16, 32]), op=ALU.mult), r=["dtt", "aneg"], w=["dat"])
    C.dump("dtt", dtt.rearrange("p a b -> p (a b)"), ["dtt"])
    C.dump("dat", dat.rearrange("p a b -> p (a b)"), ["dat"])
    C.dump("zs", zs.rearrange("p a b -> p (a b)"), ["zs"])
    if _lvl <= 1:
        C.full_barrier()
        A.release(m0)
        return
    xpad = A.alloc([S + 4], F32)
    acc = A.alloc([S], F32)
    cth = A.alloc([S], F32)
    xsT = A.alloc([S], BF16)
    p.op("pool", lambda e: e.memset(xpad, 0.0), w=["xpad"])
    for k in range(12):
        Wx = Wxs[k % 2]
        kWx = "Wx%d" % (k % 2)
        p.op("sp", lambda e, k=k, Wx=Wx: e.dma_start(out=Wx, in_=win[:, :, 4096 + k * 128: 4096 + (k + 1) * 128]), r=["wb_w_in"], w=[kWx], dma=kWx)
        for tg in range(4):
            for c in range(8):
                p.op("pe", lambda e, Wx=Wx, tg=tg, c=c: e.matmul(ps[:, 2 + tg, :], Wx[:, c, :], hnTm[:, c, tg * 512:(tg + 1) * 512], start=(c == 0), stop=(c == 7)),
                     r=[khm, kWx], w=["ps%d" % (2 + tg)])
            p.op("act", lambda e, tg=tg: e.activation(xpad[:, 2 + tg * 512: 2 + (tg + 1) * 512], ps[:, 2 + tg, :], AF.Copy), r=["ps%d" % (2 + tg)], w=["xpad"])
        p.op("dve", lambda e, k=k: e.tensor_scalar(acc, xpad[:, 0:S], C.cw[:, k, 0:1], None, op0=ALU.mult), r=["xpad", "cw"], w=["acc"])
        for j in range(1, 5):
            p.op("dve", lambda e, k=k, j=j: e.scalar_tensor_tensor(acc, xpad[:, j:j + S], C.cw[:, k, j:j + 1], acc, op0=ALU.mult, op1=ALU.add), r=["xpad", "cw", "acc"], w=["acc"])
        p.op("act", lambda e, k=k: e.activation(cth, acc, AF.Tanh, bias=C.cbh[:, k:k + 1], scale=0.5), r=["acc", "cbh"], w=["cth"])
        p.op("dve", lambda e, k=k: e.tensor_scalar(acc, acc, C.cb[:, k:k + 1], 0.5, op0=ALU.add, op1=ALU.mult), r=["acc", "cb", "cth"], w=["acc"])
        if k < 8:
            p.op("dve", lambda e: e.scalar_tensor_tensor(xsT, cth, 1.0, acc, op0=ALU.add, op1=ALU.mult), r=["cth", "acc"], w=["xsT"])
            for tq in range(2):
                pb = tq
                psT = ps[:, pb, :].bitcast(BF16).rearrange("p (t c) -> p t c", t=8)
                for tt in range(8):
                    t = tq * 8 + tt
                    p.op("pe", lambda e, psT=psT, tt=tt, t=t: e.transpose(psT[:, tt, :], xsT[:, t * 128:(t + 1) * 128], ident), r=["xsT", "ident"], w=["ps%d" % pb])
                p.op("act", lambda e, psT=psT, tq=tq, k=k: e.activation(mixedT[:, tq * 8:(tq + 1) * 8, k, :], psT, AF.Copy), r=["ps%d" % pb], w=[kmix + "x"])
        else:
            p.op("dve", lambda e, k=k: e.scalar_tensor_tensor(BCT[:, k - 8, :], cth, 1.0, acc, op0=ALU.add, op1=ALU.mult), r=["cth", "acc"], w=["BCT"])
    C.full_barrier()
    A.release(m1)
    C.dump("BCT", BCT.rearrange("p a b -> p (a b)"), ["BCT"])
    C.dump("xs", mixedT.rearrange("p a b c -> p (a b c)"), [kmix + "x"])
    C.full_barrier()
    if _lvl <= 2:
        A.release(m0)
        return
    A.words = C.words_save
    ysum = A.alloc([16, 1024], BF16)
    csc = A.alloc([16], F32)
    ecol = A.alloc([16], F32)
    dend = A.alloc([16], F32)
    cd = A.alloc([16], F32)
    Ah = A.alloc([16, 128], F32)
    dec = A.alloc([16, 128], F32)
    MT = A.alloc([16, 128], BF16)
    CBm = A.alloc([2, 128], BF16)
    xdt = A.alloc([16, 64], BF16)
    xdw = A.alloc([16, 64], BF16)
    Btok = A.alloc([2, 128], BF16)
    prev = A.alloc([16, 64], F32)
    pbf = A.alloc([16, 64], BF16)
    yc = A.alloc([16, 64], F32)
    y2 = A.alloc([16, 64], F32)
    ssq = A.alloc([8], F32)
    rst = A.alloc([8], F32)
    ynb = A.alloc([1024], BF16)
    junk = A.alloc([1024], BF16)
    gssd, kgssd = C.bcast_load("ssd_norm_g", 1024, key="gssd")

    def xs_tile(c):
        return mixedT[:, c, :, :].rearrange("p k (a b) -> p (k a) b", a=2)

    for d in (1, 0):
        Ud = C.Ub32 if d == 1 else C.Uf32
        maskd = C.maskb if d == 1 else C.maskf
        p.op("pool", lambda e: e.memset(prev.rearrange("p a b -> p (a b)"), 0.0), w=["prev"])
        p.op("pool", lambda e: e.memset(pbf.rearrange("p a b -> p (a b)"), 0.0), w=["pbf"])
        order = range(15, -1, -1) if d == 1 else range(16)
        if _lvl <= 3:
            order = list(order)[:1]
        _nch = int(_os.environ.get("SSD_NCH", "16"))
        order = list(order)[:_nch]
        if _lvl <= 4 and d == 0:
            break
        for c in order:
            da_c = dat[:, c, 16 * d:16 * d + 16]
            dt_c = dtt[:, c, 16 * d:16 * d + 16]
            xs = xs_tile(c)
            kx = kmix + "x"
            cols = slice(c * 128, (c + 1) * 128)
            p.op("pe", lambda e, da_c=da_c, Ud=Ud: e.matmul(ps[:, 0, 0:16], Ud, da_c, start=True, stop=True), r=["dat", "cst"], w=["ps0"])
            p.op("pe", lambda e, da_c=da_c: e.matmul(ps[:, 0, 16:32], C.ones32, da_c, start=True, stop=True), r=["dat", "cst"], w=["ps0"])
            p.op("dve", lambda e: e.tensor_copy(csc, ps[:, 0, 0:16]), r=["ps0"], w=["csc"])
            p.op("act", lambda e: e.activation(ecol, ps[:, 0, 0:16], AF.Exp), r=["ps0"], w=["ecol"])
            p.op("act", lambda e: e.activation(cd, ps[:, 0, 16:32], AF.Exp), r=["ps0"], w=["cd"])
            p.op("dve", lambda e: e.tensor_tensor(dend, ps[:, 0, 16:32], csc, op=ALU.subtract), r=["ps0", "csc"], w=["dend"])
            p.op("act", lambda e: e.activation(dend, dend, AF.Exp), r=["dend"], w=["dend"])
            p.op("pool", lambda e, da_c=da_c, Ud=Ud: e.tensor_tensor(Ah, Ud.unsqueeze(1).to_broadcast([128, 16, 128]), da_c.unsqueeze(2).to_broadcast([128, 16, 128]), op=ALU.mult), r=["dat", "cst"], w=["Ah"])
            for q in range(4):
                p.op("pe", lambda e, q=q: e.matmul(ps[:, 2 + q, :], C.ones32, Ah[:, 4 * q:4 * q + 4, :].rearrange("p a b -> p (a b)"), start=True, stop=True), r=["Ah", "cst"], w=["ps%d" % (2 + q)])
                p.op("dve", lambda e, q=q: e.tensor_tensor(dec[:, 4 * q:4 * q + 4, :], ps[:, 2 + q, :].rearrange("p (a b) -> p a b", a=4), csc[:, 4 * q:4 * q + 4].unsqueeze(2).to_broadcast([128, 4, 128]), op=ALU.subtract), r=["ps%d" % (2 + q), "csc"], w=["dec"])
            dflat = dec.rearrange("p a b -> p (a b)")
            p.op("dve", lambda e, dflat=dflat: e.tensor_scalar_min(dflat, dflat, 0.0), r=["dec"], w=["dec"])
            p.op("act", lambda e, dflat=dflat: e.activation(dflat, dflat, AF.Exp), r=["dec"], w=["dec"])
            for g in range(2):
                p.op("pe", lambda e, g=g, cols=cols: e.matmul(ps[:, 1, g * 128:(g + 1) * 128], BCT[:, g, cols], BCT[:, 2 + g, cols], start=True, stop=True), r=["BCT"], w=["ps1"])
            p.op("dve", lambda e, maskd=maskd: e.tensor_tensor(CBm, ps[:, 1, 0:256].rearrange("p (a b) -> p a b", a=2), maskd.unsqueeze(1).to_broadcast([128, 2, 128]), op=ALU.mult), r=["ps1", "maskf", "maskb"], w=["CBm"])
            for g in range(2):
                p.op("pool", lambda e, g=g: e.tensor_tensor(MT[:, 8 * g:8 * g + 8, :], dec[:, 8 * g:8 * g + 8, :], CBm[:, g, :].unsqueeze(1).to_broadcast([128, 8, 128]), op=ALU.mult), r=["dec", "CBm"], w=["MT"])
            p.op("dve", lambda e, xs=xs, dt_c=dt_c: e.tensor_tensor(xdt, xs, dt_c.unsqueeze(2).to_broadcast([128, 16, 64]), op=ALU.mult), r=[kx, "dtt"], w=["xdt"])
            p.op("pool", lambda e: e.tensor_tensor(xdw, xdt, dend.unsqueeze(2).to_broadcast([128, 16, 64]), op=ALU.mult), r=["xdt", "dend"], w=["xdw"])
            xdf = xdt.rearrange("p a b -> p (a b)")
            for h in range(16):
                p.op("pe", lambda e, h=h, xdf=xdf: e.matmul(ps[:, 6 + h // 8, (h % 8) * 64:(h % 8 + 1) * 64], MT[:, h, :], xdf[:, h * 64:(h + 1) * 64], start=True, stop=True), r=["MT", "xdt"], w=["ps%d" % (6 + h // 8)])
            pbff = pbf.rearrange("p a b -> p (a b)")
            for g in range(2):
                p.op("pe", lambda e, g=g, cols=cols, pbff=pbff: e.matmul(ps[:, 2 + g, :], BCT[:, 2 + g, cols], pbff[:, g * 512:(g + 1) * 512], start=True, stop=True), r=["BCT", "pbf"], w=["ps%d" % (2 + g)])
            yo = ps[:, 2:4, :].rearrange("p a b -> p (a b)").rearrange("p (h d) -> p h d", d=64)
            yd = ps[:, 6:8, :].rearrange("p a b -> p (a b)").rearrange("p (h d) -> p h d", d=64)
            p.op("dve", lambda e, yo=yo: e.tensor_tensor(yc, yo, ecol.unsqueeze(2).to_broadcast([128, 16, 64]), op=ALU.mult), r=["ps2", "ps3", "ecol"], w=["yc"])
            p.op("dve", lambda e, yd=yd: e.tensor_tensor(yc, yd, yc, op=ALU.add), r=["ps6", "ps7", "yc"], w=["yc"])
            for g in range(2):
                psB = ps[:, 1, 256 + g * 64: 256 + (g + 1) * 64].bitcast(BF16)
                p.op("pe", lambda e, g=g, cols=cols, psB=psB: e.transpose(psB, BCT[:, g, cols], ident), r=["BCT", "ident"], w=["ps1"])
            p.op("act", lambda e: e.activation(Btok, ps[:, 1, 256:384].bitcast(BF16).rearrange("p (a b) -> p a b", a=2), AF.Copy), r=["ps1"], w=["Btok"])
            xwf = xdw.rearrange("p a b -> p (a b)")
            for g in range(2):
                p.op("pe", lambda e, g=g, xwf=xwf: e.matmul(ps[:, 4 + g, :], Btok[:, g, :], xwf[:, g * 512:(g + 1) * 512], start=True, stop=True), r=["Btok", "xdw"], w=["ps%d" % (4 + g)])
            st_ = ps[:, 4:6, :].rearrange("p a b -> p (a b)").rearrange("p (h d) -> p h d", d=64)
            p.op("pool", lambda e: e.tensor_tensor(prev, prev, cd.unsqueeze(2).to_broadcast([128, 16, 64]), op=ALU.mult), r=["prev", "cd"], w=["prev"])
            p.op("dve", lambda e, st_=st_: e.tensor_tensor(prev, prev, st_, op=ALU.add), r=["prev", "ps4", "ps5"], w=["prev"])
            p.op("act", lambda e: e.activation(pbf, prev, AF.Copy), r=["prev"], w=["pbf"])
            if d == 1 and c == 15:
                C.dump("csc", csc, ["csc"]); C.dump("ecol", ecol, ["ecol"]); C.dump("dend", dend, ["dend"]); C.dump("cd", cd, ["cd"])
                C.dump("dec", dec.rearrange("p a b -> p (a b)"), ["dec"]); C.dump("MT", MT.rearrange("p a b -> p (a b)"), ["MT"])
                C.dump("yc", yc.rearrange("p a b -> p (a b)"), ["yc"]); C.dump("prev", prev.rearrange("p a b -> p (a b)"), ["prev"])
                C.dump("xdt", xdt.rearrange("p a b -> p (a b)"), ["xdt"]); C.dump("CBm", CBm.rearrange("p a b -> p (a b)"), ["CBm"])
            if d == 1:
                p.op("act", lambda e, c=c: e.activation(ysum[:, c, :], yc.rearrange("p a b -> p (a b)"), AF.Copy), r=["yc"], w=["ysum"])
            else:
                _ol = int(_os.environ.get("SSD_OUT", "9"))
                if _ol >= 1:
                    p.op("pool", lambda e, c=c: e.tensor_tensor(yc, yc, ysum[:, c, :].rearrange("p (a b) -> p a b", a=16), op=ALU.add), r=["yc", "ysum"], w=["yc"])
                    p.op("pool", lambda e, xs=xs: e.tensor_tensor(y2, xs, C.dsk.unsqueeze(2).to_broadcast([128, 16, 64]), op=ALU.mult), r=[kx, C.k_dsk], w=["y2"])
                    p.op("pool", lambda e: e.tensor_tensor(yc, yc, y2, op=ALU.add), r=["yc", "y2"], w=["yc"])
                ycf = yc.rearrange("p a b -> p (a b)")
                if _ol >= 2:
                    p.op("dve", lambda e, c=c, ycf=ycf: e.tensor_tensor(ycf, ycf, zs[:, c, :], op=ALU.mult), r=["yc", "zs"], w=["yc"])
                if _ol >= 3:
                    p.op("act", lambda e, ycf=ycf: e.activation(junk, ycf, AF.Square, accum_out=ssq[:, 0:1]), r=["yc"], w=["ssq"])
                    p.op("dve", lambda e: e.tensor_scalar(rst[:, 0:1], ssq[:, 0:1], 1.0 / 1024, 4.0 * EPS, op0=ALU.mult, op1=ALU.add), r=["ssq"], w=["rst"])
                    p.op("pool", lambda e: e.tensor_tensor(rst[:, 0:1], rst[:, 0:1], C.negh[:, 0:1], op=ALU.pow), r=["rst", "negh"], w=["rst"])
                if _ol >= 4:
                    p.op("dve", lambda e, ycf=ycf, c=c: e.scalar_tensor_tensor(ysum[:, c, :], ycf, rst[:, 0:1], gssd, op0=ALU.mult, op1=ALU.mult), r=["yc", "rst", kgssd, "ysum"], w=["ysum"])
    C.full_barrier()
    for c in range(16):
        bank = c % 2
        psT = ps[:, bank, :].bitcast(BF16).rearrange("p (k t) -> p k t", k=8)
        for k in range(8):
            p.op("pe", lambda e, k=k, psT=psT, c=c: e.transpose(psT[:, k, :], ysum[:, c, k * 128:(k + 1) * 128], ident), r=["ysum", "ident"], w=["ps%d" % bank])
        p.op("act", lambda e, c=c, psT=psT: e.activation(mixedT[:, c, :, :], psT, AF.Copy), r=["ps%d" % bank], w=[kmix + "x"])
    C.full_barrier()
    A.release(m0)


def oproj_ffn2_phase(C, seq, mixedT, kmix):
    p, A, ps = C.p, C.A, C.ps
    tok0 = seq * S
    words_save = A.words
    h2 = A.alloc_top([16, 1024], F32)
    m = A.mark()
    Wout = A.alloc([16, 1024], BF16)
    p.op("sp", lambda e: e.dma_start(out=Wout, in_=C.wb["w_out"].rearrange("(k p) n -> p k n", p=128)), r=["wb_w_out"], w=["Wout"], dma="Wout")
    gmp, kgmp = C.bcast_load("mix_post_g", 1024, key="gmp")
    t1 = A.alloc([1024], F32)
    junk = A.alloc([1024], BF16)
    ss = A.alloc([8], F32)
    rs = A.alloc([8], F32)
    allmix = [kmix + "x"] + [kmix + "a%d" % t for t in range(16)]
    for t in range(16):
        r0 = tok0 + t * 128
        p.op("sp", lambda e, t=t, r0=r0: e.dma_start(out=h2[:, t, :], in_=C.h1d[r0:r0 + 128, :]), r=["h1d"], w=["h2_%d" % (t // 4)], dma="h2ld")
        b0 = 2 + 2 * (t % 3)
        for n in range(2):
            for kc in range(16):
                p.op("pe", lambda e, t=t, n=n, kc=kc, b0=b0: e.matmul(ps[:, b0 + n, :], (C.attnT[:, t, kc, :] if kc < 8 else C.ssdT[:, t, kc - 8, :]), Wout[:, kc, n * 512:(n + 1) * 512], start=(kc == 0), stop=(kc == 15)),
                     r=allmix + ["Wout"], w=["ps%d" % (b0 + n)])
        kps = ["ps%d" % b0, "ps%d" % (b0 + 1)]
        fps = ps[:, b0:b0 + 2, :].rearrange("p a b -> p (a b)")
        p.op("act", lambda e, fps=fps: e.activation(junk, fps, AF.Square, accum_out=ss[:, 0:1]), r=kps, w=["oss"])
        C.rms_rstd(ss[:, 0:1], rs[:, 0:1], 1, "oss", "ors", 1024.0)
        p.op("dve", lambda e, fps=fps: e.scalar_tensor_tensor(t1, fps, rs[:, 0:1], gmp, op0=ALU.mult, op1=ALU.mult), r=kps + ["ors", kgmp], w=["ot1"])
        p.op("pool", lambda e, t=t: e.tensor_tensor(h2[:, t, :], h2[:, t, :], t1, op=ALU.add), r=["ot1", "h2_%d" % (t // 4)], w=["h2_%d" % (t // 4)])
    C.full_barrier()
    A.release(C.seq_mark)
    if "f2" in C.stages:
        gfin, kgfin = C.bcast_load("final_g", 1024, key="gfin")
        ot1 = A.alloc([1024], F32)
        ot = [ot1, ot1]
        fss = A.alloc([8], F32)
        frs = A.alloc([8], F32)
        junk2 = A.alloc([1024], BF16)

        def get_src(g):
            return h2[:, 4 * g:4 * g + 4, :], "h2_%d" % g

        def epi(g, t, res, rkey):
            i2 = t % 2
            r0 = tok0 + g * G + t * 128
            p.op("act", lambda e: e.activation(junk2, res, AF.Square, accum_out=fss[:, 0:1]), r=[rkey], w=["fss"])
            C.rms_rstd(fss[:, 0:1], frs[:, 0:1], 1, "fss", "frs", 1024.0)
            p.op("dve", lambda e: e.scalar_tensor_tensor(ot[i2], res, frs[:, 0:1], gfin, op0=ALU.mult, op1=ALU.mult), r=[rkey, "frs", kgfin], w=["otf"])
            p.op("sp", lambda e: e.dma_start(out=C.out[r0:r0 + 128, :], in_=ot[i2]), r=["otf"], w=["outd"], dma="ost")

        C.ffn_phase("ffn2", seq, get_src, epi)
    A.words = words_save


_WNAMES = ["ffn1_w_gate", "ffn1_w_up", "ffn1_w_down", "ffn2_w_gate", "ffn2_w_up", "ffn2_w_down", "w_in", "w_out"]
_VNAMES = ["ffn1_pre_g", "ffn1_post_g", "mix_pre_g", "mix_post_g", "ffn2_pre_g", "ffn2_post_g", "final_g", "ssd_norm_g",
           "attn_subln_g", "lambda_q1", "lambda_k1", "lambda_q2", "lambda_k2", "a_log_fwd", "a_log_bwd",
           "dt_bias_fwd", "dt_bias_bwd", "d_skip"]


def make_in_map(inp, xs):
    m = {"x": np.ascontiguousarray(xs, dtype=np.float32), "consts": _const_pack()}
    for n in _WNAMES:
        m[n] = np.ascontiguousarray(np.asarray(inp[n])[0], dtype=np.float32)
    for n in _VNAMES:
        m[n] = np.ascontiguousarray(np.asarray(inp[n]).reshape(1, -1), dtype=np.float32)
    cwt = np.asarray(inp["conv_w"])[0].T.reshape(12, 128, 5).transpose(1, 0, 2).reshape(128, 60)
    m["conv_w"] = np.ascontiguousarray(cwt, dtype=np.float32)
    m["conv_b"] = np.ascontiguousarray(np.asarray(inp["conv_b"]).reshape(12, 128).T, dtype=np.float32)
    return m


_CACHE = {}


def kernel(**inputs):
    x = np.asarray(inputs["x"], dtype=np.float32)
    B = x.shape[0]
    per = B // NCORES
    if "nc" not in _CACHE:
        _CACHE["nc"] = build_program(per)
    nc = _CACHE["nc"]
    in_maps = [make_in_map(inputs, x[i * per:(i + 1) * per].reshape(per * S, D)) for i in range(NCORES)]
    res = run_bass_kernel_spmd(nc, in_maps, core_ids=list(range(NCORES)))
    outs = [np.asarray(r["out"]).reshape(per, S, D) for r in res.results]
    return np.concatenate(outs, axis=0).astype(np.float32)
```

```python
import numpy as np
import concourse.bass as bass
import concourse.mybir as mybir
from concourse.bass_utils import run_bass_kernel_spmd

F32 = mybir.dt.float32
BF16 = mybir.dt.bfloat16
AF = mybir.ActivationFunctionType
ALU = mybir.AluOpType
AX = mybir.AxisListType

D = 1024
S = 2048
DFF = 2816
NJ = DFF // 128
DIN = 5664
EPS = 1e-6
NCORES = 8
G = 512
NG = S // G

SAME_ENGINE_SYNC = True


class _Op:
    __slots__ = ("eng", "fn", "deps", "dma", "dmacnt", "sig", "seq", "dmadeps", "strict")


class Prog:
    def __init__(self, nc):
        self.nc = nc
        self.ops = []
        self.last_w = {}
        self.readers = {}
        self.dma_cnt = {}

    def op(self, eng, fn, r=(), w=(), dma=None, strict=False):
        idx = len(self.ops)
        o = _Op()
        o.strict = strict
        o.eng = eng
        o.fn = fn
        o.dma = dma
        deps = set()
        for k in r:
            d = self.last_w.get(k)
            if d is not None:
                deps.add(d)
        for k in w:
            d = self.last_w.get(k)
            if d is not None:
                deps.add(d)
            for x in self.readers.get(k, ()):
                deps.add(x)
        deps.discard(idx)
        o.deps = []
        o.dmadeps = []
        for d in deps:
            od = self.ops[d]
            if od.dma is not None:
                o.dmadeps.append((od.dma, od.dmacnt))
            else:
                o.deps.append(d)
        if dma is not None:
            self.dma_cnt[dma] = self.dma_cnt.get(dma, 0) + 16
            o.dmacnt = self.dma_cnt[dma]
        else:
            o.dmacnt = 0
        for k in w:
            self.last_w[k] = idx
            self.readers[k] = []
        for k in r:
            lst = self.readers.setdefault(k, [])
            if dma is None:
                lst[:] = [x for x in lst if not (self.ops[x].eng == eng and self.ops[x].dma is None)]
            lst.append(idx)
        o.sig = False
        o.seq = 0
        self.ops.append(o)
        return idx

    def emit(self, stack):
        nc = self.nc
        ops = self.ops
        engs = {"pe": nc.tensor, "act": nc.scalar, "dve": nc.vector, "pool": nc.gpsimd, "sp": nc.sync}
        for o in ops:
            for d in o.deps:
                od = ops[d]
                if od.eng == o.eng and not o.strict:
                    if od.eng == "pe" or od.eng == "sp" or not SAME_ENGINE_SYNC:
                        continue
                od.sig = True
        cnt = {e: 0 for e in engs}
        for o in ops:
            if o.dma is None and o.sig:
                cnt[o.eng] += 1
                o.seq = cnt[o.eng]
        print("SEMCNT", cnt, "nops", len(ops), "dma", {k: v // 16 for k, v in self.dma_cnt.items() if v > 16 * 100})
        esem = {e: stack.enter_context(nc.semaphore("s_" + e)) for e in engs}
        dsem = {k: stack.enter_context(nc.semaphore("d_" + str(k))) for k in self.dma_cnt}
        waited = {e: {} for e in engs}
        for o in ops:
            E = engs[o.eng]
            wt = waited[o.eng]
            need = {}
            for d in o.deps:
                od = ops[d]
                if not od.sig:
                    continue
                if od.eng == o.eng and not o.strict and (od.eng in ("pe", "sp") or not SAME_ENGINE_SYNC):
                    continue
                if od.seq > need.get(od.eng, 0):
                    need[od.eng] = od.seq
            for e2, v in need.items():
                if wt.get(e2, 0) < v:
                    E.wait_ge(esem[e2], v)
                    wt[e2] = v
            for (k, c) in o.dmadeps:
                kk = ("dma", k)
                if wt.get(kk, 0) < c:
                    E.wait_ge(dsem[k], c)
                    wt[kk] = c
            ins = o.fn(E)
            if o.dma is not None:
                ins.then_inc(dsem[o.dma], 16)
            elif o.sig:
                ins.then_inc(esem[o.eng], 1)
        for k, c in self.dma_cnt.items():
            nc.sync.wait_ge(dsem[k], c)
        for e in engs:
            if e != "sp" and cnt[e] > 0:
                nc.sync.wait_ge(esem[e], cnt[e])


class Arena:
    def __init__(self, t32, words):
        self.t = t32
        self.words = words
        self.top = 0

    def alloc(self, shape, dtype):
        n = 1
        for s in shape:
            n *= s
        nbytes = n * (4 if dtype == F32 else 2)
        w = (nbytes + 3) // 4
        w = (w + 7) // 8 * 8
        off = self.top
        self.top += w
        assert self.top <= self.words, ("arena overflow", self.top, self.words)
        ap = self.t[:, off:off + w]
        if dtype != F32:
            ap = ap.bitcast(dtype)[:, 0:n]
        else:
            ap = ap[:, 0:n]
        if len(shape) == 2:
            return ap.rearrange("p (a b) -> p a b", a=shape[0])
        if len(shape) == 3:
            return ap.rearrange("p (a b c) -> p a b c", a=shape[0], b=shape[1])
        return ap

    def alloc_top(self, shape, dtype):
        n = 1
        for s in shape:
            n *= s
        nbytes = n * (4 if dtype == F32 else 2)
        w = (nbytes + 3) // 4
        w = (w + 7) // 8 * 8
        self.words -= w
        assert self.top <= self.words, ("arena overflow(top)", self.top, self.words)
        off = self.words
        ap = self.t[:, off:off + w]
        ap = ap.bitcast(dtype)[:, 0:n] if dtype != F32 else ap[:, 0:n]
        if len(shape) == 2:
            return ap.rearrange("p (a b) -> p a b", a=shape[0])
        if len(shape) == 3:
            return ap.rearrange("p (a b c) -> p a b c", a=shape[0], b=shape[1])
        return ap

    def mark(self):
        return self.top

    def release(self, m):
        self.top = m


NCONST = 128 * 4 + 2 * 16 * 8


def _const_pack():
    c = np.zeros((128, NCONST), np.float32)
    i = np.arange(128)
    c[:, 0:128] = np.eye(128, dtype=np.float32)
    c[:, 128:256] = (i[:, None] <= i[None, :]).astype(np.float32)
    c[:, 256:384] = (i[:, None] >= i[None, :]).astype(np.float32)
    c[:, 384:512] = 1.0
    pos = np.arange(S, dtype=np.float32)
    inv = np.power(np.float32(500000.0), -np.arange(0, 16, 2, dtype=np.float32) / np.float32(16)).astype(np.float32)
    ang = (pos[:, None] * inv[None, :]).astype(np.float32)
    cs = np.cos(ang).astype(np.float32).reshape(16, 128, 8).transpose(1, 0, 2).reshape(128, 128)
    sn = np.sin(ang).astype(np.float32).reshape(16, 128, 8).transpose(1, 0, 2).reshape(128, 128)
    c[:, 512:640] = cs
    c[:, 640:768] = sn
    return c


class Ctx:
    pass


def build_program(nseq, dbg=None, stages=("f1", "attn", "ssd", "o", "f2")):
    from contextlib import ExitStack
    nc = bass.Bass("TRN2", target_bir_lowering=False)
    T = nseq * S
    dr = lambda n, sh, dt=F32, kind="ExternalInput": nc.dram_tensor(n, sh, dt, kind=kind).ap()
    x = dr("x", [T, D])
    out = dr("out", [T, D], kind="ExternalOutput")
    consts = dr("consts", [128, NCONST])
    wnames = {"ffn1_w_gate": [D, DFF], "ffn1_w_up": [D, DFF], "ffn1_w_down": [DFF, D],
              "ffn2_w_gate": [D, DFF], "ffn2_w_up": [D, DFF], "ffn2_w_down": [DFF, D],
              "w_in": [D, DIN], "w_out": [2048, D]}
    wf = {n: dr(n, sh) for n, sh in wnames.items()}
    wb = {n: dr(n + "_bf", sh, BF16, kind="Internal") for n, sh in wnames.items()}
    vnames = {"ffn1_pre_g": 1024, "ffn1_post_g": 1024, "mix_pre_g": 1024, "mix_post_g": 1024,
              "ffn2_pre_g": 1024, "ffn2_post_g": 1024, "final_g": 1024, "ssd_norm_g": 1024,
              "attn_subln_g": 128, "lambda_q1": 64, "lambda_k1": 64, "lambda_q2": 64, "lambda_k2": 64,
              "a_log_fwd": 16, "a_log_bwd": 16, "dt_bias_fwd": 16, "dt_bias_bwd": 16,
              "d_skip": 16}
    vf = {n: dr(n, [1, k]) for n, k in vnames.items()}
    conv_w = dr("conv_w", [128, 60])
    conv_b = dr("conv_b", [128, 12])
    h1d = dr("h1_spill", [T, D], F32, kind="Internal")
    dbg_out = {}
    if dbg:
        for n, sh in dbg.items():
            if isinstance(sh, tuple):
                dbg_out[n] = dr("dbg_" + n, sh[0], sh[1], kind="ExternalOutput")
            else:
                dbg_out[n] = dr("dbg_" + n, sh, kind="ExternalOutput")

    with ExitStack() as st:
        AW = 53000
        arena_t = st.enter_context(nc.sbuf_tensor("arena", [128, AW], F32))
        ps = st.enter_context(nc.psum_tensor("ps", [128, 8, 512], F32))
        A = Arena(arena_t, AW)
        p = Prog(nc)
        C = Ctx()
        C.nc, C.p, C.A, C.ps = nc, p, A, ps
        uid = [0]

        def U(s):
            uid[0] += 1
            return "%s#%d" % (s, uid[0])

        live_regions = set()
        _orig_op = p.op

        def op(eng, fn, r=(), w=(), dma=None):
            for k in w:
                live_regions.add(k)
            for k in r:
                live_regions.add(k)
            return _orig_op(eng, fn, r=r, w=w, dma=dma)
        p.op = op

        def full_barrier():
            regs = list(live_regions)
            for e in ("pe", "act", "dve", "pool", "sp"):
                _orig_op(e, (lambda E: E.nop()), r=(), w=regs, strict=True)

        cst = A.alloc([NCONST], F32)
        p.op("sp", lambda e: e.dma_start(out=cst, in_=consts), w=["cst"], dma="cst")
        ident = A.alloc([128], BF16)
        p.op("dve", lambda e: e.tensor_copy(ident, cst[:, 0:128]), r=["cst"], w=["ident"])
        Uf32 = cst[:, 128:256]
        Ub32 = cst[:, 256:384]
        ones32 = cst[:, 384:512]
        maskf = A.alloc([128], BF16)
        maskb = A.alloc([128], BF16)
        p.op("dve", lambda e: e.tensor_copy(maskf, Uf32), r=["cst"], w=["maskf"])
        p.op("dve", lambda e: e.tensor_copy(maskb, Ub32), r=["cst"], w=["maskb"])
        cosT = cst[:, 512:640].rearrange("p (t f) -> p t f", t=16)
        sinT = cst[:, 640:768].rearrange("p (t f) -> p t f", t=16)
        negh = A.alloc([16], F32)
        p.op("pool", lambda e: e.memset(negh, -0.5), w=["negh"])

        def bcast_load(name, n, key=None):
            t = A.alloc([n], F32)
            k = key or ("v_" + name)
            p.op("sp", lambda e: e.dma_start(out=t, in_=vf[name].partition_broadcast(128)), w=[k], dma=k)
            return t, k

        gsub, k_gsub = bcast_load("attn_subln_g", 128)
        p.op("dve", lambda e: e.tensor_scalar(gsub, gsub, 1.0 - (0.8 - 0.6), None, op0=ALU.mult), r=[k_gsub], w=[k_gsub])
        lq1, k1 = bcast_load("lambda_q1", 64)
        lk1, k2 = bcast_load("lambda_k1", 64)
        lq2, k3 = bcast_load("lambda_q2", 64)
        lk2, k4 = bcast_load("lambda_k2", 64)
        lamt = A.alloc([8], F32)
        ljunk = A.alloc([64], F32)
        p.op("dve", lambda e: e.tensor_tensor(ljunk, lq1, lk1, op=ALU.mult), r=[k1, k2], w=["ljunk"])
        p.op("dve", lambda e: e.reduce_sum(lamt[:, 0:1], ljunk, axis=AX.X), r=["ljunk"], w=["lamt"])
        p.op("dve", lambda e: e.tensor_tensor(ljunk, lq2, lk2, op=ALU.mult), r=[k3, k4, "lamt"], w=["ljunk"])
        p.op("dve", lambda e: e.reduce_sum(lamt[:, 1:2], ljunk, axis=AX.X), r=["ljunk"], w=["lamt"])
        p.op("act", lambda e: e.activation(lamt[:, 0:2], lamt[:, 0:2], AF.Exp), r=["lamt"], w=["lamt"])
        p.op("dve", lambda e: e.tensor_tensor(lamt[:, 2:3], lamt[:, 0:1], lamt[:, 1:2], op=ALU.subtract), r=["lamt"], w=["lamt"])
        p.op("dve", lambda e: e.tensor_scalar(lamt[:, 2:3], lamt[:, 2:3], 0.8 - 0.6, None, op0=ALU.add), r=["lamt"], w=["lamt"])
        p.op("dve", lambda e: e.tensor_scalar(lamt[:, 3:4], lamt[:, 2:3], -1.0, None, op0=ALU.mult), r=["lamt"], w=["lamt"])
        alog = A.alloc([32], F32)
        p.op("sp", lambda e: e.dma_start(out=alog[:, 0:16], in_=vf["a_log_fwd"].partition_broadcast(128)), w=["alog"], dma="alog")
        p.op("sp", lambda e: e.dma_start(out=alog[:, 16:32], in_=vf["a_log_bwd"].partition_broadcast(128)), w=["alog"], dma="alog")
        aneg = A.alloc([32], F32)
        p.op("act", lambda e: e.activation(aneg, alog, AF.Exp), r=["alog"], w=["aneg"])
        p.op("dve", lambda e: e.tensor_scalar(aneg, aneg, -1.0, None, op0=ALU.mult), r=["aneg"], w=["aneg"])
        dtb = A.alloc([32], F32)
        p.op("sp", lambda e: e.dma_start(out=dtb[:, 0:16], in_=vf["dt_bias_fwd"].partition_broadcast(128)), w=["dtb"], dma="dtb")
        p.op("sp", lambda e: e.dma_start(out=dtb[:, 16:32], in_=vf["dt_bias_bwd"].partition_broadcast(128)), w=["dtb"], dma="dtb")
        dsk, k_dsk = bcast_load("d_skip", 16)
        cw = A.alloc([12, 5], F32)
        cb = A.alloc([12], F32)
        p.op("sp", lambda e: e.dma_start(out=cw.rearrange("p c k -> p (c k)"), in_=conv_w), w=["cw"], dma="cw")
        p.op("sp", lambda e: e.dma_start(out=cb, in_=conv_b), w=["cb"], dma="cb")
        cbh = A.alloc([12], F32)
        p.op("dve", lambda e: e.tensor_scalar(cbh, cb, 0.5, None, op0=ALU.mult), r=["cb"], w=["cbh"])

        def cast_weight(n):
            rows = wnames[n][0]
            nsplit = 4
            rs = rows // nsplit
            for i in range(nsplit):
                p.op("pool", lambda e, n=n, i=i, rs=rs: e.dma_start(out=wb[n][i * rs:(i + 1) * rs, :], in_=wf[n][i * rs:(i + 1) * rs, :]),
                     w=["wb_" + n], dma="wb_" + n)
        C.deferred_casts = []
        for n in wnames:
            if n.startswith("ffn1"):
                cast_weight(n)
            else:
                C.deferred_casts.append(n)
        C.cast_weight = cast_weight

        C.ident, C.negh = ident, negh
        junk1 = A.alloc([1024], BF16)
        base_mark = A.mark()

        def rms_rstd(ssq_ap, rstd_ap, n, kr, kw, width):
            p.op("dve", lambda e: e.tensor_scalar(rstd_ap, ssq_ap, 1.0 / width, EPS, op0=ALU.mult, op1=ALU.add), r=[kr], w=[kw])
            p.op("pool", lambda e: e.tensor_tensor(rstd_ap, rstd_ap, negh[:, 0:n], op=ALU.pow), r=[kw, "negh"], w=[kw])

        tr_ctr = [0]

        def norm_transpose(src, src_key, g_b, g_key, dstT, dst_key, dst_cols, bufs):
            i = tr_ctr[0] % 2
            tr_ctr[0] += 1
            junk, ssq, rstd, xn = bufs["junk"][i], bufs["ssq"][i], bufs["rstd"][i], bufs["xn"][i]
            kj, ks, kr, kx = "ntj%d" % i, "nts%d" % i, "ntr%d" % i, "ntx%d" % i
            bank = 0 + i
            kb = "ps%d" % bank
            p.op("act", lambda e: e.activation(junk, src, AF.Square, accum_out=ssq[:, 0:1]), r=[src_key], w=[ks])
            rms_rstd(ssq[:, 0:1], rstd[:, 0:1], 1, ks, kr, 1024.0)
            p.op("dve", lambda e: e.scalar_tensor_tensor(xn, src, rstd[:, 0:1], g_b, op0=ALU.mult, op1=ALU.mult), r=[src_key, kr, g_key], w=[kx])
            psT = ps[:, bank, :].bitcast(BF16).rearrange("p (c t) -> p c t", c=8)
            for c in range(8):
                p.op("pe", lambda e, c=c: e.transpose(psT[:, c, :], xn[:, c * 128:(c + 1) * 128], ident), r=[kx, "ident"], w=[kb])
            p.op("act", lambda e: e.activation(dstT[:, :, dst_cols], psT, AF.Copy), r=[kb], w=[dst_key])

        def alloc_norm_bufs():
            return {"junk": [junk1, junk1],
                    "ssq": [A.alloc([8], F32) for _ in range(2)],
                    "rstd": [A.alloc([8], F32) for _ in range(2)],
                    "xn": [A.alloc([1024], BF16) for _ in range(2)]}

        def ffn_phase(which, seq, get_src, epilogue):
            m = A.mark()
            wg, wu, wd = wb[which + "_w_gate"], wb[which + "_w_up"], wb[which + "_w_down"]
            gpre, kpre = bcast_load(which + "_pre_g", 1024, key="gpre")
            gpost, kpost = bcast_load(which + "_post_g", 1024, key="gpost")
            p.op("dve", lambda e: e.tensor_scalar(gpost, gpost, 0.5, None, op0=ALU.mult), r=[kpost], w=[kpost])
            nb = alloc_norm_bufs()
            hnT = [A.alloc([8, G], BF16) for _ in range(2)]
            actT = A.alloc([NJ, G], BF16)
            Wd = A.alloc([NJ, 1024], BF16)
            Wgu = [A.alloc([2, 8, 256], BF16) for _ in range(2)]
            th = [A.alloc([512], F32) for _ in range(3)]
            aa = [A.alloc([512], F32) for _ in range(2)]
            t1x = A.alloc([1024], F32)
            t1 = [t1x, t1x]
            fj = [junk1, junk1]
            fs = [A.alloc([8], F32) for _ in range(2)]
            fr = [A.alloc([8], F32) for _ in range(2)]
            NJB = 11
            wctr = 0
            srcs = {}
            pend_epi = []
            srcs[0] = get_src(0)
            for t in range(4):
                norm_transpose(srcs[0][0][:, t, :], srcs[0][1], gpre, kpre, hnT[0], "hnT0", slice(t * 128, (t + 1) * 128), nb)
            for g in range(NG):
                src, skey = srcs[g]
                hT = hnT[g % 2]
                khT = "hnT%d" % (g % 2)
                for jb in range(NJB):
                    ncols = 256
                    slot = wctr % 2
                    wctr += 1
                    W = Wgu[slot]
                    kW = "Wgu%d" % slot
                    for wi, wsrc in enumerate((wg, wu)):
                        p.op("sp", lambda e, wi=wi, wsrc=wsrc, jb=jb, ncols=ncols, W=W: e.dma_start(
                            out=W[:, wi, :, 0:ncols], in_=wsrc.rearrange("(c p) n -> p c n", p=128)[:, :, jb * 256: jb * 256 + ncols]),
                            r=["wb_" + which + ("_w_gate" if wi == 0 else "_w_up")], w=[kW], dma=kW)
                    if jb == 2:
                        for hh in range(2):
                            p.op("sp", lambda e, hh=hh: e.dma_start(out=Wd[:, hh * 11:(hh + 1) * 11, :], in_=wd.rearrange("(j p) n -> p j n", p=128)[:, hh * 11:(hh + 1) * 11, :]),
                                 r=["wb_" + which + "_w_down"], w=["Wd"], dma="Wd")
                    if jb < 4 and pend_epi:
                        epilogue(*pend_epi.pop(0))
                    if g + 1 < NG and jb == 4:
                        srcs[g + 1] = get_src(g + 1)
                    if g + 1 < NG and jb in (5, 6, 7, 8):
                        tn = jb - 5
                        nsrc, nkey = srcs[g + 1]
                        norm_transpose(nsrc[:, tn, :], nkey, gpre, kpre, hnT[(g + 1) % 2], "hnT%d" % ((g + 1) % 2), slice(tn * 128, (tn + 1) * 128), nb)
                    for jj in range(ncols // 128):
                        j = jb * 2 + jj
                        pb = 2 + 2 * (j % 3)
                        for wi in range(2):
                            for c in range(8):
                                p.op("pe", lambda e, wi=wi, c=c, jj=jj, pb=pb, W=W, hT=hT: e.matmul(ps[:, pb + wi, :], W[:, wi, c, jj * 128:(jj + 1) * 128], hT[:, c, :], start=(c == 0), stop=(c == 7)),
                                     r=[kW, khT], w=["ps%d" % (pb + wi)])
                        i2 = j % 3
                        p.op("act", lambda e, pb=pb, i2=i2: e.activation(th[i2], ps[:, pb, :], AF.Tanh, scale=0.5), r=["ps%d" % pb], w=["th%d" % i2])
                        i3 = j % 2
                        p.op("dve", lambda e, pb=pb, i2=i2, i3=i3: e.scalar_tensor_tensor(aa[i3], th[i2], 1.0, ps[:, pb, :], op0=ALU.add, op1=ALU.mult), r=["th%d" % i2, "ps%d" % pb], w=["aa%d" % i3])
                        p.op("dve", lambda e, pb=pb, i3=i3, j=j: e.scalar_tensor_tensor(actT[:, j, :], aa[i3], 0.5, ps[:, pb + 1, :], op0=ALU.mult, op1=ALU.mult), r=["aa%d" % i3, "ps%d" % (pb + 1)], w=["actT"])
                if C.deferred_casts and g == 0:
                    for n_ in C.deferred_casts:
                        C.cast_weight(n_)
                    C.deferred_casts = []
                for t in range(4):
                    pb = 2 + 2 * (t % 3)
                    i2 = t % 2
                    for n in range(2):
                        for j in range(NJ):
                            p.op("pe", lambda e, n=n, j=j, t=t, pb=pb: e.matmul(ps[:, pb + n, :], actT[:, j, t * 128:(t + 1) * 128], Wd[:, j, n * 512:(n + 1) * 512], start=(j == 0), stop=(j == NJ - 1)),
                                 r=["actT", "Wd"], w=["ps%d" % (pb + n)])
                    fps = ps[:, pb:pb + 2, :].rearrange("p a b -> p (a b)")
                    kps = ["ps%d" % pb, "ps%d" % (pb + 1)]
                    p.op("act", lambda e, fps=fps, i2=i2: e.activation(fj[i2], fps, AF.Square, accum_out=fs[i2][:, 0:1]), r=kps, w=["fs%d" % i2])
                    rms_rstd(fs[i2][:, 0:1], fr[i2][:, 0:1], 1, "fs%d" % i2, "fr%d" % i2, 1024.0)
                    p.op("dve", lambda e, fps=fps, i2=i2: e.scalar_tensor_tensor(t1[i2], fps, fr[i2][:, 0:1], gpost, op0=ALU.mult, op1=ALU.mult), r=kps + ["fr%d" % i2, kpost], w=["t1"])
                    p.op("pool", lambda e, i2=i2, t=t, src=src: e.tensor_tensor(src[:, t, :], src[:, t, :], t1[i2], op=ALU.add), r=["t1", skey], w=[skey])
                    pend_epi.append((g, t, src[:, t, :], skey))
            while pend_epi:
                epilogue(*pend_epi.pop(0))
            full_barrier()
            A.release(m)

        def dump(name, ap, keys):
            if name in dbg_out:
                p.op("sp", lambda e: e.dma_start(out=dbg_out[name], in_=ap), r=list(keys), w=["dbgo_" + name], dma="dbg")
        C.dump = dump
        C.ffn_phase = ffn_phase
        C.norm_transpose = norm_transpose
        C.alloc_norm_bufs = alloc_norm_bufs
        C.bcast_load = bcast_load
        C.full_barrier = full_barrier
        C.rms_rstd = rms_rstd
        C.U = U
        C.x, C.out, C.h1d, C.wb, C.vf, C.dbg_out = x, out, h1d, wb, vf, dbg_out
        C.cst, C.cosT, C.sinT, C.maskf, C.maskb, C.Uf32, C.Ub32, C.ones32 = cst, cosT, sinT, maskf, maskb, Uf32, Ub32, ones32
        C.gsub, C.k_gsub, C.lamt, C.aneg, C.dtb, C.dsk, C.k_dsk, C.cw, C.cb, C.cbh = gsub, k_gsub, lamt, aneg, dtb, dsk, k_dsk, cw, cb, cbh
        C.stages = stages

        for seq in range(nseq):
            run_sequence(C, seq)

        p.emit(st)
    return nc


def run_sequence(C, seq):
    p, A, ps = C.p, C.A, C.ps
    x, out, h1d, wb = C.x, C.out, C.h1d, C.wb
    U = C.U
    stages = C.stages
    tok0 = seq * S
    words_save = A.words
    seq_mark = A.mark()
    hnTm = A.alloc_top([8, S], BF16)
    kmix = "mixedT"
    khm = "hnTm"

    if "f1" in stages:
        m = A.mark()
        xt = [A.alloc([4, 1024], F32) for _ in range(2)]
        gmix, kgmix = C.bcast_load("mix_pre_g", 1024, key="gmix")
        nb2 = C.alloc_norm_bufs()

        def get_src(g):
            slot = g % 2
            k = "xt%d" % slot
            p.op("sp", lambda e: e.dma_start(out=xt[slot], in_=x[tok0 + g * G: tok0 + (g + 1) * G, :].rearrange("(t p) d -> p t d", p=128)), w=[k], dma=k)
            return xt[slot], k

        def epi(g, t, res, rkey):
            r0 = tok0 + g * G + t * 128
            p.op("sp", lambda e: e.dma_start(out=h1d[r0:r0 + 128, :], in_=res), r=[rkey], w=["h1d"], dma="h1st%d" % (g % 2))
            C.norm_transpose(res, rkey, gmix, kgmix, hnTm, khm, slice(g * G + t * 128, g * G + (t + 1) * 128), nb2)

        C.ffn_phase("ffn1", seq, get_src, epi)
        A.release(m)
    C.seq_mark = seq_mark
    C.words_save = words_save
    attnT = A.alloc([16, 8, 128], BF16)
    mixedT = attnT
    if "hn" in C.dbg_out and seq == 0:
        m = A.mark()
        tmp = A.alloc([8 * S // 4], F32)
        for q in range(4):
            p.op("dve", lambda e, q=q: e.tensor_copy(tmp, hnTm.rearrange("p c t -> p (c t)")[:, q * 4096:(q + 1) * 4096]), r=[khm], w=["dbgtmp"])
            p.op("sp", lambda e, q=q: e.dma_start(out=C.dbg_out["hn"][:, q * 4096:(q + 1) * 4096], in_=tmp), r=["dbgtmp"], w=["dbgo"], dma="dbg")
        C.full_barrier()
        A.release(m)

    if "attn" in stages:
        attention_phase(C, seq, mixedT, kmix, hnTm, khm)
    ssdT = A.alloc([16, 8, 128], BF16)
    C.attnT, C.ssdT = attnT, ssdT
    if "ssd" in stages:
        ssd_phase(C, seq, ssdT, kmix, hnTm, khm)
    for nm, tt in (("attnT", attnT), ("ssdT", ssdT)):
        if nm in C.dbg_out and seq == 0:
            m = A.mark()
            tmp = A.alloc([4096], F32)
            mf = tt.rearrange("p a b c -> p (a b c)")
            for q in range(4):
                p.op("dve", lambda e, q=q, mf=mf: e.tensor_copy(tmp, mf[:, q * 4096:(q + 1) * 4096]), r=[kmix + "x"] + [kmix + "a%d" % t for t in range(16)], w=["dbgtmp"])
                p.op("sp", lambda e, q=q, nm=nm: e.dma_start(out=C.dbg_out[nm][:, q * 4096:(q + 1) * 4096], in_=tmp), r=["dbgtmp"], w=["dbgo"], dma="dbg")
            C.full_barrier()
            A.release(m)
    A.words = words_save
    if "o" in stages:
        oproj_ffn2_phase(C, seq, mixedT, kmix)
    C.full_barrier()
    A.release(seq_mark)


def attention_phase(C, seq, mixedT, kmix, hnTm, khm):
    p, A, ps = C.p, C.A, C.ps
    ident = C.ident
    win = C.wb["w_in"].rearrange("(c p) n -> p c n", p=128)
    m = A.mark()
    QT = A.alloc([8, S], BF16)
    KT = A.alloc([8, S], BF16)
    Vp = A.alloc([16, 8, 130], BF16)
    m_in = A.mark()
    Wqkv = A.alloc([8, 1024], BF16)
    p.op("pool", lambda e: e.memset(Vp.rearrange("p a b c -> p (a b c)"), 1.0), w=["Vp"])
    qtok = [A.alloc([16, 64], BF16) for _ in range(2)]
    rta = [A.alloc([16, 8], F32) for _ in range(4)]
    rtb = [A.alloc([16, 8], F32) for _ in range(4)]
    for i in range(3):
        p.op("sp", lambda e, i=i: e.dma_start(out=Wqkv, in_=win[:, :, i * 1024:(i + 1) * 1024]), r=["wb_w_in"], w=["Wqkv"], dma="Wqkv")
        for t in range(16):
            cs_ = C.cosT[:, t, :].unsqueeze(1).to_broadcast([128, 16, 8])
            sn_ = C.sinT[:, t, :].unsqueeze(1).to_broadcast([128, 16, 8])
            b0 = 2 + 2 * (t % 3)
            for n in range(2):
                for c in range(8):
                    p.op("pe", lambda e, i=i, n=n, c=c, b0=b0, t=t: e.matmul(ps[:, b0 + n, :], hnTm[:, c, t * 128:(t + 1) * 128], Wqkv[:, c, n * 512:(n + 1) * 512], start=(c == 0), stop=(c == 7)),
                         r=[khm, "Wqkv"], w=["ps%d" % (b0 + n)])
            kps = ["ps%d" % b0, "ps%d" % (b0 + 1)]
            pv = ps[:, b0:b0 + 2, :].rearrange("p a b -> p (a b)")
            if i == 2:
                p.op("act", lambda e, pv=pv, t=t: e.activation(Vp[:, t, :, 0:128], pv.rearrange("p (h d) -> p h d", d=128), AF.Copy), r=kps, w=["Vp"])
                continue
            qv = pv.rearrange("p (h d) -> p h d", d=64)
            qt_ = qtok[i]
            kq = "qtok%d" % i
            ra, rb = rta[2 * i], rta[2 * i + 1]
            rc, rd = rtb[2 * i], rtb[2 * i + 1]
            p.op("dve", lambda e, qv=qv, ra=ra, cs_=cs_: e.tensor_tensor(ra, qv[:, :, 0:8], cs_, op=ALU.mult), r=kps + ["cst"], w=["ra%d" % i])
            p.op("dve", lambda e, qv=qv, rb=rb, sn_=sn_: e.tensor_tensor(rb, qv[:, :, 8:16], sn_, op=ALU.mult), r=kps + ["cst"], w=["rb%d" % i])
            p.op("dve", lambda e, qv=qv, rc=rc, cs_=cs_: e.tensor_tensor(rc, qv[:, :, 8:16], cs_, op=ALU.mult), r=kps + ["cst"], w=["rc%d" % i])
            p.op("dve", lambda e, qv=qv, rd=rd, sn_=sn_: e.tensor_tensor(rd, qv[:, :, 0:8], sn_, op=ALU.mult), r=kps + ["cst"], w=["rd%d" % i])
            p.op("pool", lambda e, qt_=qt_, ra=ra, rb=rb: e.tensor_tensor(qt_[:, :, 0:8], ra, rb, op=ALU.subtract), r=["ra%d" % i, "rb%d" % i], w=[kq])
            p.op("pool", lambda e, qt_=qt_, rc=rc, rd=rd: e.tensor_tensor(qt_[:, :, 8:16], rc, rd, op=ALU.add), r=["rc%d" % i, "rd%d" % i], w=[kq])
            p.op("act", lambda e, qt_=qt_, qv=qv: e.activation(qt_[:, :, 16:64], qv[:, :, 16:64], AF.Copy), r=kps, w=[kq])
            tb = t % 2
            psT = ps[:, tb, :].bitcast(BF16).rearrange("p (c t) -> p c t", c=8)
            qf = qt_.rearrange("p a b -> p (a b)")
            for h in range(8):
                p.op("pe", lambda e, h=h, psT=psT, qf=qf: e.transpose(psT[:, h, :], qf[:, h * 128:(h + 1) * 128], ident), r=[kq, "ident"], w=["ps%d" % tb])
            dst = QT if i == 0 else KT
            p.op("act", lambda e, dst=dst, psT=psT, t=t: e.activation(dst[:, :, t * 128:(t + 1) * 128], psT, AF.Copy), r=["ps%d" % tb], w=["QT" if i == 0 else "KT"])
    C.full_barrier()
    A.release(m_in)
    Qpad = [A.alloc([2, S], BF16) for _ in range(2)]
    for i_ in range(2):
        p.op("pool", lambda e, i_=i_: e.memset(Qpad[i_].rearrange("p a b -> p (a b)"), 0.0), w=["Qpad%d" % i_])
    NSB = 3
    SB = (0, 1, 6)
    PF = 2
    E = [A.alloc([512], BF16) for _ in range(NSB)]
    rr = [A.alloc([8], F32) for _ in range(2)]
    o1 = [A.alloc([128], F32) for _ in range(2)]
    o2 = [A.alloc([128], F32) for _ in range(2)]
    ss = [A.alloc([8], F32) for _ in range(2)]
    rs = [A.alloc([8], F32) for _ in range(2)]
    at = [A.alloc([128], BF16) for _ in range(2)]
    junk = A.alloc([128], BF16)
    import os as _os
    _nh = int(_os.environ.get("ATTN_HEADS_DBG", "8"))
    steps = [(h, qg, kt) for h in range(_nh) for qg in range(8) for kt in range(16)]

    def load_qpad(h):
        qp = Qpad[h % 2]
        kqp = "Qpad%d" % (h % 2)
        p.op("act", lambda e, qp=qp, h=h: e.activation(qp[0:64, 0, :], QT[0:64, h, :], AF.Copy), r=["QT"], w=[kqp])
        p.op("pool", lambda e, qp=qp, h=h: e.tensor_copy(qp[64:128, 1, :], QT[64:128, h, :]), r=["QT"], w=[kqp])

    def issue_qk(i):
        h, qg, kt = steps[i]
        sb = SB[i % NSB]
        qp = Qpad[h % 2]
        kqp = "Qpad%d" % (h % 2)
        for cm in range(2):
            p.op("pe", lambda e, sb=sb, cm=cm, h=h, kt=kt, qg=qg, qp=qp: e.matmul(ps[:, sb, cm * 256:(cm + 1) * 256], KT[:, h, kt * 128:(kt + 1) * 128], qp[:, cm, qg * 256:(qg + 1) * 256], start=True, stop=True),
                 r=["KT", kqp], w=["ps%d" % sb])

    if _nh > 0:
        load_qpad(0)
        if _nh > 1:
            load_qpad(1)
    for i in range(min(PF, len(steps))):
        issue_qk(i)
    ectr = 0
    for i, (h, qg, kt) in enumerate(steps):
        accs = (2, 3) if ((h * 8 + qg) % 2 == 0) else (4, 5)
        sl = i % NSB
        sb = SB[sl]
        p.op("act", lambda e, sb=sb, sl=sl: e.activation(E[sl], ps[:, sb, :], AF.Exp, scale=0.125), r=["ps%d" % sb], w=["E%d" % sl])
        if i + PF < len(steps):
            issue_qk(i + PF)
        for cm in range(2):
            for qt in range(2):
                p.op("pe", lambda e, sl=sl, cm=cm, qt=qt, kt=kt, h=h, accs=accs: e.matmul(ps[:, accs[cm], qt * 130:(qt + 1) * 130], E[sl][:, cm * 256 + qt * 128: cm * 256 + (qt + 1) * 128], Vp[:, kt, h, 0:130], start=(kt == 0 and qt == 0), stop=(kt == 15), skip_group_check=True),
                     r=["E%d" % sl, "Vp"], w=["ps%d" % accs[cm]])
        if kt == 15 and qg == 7 and h + 2 < _nh:
            load_qpad(h + 2)
        if kt == 15:
            for qt in range(2):
                t = qg * 2 + qt
                i2 = ectr % 2
                ectr += 1
                ka = ["ps%d" % accs[0], "ps%d" % accs[1]]
                O1 = ps[:, accs[0], qt * 130: qt * 130 + 128]
                O2 = ps[:, accs[1], qt * 130: qt * 130 + 128]
                s1 = ps[:, accs[0], qt * 130 + 128: qt * 130 + 129]
                s2 = ps[:, accs[1], qt * 130 + 128: qt * 130 + 129]
                r_ = rr[i2]
                p.op("dve", lambda e, r_=r_, s1=s1: e.reciprocal(r_[:, 0:1], s1), r=ka, w=["rr%d" % i2])
                p.op("dve", lambda e, r_=r_, s2=s2: e.reciprocal(r_[:, 1:2], s2), r=ka, w=["rr%d" % i2])
                p.op("dve", lambda e, r_=r_: e.tensor_tensor(r_[:, 1:2], r_[:, 1:2], C.lamt[:, 3:4], op=ALU.mult), r=["rr%d" % i2, "lamt"], w=["rr%d" % i2])
                p.op("dve", lambda e, r_=r_, O1=O1, i2=i2: e.tensor_scalar(o1[i2], O1, r_[:, 0:1], None, op0=ALU.mult), r=ka + ["rr%d" % i2], w=["o1%d" % i2])
                p.op("dve", lambda e, r_=r_, O2=O2, i2=i2: e.scalar_tensor_tensor(o2[i2], O2, r_[:, 1:2], o1[i2], op0=ALU.mult, op1=ALU.add), r=ka + ["rr%d" % i2, "o1%d" % i2], w=["o2%d" % i2])
                p.op("act", lambda e, i2=i2: e.activation(junk, o2[i2], AF.Square, accum_out=ss[i2][:, 0:1]), r=["o2%d" % i2], w=["ss%d" % i2])
                C.rms_rstd(ss[i2][:, 0:1], rs[i2][:, 0:1], 1, "ss%d" % i2, "rs%d" % i2, 128.0)
                p.op("dve", lambda e, i2=i2: e.scalar_tensor_tensor(at[i2], o2[i2], rs[i2][:, 0:1], C.gsub, op0=ALU.mult, op1=ALU.mult), r=["o2%d" % i2, "rs%d" % i2, C.k_gsub], w=["at%d" % i2])
                psT = ps[:, 7, i2 * 64:(i2 + 1) * 64].bitcast(BF16)
                p.op("pe", lambda e, psT=psT, i2=i2: e.transpose(psT, at[i2], ident), r=["at%d" % i2, "ident"], w=["ps7_%d" % i2])
                p.op("act", lambda e, psT=psT, t=t, h=h: e.activation(mixedT[:, t, h, :], psT, AF.Copy), r=["ps7_%d" % i2], w=[kmix + "a%d" % t])
    C.full_barrier()
    A.release(m)


def ssd_phase(C, seq, mixedT, kmix, hnTm, khm):
    p, A, ps = C.p, C.A, C.ps
    ident = C.ident
    win = C.wb["w_in"].rearrange("(c p) n -> p c n", p=128)
    m0 = A.mark()
    zs = A.alloc([16, 1024], BF16)
    BCT = A.alloc([4, S], BF16)
    dtt = A.alloc([16, 32], F32)
    dat = A.alloc([16, 32], F32)
    m1 = A.mark()
    Wz = A.alloc([8, 1024], BF16)
    p.op("sp", lambda e: e.dma_start(out=Wz, in_=win[:, :, 3072:4096]), r=["wb_w_in"], w=["Wz"], dma="Wz")
    th = [A.alloc([1024], F32) for _ in range(2)]
    for t in range(16):
        b0 = 2 + 2 * (t % 2)
        i2 = t % 2
        for n in range(2):
            for c in range(8):
                p.op("pe", lambda e, n=n, c=c, b0=b0, t=t: e.matmul(ps[:, b0 + n, :], hnTm[:, c, t * 128:(t + 1) * 128], Wz[:, c, n * 512:(n + 1) * 512], start=(c == 0), stop=(c == 7)),
                     r=[khm, "Wz"], w=["ps%d" % (b0 + n)])
        kps = ["ps%d" % b0, "ps%d" % (b0 + 1)]
        pv = ps[:, b0:b0 + 2, :].rearrange("p a b -> p (a b)")
        p.op("act", lambda e, pv=pv, i2=i2: e.activation(th[i2], pv, AF.Tanh, scale=0.5), r=kps, w=["zth%d" % i2])
        p.op("dve", lambda e, pv=pv, i2=i2, t=t: e.scalar_tensor_tensor(zs[:, t, :], th[i2], 1.0, pv, op0=ALU.add, op1=ALU.mult), r=kps + ["zth%d" % i2], w=["zs"])
    C.full_barrier()
    A.release(m1)
    import os as _os
    _lvl = int(_os.environ.get("SSD_DBG", "9"))
    if _lvl <= 0:
        C.full_barrier()
        A.release(m0)
        return
    Wxs = [A.alloc([8, 128], BF16) for _ in range(2)]
    Wdt = A.alloc([8, 32], BF16)
    p.op("sp", lambda e: e.dma_start(out=Wdt, in_=win[:, :, 5632:5664]), r=["wb_w_in"], w=["Wdt"], dma="Wdt")
    dps = ps[:, 6, :].rearrange("p (t k) -> p t k", k=32)
    for t in range(16):
        for c in range(8):
            p.op("pe", lambda e, c=c, t=t: e.matmul(dps[:, t, :], hnTm[:, c, t * 128:(t + 1) * 128], Wdt[:, c, :], start=(c == 0), stop=(c == 7)), r=[khm, "Wdt"], w=["ps6"])
    p.op("dve", lambda e: e.tensor_tensor(dtt, dps, C.dtb.unsqueeze(1).to_broadcast([128, 16, 32]), op=ALU.add), r=["ps6", "dtb"], w=["dtt"])
    dflat = dtt.rearrange("p a b -> p (a b)")
    _dtl = int(_os.environ.get("SSD_DT", "9"))
    if _dtl >= 2:
        p.op("act", lambda e, dflat=dflat: e.activation(dflat, dflat, AF.Exp), r=["dtt"], w=["dtt"])
    if _dtl >= 3:
        p.op("act", lambda e, dflat=dflat: e.activation(dflat, dflat, AF.Ln, bias=C.ones32[:, 0:1]), r=["dtt", "cst"], w=["dtt"])
    if _dtl >= 4:
        p.op("dve", lambda e: e.tensor_tensor(dat, dtt, C.aneg.unsqueeze(1).to_broadcast([128, 16, 32]), op=ALU.mult), r=["dtt", "aneg"], w=["dat"])
    C.dump("dtt", dtt.rearrange("p a b -> p (a b)"), ["dtt"])
    C.dump("dat", dat.rearrange("p a b -> p (a b)"), ["dat"])
    C.dump("zs", zs.rearrange("p a b -> p (a b)"), ["zs"])
    if _lvl <= 1:
        C.full_barrier()
        A.release(m0)
        return
    xpad = A.alloc([S + 4], F32)
    acc = A.alloc([S], F32)
    cth = A.alloc([S], F32)
    xsT = A.alloc([S], BF16)
    p.op("pool", lambda e: e.memset(xpad, 0.0), w=["xpad"])
    for k in range(12):
        Wx = Wxs[k % 2]
        kWx = "Wx%d" % (k % 2)
        p.op("sp", lambda e, k=k, Wx=Wx: e.dma_start(out=Wx, in_=win[:, :, 4096 + k * 128: 4096 + (k + 1) * 128]), r=["wb_w_in"], w=[kWx], dma=kWx)
        for tg in range(4):
            for c in range(8):
                p.op("pe", lambda e, Wx=Wx, tg=tg, c=c: e.matmul(ps[:, 2 + tg, :], Wx[:, c, :], hnTm[:, c, tg * 512:(tg + 1) * 512], start=(c == 0), stop=(c == 7)),
                     r=[khm, kWx], w=["ps%d" % (2 + tg)])
            p.op("act", lambda e, tg=tg: e.activation(xpad[:, 2 + tg * 512: 2 + (tg + 1) * 512], ps[:, 2 + tg, :], AF.Copy), r=["ps%d" % (2 + tg)], w=["xpad"])
        p.op("dve", lambda e, k=k: e.tensor_scalar(acc, xpad[:, 0:S], C.cw[:, k, 0:1], None, op0=ALU.mult), r=["xpad", "cw"], w=["acc"])
        for j in range(1, 5):
            p.op("dve", lambda e, k=k, j=j: e.scalar_tensor_tensor(acc, xpad[:, j:j + S], C.cw[:, k, j:j + 1], acc, op0=ALU.mult, op1=ALU.add), r=["xpad", "cw", "acc"], w=["acc"])
        p.op("act", lambda e, k=k: e.activation(cth, acc, AF.Tanh, bias=C.cbh[:, k:k + 1], scale=0.5), r=["acc", "cbh"], w=["cth"])
        p.op("dve", lambda e, k=k: e.tensor_scalar(acc, acc, C.cb[:, k:k + 1], 0.5, op0=ALU.add, op1=ALU.mult), r=["acc", "cb", "cth"], w=["acc"])
        if k < 8:
            p.op("dve", lambda e: e.scalar_tensor_tensor(xsT, cth, 1.0, acc, op0=ALU.add, op1=ALU.mult), r=["cth", "acc"], w=["xsT"])
            for tq in range(2):
                pb = tq
                psT = ps[:, pb, :].bitcast(BF16).rearrange("p (t c) -> p t c", t=8)
                for tt in range(8):
                    t = tq * 8 + tt
                    p.op("pe", lambda e, psT=psT, tt=tt, t=t: e.transpose(psT[:, tt, :], xsT[:, t * 128:(t + 1) * 128], ident), r=["xsT", "ident"], w=["ps%d" % pb])
                p.op("act", lambda e, psT=psT, tq=tq, k=k: e.activation(mixedT[:, tq * 8:(tq + 1) * 8, k, :], psT, AF.Copy), r=["ps%d" % pb], w=[kmix + "x"])
        else:
            p.op("dve", lambda e, k=k: e.scalar_tensor_tensor(BCT[:, k - 8, :], cth, 1.0, acc, op0=ALU.add, op1=ALU.mult), r=["cth", "acc"], w=["BCT"])
    C.full_barrier()
    A.release(m1)
    C.dump("BCT", BCT.rearrange("p a b -> p (a b)"), ["BCT"])
    C.dump("xs", mixedT.rearrange("p a b c -> p (a b c)"), [kmix + "x"])
    C.full_barrier()
    if _lvl <= 2:
        A.release(m0)
        return
    A.words = C.words_save
    ysum = A.alloc([16, 1024], BF16)
    csc = A.alloc([16], F32)
    ecol = A.alloc([16], F32)
    dend = A.alloc([16], F32)
    cd = A.alloc([16], F32)
    Ah = A.alloc([16, 128], F32)
    dec = A.alloc([16, 128], F32)
    MT = A.alloc([16, 128], BF16)
    CBm = A.alloc([2, 128], BF16)
    xdt = A.alloc([16, 64], BF16)
    xdw = A.alloc([16, 64], BF16)
    Btok = A.alloc([2, 128], BF16)
    prev = A.alloc([16, 64], F32)
    pbf = A.alloc([16, 64], BF16)
    yc = A.alloc([16, 64], F32)
    y2 = A.alloc([16, 64], F32)
    ssq = A.alloc([8], F32)
    rst = A.alloc([8], F32)
    ynb = A.alloc([1024], BF16)
    junk = A.alloc([1024], BF16)
    gssd, kgssd = C.bcast_load("ssd_norm_g", 1024, key="gssd")

    def xs_tile(c):
        return mixedT[:, c, :, :].rearrange("p k (a b) -> p (k a) b", a=2)

    for d in (1, 0):
        Ud = C.Ub32 if d == 1 else C.Uf32
        maskd = C.maskb if d == 1 else C.maskf
        p.op("pool", lambda e: e.memset(prev.rearrange("p a b -> p (a b)"), 0.0), w=["prev"])
        p.op("pool", lambda e: e.memset(pbf.rearrange("p a b -> p (a b)"), 0.0), w=["pbf"])
        order = range(15, -1, -1) if d == 1 else range(16)
        if _lvl <= 3:
            order = list(order)[:1]
        _nch = int(_os.environ.get("SSD_NCH", "16"))
        order = list(order)[:_nch]
        if _lvl <= 4 and d == 0:
            break
        for c in order:
            da_c = dat[:, c, 16 * d:16 * d + 16]
            dt_c = dtt[:, c, 16 * d:16 * d + 16]
            xs = xs_tile(c)
            kx = kmix + "x"
            cols = slice(c * 128, (c + 1) * 128)
            p.op("pe", lambda e, da_c=da_c, Ud=Ud: e.matmul(ps[:, 0, 0:16], Ud, da_c, start=True, stop=True), r=["dat", "cst"], w=["ps0"])
            p.op("pe", lambda e, da_c=da_c: e.matmul(ps[:, 0, 16:32], C.ones32, da_c, start=True, stop=True), r=["dat", "cst"], w=["ps0"])
            p.op("dve", lambda e: e.tensor_copy(csc, ps[:, 0, 0:16]), r=["ps0"], w=["csc"])
            p.op("act", lambda e: e.activation(ecol, ps[:, 0, 0:16], AF.Exp), r=["ps0"], w=["ecol"])
            p.op("act", lambda e: e.activation(cd, ps[:, 0, 16:32], AF.Exp), r=["ps0"], w=["cd"])
            p.op("dve", lambda e: e.tensor_tensor(dend, ps[:, 0, 16:32], csc, op=ALU.subtract), r=["ps0", "csc"], w=["dend"])
            p.op("act", lambda e: e.activation(dend, dend, AF.Exp), r=["dend"], w=["dend"])
            p.op("pool", lambda e, da_c=da_c, Ud=Ud: e.tensor_tensor(Ah, Ud.unsqueeze(1).to_broadcast([128, 16, 128]), da_c.unsqueeze(2).to_broadcast([128, 16, 128]), op=ALU.mult), r=["dat", "cst"], w=["Ah"])
            for q in range(4):
                p.op("pe", lambda e, q=q: e.matmul(ps[:, 2 + q, :], C.ones32, Ah[:, 4 * q:4 * q + 4, :].rearrange("p a b -> p (a b)"), start=True, stop=True), r=["Ah", "cst"], w=["ps%d" % (2 + q)])
                p.op("dve", lambda e, q=q: e.tensor_tensor(dec[:, 4 * q:4 * q + 4, :], ps[:, 2 + q, :].rearrange("p (a b) -> p a b", a=4), csc[:, 4 * q:4 * q + 4].unsqueeze(2).to_broadcast([128, 4, 128]), op=ALU.subtract), r=["ps%d" % (2 + q), "csc"], w=["dec"])
            dflat = dec.rearrange("p a b -> p (a b)")
            p.op("dve", lambda e, dflat=dflat: e.tensor_scalar_min(dflat, dflat, 0.0), r=["dec"], w=["dec"])
            p.op("act", lambda e, dflat=dflat: e.activation(dflat, dflat, AF.Exp), r=["dec"], w=["dec"])
            for g in range(2):
                p.op("pe", lambda e, g=g, cols=cols: e.matmul(ps[:, 1, g * 128:(g + 1) * 128], BCT[:, g, cols], BCT[:, 2 + g, cols], start=True, stop=True), r=["BCT"], w=["ps1"])
            p.op("dve", lambda e, maskd=maskd: e.tensor_tensor(CBm, ps[:, 1, 0:256].rearrange("p (a b) -> p a b", a=2), maskd.unsqueeze(1).to_broadcast([128, 2, 128]), op=ALU.mult), r=["ps1", "maskf", "maskb"], w=["CBm"])
            for g in range(2):
                p.op("pool", lambda e, g=g: e.tensor_tensor(MT[:, 8 * g:8 * g + 8, :], dec[:, 8 * g:8 * g + 8, :], CBm[:, g, :].unsqueeze(1).to_broadcast([128, 8, 128]), op=ALU.mult), r=["dec", "CBm"], w=["MT"])
            p.op("dve", lambda e, xs=xs, dt_c=dt_c: e.tensor_tensor(xdt, xs, dt_c.unsqueeze(2).to_broadcast([128, 16, 64]), op=ALU.mult), r=[kx, "dtt"], w=["xdt"])
            p.op("pool", lambda e: e.tensor_tensor(xdw, xdt, dend.unsqueeze(2).to_broadcast([128, 16, 64]), op=ALU.mult), r=["xdt", "dend"], w=["xdw"])
            xdf = xdt.rearrange("p a b -> p (a b)")
            for h in range(16):
                p.op("pe", lambda e, h=h, xdf=xdf: e.matmul(ps[:, 6 + h // 8, (h % 8) * 64:(h % 8 + 1) * 64], MT[:, h, :], xdf[:, h * 64:(h + 1) * 64], start=True, stop=True), r=["MT", "xdt"], w=["ps%d" % (6 + h // 8)])
            pbff = pbf.rearrange("p a b -> p (a b)")
            for g in range(2):
                p.op("pe", lambda e, g=g, cols=cols, pbff=pbff: e.matmul(ps[:, 2 + g, :], BCT[:, 2 + g, cols], pbff[:, g * 512:(g + 1) * 512], start=True, stop=True), r=["BCT", "pbf"], w=["ps%d" % (2 + g)])
            yo = ps[:, 2:4, :].rearrange("p a b -> p (a b)").rearrange("p (h d) -> p h d", d=64)
            yd = ps[:, 6:8, :].rearrange("p a b -> p (a b)").rearrange("p (h d) -> p h d", d=64)
            p.op("dve", lambda e, yo=yo: e.tensor_tensor(yc, yo, ecol.unsqueeze(2).to_broadcast([128, 16, 64]), op=ALU.mult), r=["ps2", "ps3", "ecol"], w=["yc"])
            p.op("dve", lambda e, yd=yd: e.tensor_tensor(yc, yd, yc, op=ALU.add), r=["ps6", "ps7", "yc"], w=["yc"])
            for g in range(2):
                psB = ps[:, 1, 256 + g * 64: 256 + (g + 1) * 64].bitcast(BF16)
                p.op("pe", lambda e, g=g, cols=cols, psB=psB: e.transpose(psB, BCT[:, g, cols], ident), r=["BCT", "ident"], w=["ps1"])
            p.op("act", lambda e: e.activation(Btok, ps[:, 1, 256:384].bitcast(BF16).rearrange("p (a b) -> p a b", a=2), AF.Copy), r=["ps1"], w=["Btok"])
            xwf = xdw.rearrange("p a b -> p (a b)")
            for g in range(2):
                p.op("pe", lambda e, g=g, xwf=xwf: e.matmul(ps[:, 4 + g, :], Btok[:, g, :], xwf[:, g * 512:(g + 1) * 512], start=True, stop=True), r=["Btok", "xdw"], w=["ps%d" % (4 + g)])
            st_ = ps[:, 4:6, :].rearrange("p a b -> p (a b)").rearrange("p (h d) -> p h d", d=64)
            p.op("pool", lambda e: e.tensor_tensor(prev, prev, cd.unsqueeze(2).to_broadcast([128, 16, 64]), op=ALU.mult), r=["prev", "cd"], w=["prev"])
            p.op("dve", lambda e, st_=st_: e.tensor_tensor(prev, prev, st_, op=ALU.add), r=["prev", "ps4", "ps5"], w=["prev"])
            p.op("act", lambda e: e.activation(pbf, prev, AF.Copy), r=["prev"], w=["pbf"])
            if d == 1 and c == 15:
                C.dump("csc", csc, ["csc"]); C.dump("ecol", ecol, ["ecol"]); C.dump("dend", dend, ["dend"]); C.dump("cd", cd, ["cd"])
                C.dump("dec", dec.rearrange("p a b -> p (a b)"), ["dec"]); C.dump("MT", MT.rearrange("p a b -> p (a b)"), ["MT"])
                C.dump("yc", yc.rearrange("p a b -> p (a b)"), ["yc"]); C.dump("prev", prev.rearrange("p a b -> p (a b)"), ["prev"])
                C.dump("xdt", xdt.rearrange("p a b -> p (a b)"), ["xdt"]); C.dump("CBm", CBm.rearrange("p a b -> p (a b)"), ["CBm"])
            if d == 1:
                p.op("act", lambda e, c=c: e.activation(ysum[:, c, :], yc.rearrange("p a b -> p (a b)"), AF.Copy), r=["yc"], w=["ysum"])
            else:
                _ol = int(_os.environ.get("SSD_OUT", "9"))
                if _ol >= 1:
                    p.op("pool", lambda e, c=c: e.tensor_tensor(yc, yc, ysum[:, c, :].rearrange("p (a b) -> p a b", a=16), op=ALU.add), r=["yc", "ysum"], w=["yc"])
                    p.op("pool", lambda e, xs=xs: e.tensor_tensor(y2, xs, C.dsk.unsqueeze(2).to_broadcast([128, 16, 64]), op=ALU.mult), r=[kx, C.k_dsk], w=["y2"])
                    p.op("pool", lambda e: e.tensor_tensor(yc, yc, y2, op=ALU.add), r=["yc", "y2"], w=["yc"])
                ycf = yc.rearrange("p a b -> p (a b)")
                if _ol >= 2:
                    p.op("dve", lambda e, c=c, ycf=ycf: e.tensor_tensor(ycf, ycf, zs[:, c, :], op=ALU.mult), r=["yc", "zs"], w=["yc"])
                if _ol >= 3:
                    p.op("act", lambda e, ycf=ycf: e.activation(junk, ycf, AF.Square, accum_out=ssq[:, 0:1]), r=["yc"], w=["ssq"])
                    p.op("dve", lambda e: e.tensor_scalar(rst[:, 0:1], ssq[:, 0:1], 1.0 / 1024, 4.0 * EPS, op0=ALU.mult, op1=ALU.add), r=["ssq"], w=["rst"])
                    p.op("pool", lambda e: e.tensor_tensor(rst[:, 0:1], rst[:, 0:1], C.negh[:, 0:1], op=ALU.pow), r=["rst", "negh"], w=["rst"])
                if _ol >= 4:
                    p.op("dve", lambda e, ycf=ycf, c=c: e.scalar_tensor_tensor(ysum[:, c, :], ycf, rst[:, 0:1], gssd, op0=ALU.mult, op1=ALU.mult), r=["yc", "rst", kgssd, "ysum"], w=["ysum"])
    C.full_barrier()
    for c in range(16):
        bank = c % 2
        psT = ps[:, bank, :].bitcast(BF16).rearrange("p (k t) -> p k t", k=8)
        for k in range(8):
            p.op("pe", lambda e, k=k, psT=psT, c=c: e.transpose(psT[:, k, :], ysum[:, c, k * 128:(k + 1) * 128], ident), r=["ysum", "ident"], w=["ps%d" % bank])
        p.op("act", lambda e, c=c, psT=psT: e.activation(mixedT[:, c, :, :], psT, AF.Copy), r=["ps%d" % bank], w=[kmix + "x"])
    C.full_barrier()
    A.release(m0)


def oproj_ffn2_phase(C, seq, mixedT, kmix):
    p, A, ps = C.p, C.A, C.ps
    tok0 = seq * S
    words_save = A.words
    h2 = A.alloc_top([16, 1024], F32)
    m = A.mark()
    Wout = A.alloc([16, 1024], BF16)
    p.op("sp", lambda e: e.dma_start(out=Wout, in_=C.wb["w_out"].rearrange("(k p) n -> p k n", p=128)), r=["wb_w_out"], w=["Wout"], dma="Wout")
    gmp, kgmp = C.bcast_load("mix_post_g", 1024, key="gmp")
    t1 = A.alloc([1024], F32)
    junk = A.alloc([1024], BF16)
    ss = A.alloc([8], F32)
    rs = A.alloc([8], F32)
    allmix = [kmix + "x"] + [kmix + "a%d" % t for t in range(16)]
    for t in range(16):
        r0 = tok0 + t * 128
        p.op("sp", lambda e, t=t, r0=r0: e.dma_start(out=h2[:, t, :], in_=C.h1d[r0:r0 + 128, :]), r=["h1d"], w=["h2_%d" % (t // 4)], dma="h2ld")
        b0 = 2 + 2 * (t % 3)
        for n in range(2):
            for kc in range(16):
                p.op("pe", lambda e, t=t, n=n, kc=kc, b0=b0: e.matmul(ps[:, b0 + n, :], (C.attnT[:, t, kc, :] if kc < 8 else C.ssdT[:, t, kc - 8, :]), Wout[:, kc, n * 512:(n + 1) * 512], start=(kc == 0), stop=(kc == 15)),
                     r=allmix + ["Wout"], w=["ps%d" % (b0 + n)])
        kps = ["ps%d" % b0, "ps%d" % (b0 + 1)]
        fps = ps[:, b0:b0 + 2, :].rearrange("p a b -> p (a b)")
        p.op("act", lambda e, fps=fps: e.activation(junk, fps, AF.Square, accum_out=ss[:, 0:1]), r=kps, w=["oss"])
        C.rms_rstd(ss[:, 0:1], rs[:, 0:1], 1, "oss", "ors", 1024.0)
        p.op("dve", lambda e, fps=fps: e.scalar_tensor_tensor(t1, fps, rs[:, 0:1], gmp, op0=ALU.mult, op1=ALU.mult), r=kps + ["ors", kgmp], w=["ot1"])
        p.op("pool", lambda e, t=t: e.tensor_tensor(h2[:, t, :], h2[:, t, :], t1, op=ALU.add), r=["ot1", "h2_%d" % (t // 4)], w=["h2_%d" % (t // 4)])
    C.full_barrier()
    A.release(C.seq_mark)
    if "f2" in C.stages:
        gfin, kgfin = C.bcast_load("final_g", 1024, key="gfin")
        ot1 = A.alloc([1024], F32)
        ot = [ot1, ot1]
        fss = A.alloc([8], F32)
        frs = A.alloc([8], F32)
        junk2 = A.alloc([1024], BF16)

        def get_src(g):
            return h2[:, 4 * g:4 * g + 4, :], "h2_%d" % g

        def epi(g, t, res, rkey):
            i2 = t % 2
            r0 = tok0 + g * G + t * 128
            p.op("act", lambda e: e.activation(junk2, res, AF.Square, accum_out=fss[:, 0:1]), r=[rkey], w=["fss"])
            C.rms_rstd(fss[:, 0:1], frs[:, 0:1], 1, "fss", "frs", 1024.0)
            p.op("dve", lambda e: e.scalar_tensor_tensor(ot[i2], res, frs[:, 0:1], gfin, op0=ALU.mult, op1=ALU.mult), r=[rkey, "frs", kgfin], w=["otf"])
            p.op("sp", lambda e: e.dma_start(out=C.out[r0:r0 + 128, :], in_=ot[i2]), r=["otf"], w=["outd"], dma="ost")

        C.ffn_phase("ffn2", seq, get_src, epi)
    A.words = words_save


_WNAMES = ["ffn1_w_gate", "ffn1_w_up", "ffn1_w_down", "ffn2_w_gate", "ffn2_w_up", "ffn2_w_down", "w_in", "w_out"]
_VNAMES = ["ffn1_pre_g", "ffn1_post_g", "mix_pre_g", "mix_post_g", "ffn2_pre_g", "ffn2_post_g", "final_g", "ssd_norm_g",
           "attn_subln_g", "lambda_q1", "lambda_k1", "lambda_q2", "lambda_k2", "a_log_fwd", "a_log_bwd",
           "dt_bias_fwd", "dt_bias_bwd", "d_skip"]


def make_in_map(inp, xs):
    m = {"x": np.ascontiguousarray(xs, dtype=np.float32), "consts": _const_pack()}
    for n in _WNAMES:
        m[n] = np.ascontiguousarray(np.asarray(inp[n])[0], dtype=np.float32)
    for n in _VNAMES:
        m[n] = np.ascontiguousarray(np.asarray(inp[n]).reshape(1, -1), dtype=np.float32)
    cwt = np.asarray(inp["conv_w"])[0].T.reshape(12, 128, 5).transpose(1, 0, 2).reshape(128, 60)
    m["conv_w"] = np.ascontiguousarray(cwt, dtype=np.float32)
    m["conv_b"] = np.ascontiguousarray(np.asarray(inp["conv_b"]).reshape(12, 128).T, dtype=np.float32)
    return m


_CACHE = {}


def kernel(**inputs):
    x = np.asarray(inputs["x"], dtype=np.float32)
    B = x.shape[0]
    per = B // NCORES
    if "nc" not in _CACHE:
        _CACHE["nc"] = build_program(per)
    nc = _CACHE["nc"]
    in_maps = [make_in_map(inputs, x[i * per:(i + 1) * per].reshape(per * S, D)) for i in range(NCORES)]
    res = run_bass_kernel_spmd(nc, in_maps, core_ids=list(range(NCORES)))
    outs = [np.asarray(r["out"]).reshape(per, S, D) for r in res.results]
    return np.concatenate(outs, axis=0).astype(np.float32)
```

```python
import numpy as np
import concourse.bass as bass
import concourse.mybir as mybir
from concourse.bass_utils import run_bass_kernel_spmd

F32 = mybir.dt.float32
BF16 = mybir.dt.bfloat16
AF = mybir.ActivationFunctionType
ALU = mybir.AluOpType
AX = mybir.AxisListType

D = 1024
S = 2048
DFF = 2816
NJ = DFF // 128
DIN = 5664
EPS = 1e-6
NCORES = 8
G = 512
NG = S // G

SAME_ENGINE_SYNC = True


class _Op:
    __slots__ = ("eng", "fn", "deps", "dma", "dmacnt", "sig", "seq", "dmadeps", "strict")


class Prog:
    def __init__(self, nc):
        self.nc = nc
        self.ops = []
        self.last_w = {}
        self.readers = {}
        self.dma_cnt = {}

    def op(self, eng, fn, r=(), w=(), dma=None, strict=False):
        idx = len(self.ops)
        o = _Op()
        o.strict = strict
        o.eng = eng
        o.fn = fn
        o.dma = dma
        deps = set()
        for k in r:
            d = self.last_w.get(k)
            if d is not None:
                deps.add(d)
        for k in w:
            d = self.last_w.get(k)
            if d is not None:
                deps.add(d)
            for x in self.readers.get(k, ()):
                deps.add(x)
        deps.discard(idx)
        o.deps = []
        o.dmadeps = []
        for d in deps:
            od = self.ops[d]
            if od.dma is not None:
                o.dmadeps.append((od.dma, od.dmacnt))
            else:
                o.deps.append(d)
        if dma is not None:
            self.dma_cnt[dma] = self.dma_cnt.get(dma, 0) + 16
            o.dmacnt = self.dma_cnt[dma]
        else:
            o.dmacnt = 0
        for k in w:
            self.last_w[k] = idx
            self.readers[k] = []
        for k in r:
            lst = self.readers.setdefault(k, [])
            if dma is None:
                lst[:] = [x for x in lst if not (self.ops[x].eng == eng and self.ops[x].dma is None)]
            lst.append(idx)
        o.sig = False
        o.seq = 0
        self.ops.append(o)
        return idx

    def emit(self, stack):
        nc = self.nc
        ops = self.ops
        engs = {"pe": nc.tensor, "act": nc.scalar, "dve": nc.vector, "pool": nc.gpsimd, "sp": nc.sync}
        for o in ops:
            for d in o.deps:
                od = ops[d]
                if od.eng == o.eng and not o.strict:
                    if od.eng == "pe" or od.eng == "sp" or not SAME_ENGINE_SYNC:
                        continue
                od.sig = True
        cnt = {e: 0 for e in engs}
        for o in ops:
            if o.dma is None and o.sig:
                cnt[o.eng] += 1
                o.seq = cnt[o.eng]
        print("SEMCNT", cnt, "nops", len(ops), "dma", {k: v // 16 for k, v in self.dma_cnt.items() if v > 16 * 100})
        esem = {e: stack.enter_context(nc.semaphore("s_" + e)) for e in engs}
        dsem = {k: stack.enter_context(nc.semaphore("d_" + str(k))) for k in self.dma_cnt}
        waited = {e: {} for e in engs}
        for o in ops:
            E = engs[o.eng]
            wt = waited[o.eng]
            need = {}
            for d in o.deps:
                od = ops[d]
                if not od.sig:
                    continue
                if od.eng == o.eng and not o.strict and (od.eng in ("pe", "sp") or not SAME_ENGINE_SYNC):
                    continue
                if od.seq > need.get(od.eng, 0):
                    need[od.eng] = od.seq
            for e2, v in need.items():
                if wt.get(e2, 0) < v:
                    E.wait_ge(esem[e2], v)
                    wt[e2] = v
            for (k, c) in o.dmadeps:
                kk = ("dma", k)
                if wt.get(kk, 0) < c:
                    E.wait_ge(dsem[k], c)
                    wt[kk] = c
            ins = o.fn(E)
            if o.dma is not None:
                ins.then_inc(dsem[o.dma], 16)
            elif o.sig:
                ins.then_inc(esem[o.eng], 1)
        for k, c in self.dma_cnt.items():
            nc.sync.wait_ge(dsem[k], c)
        for e in engs:
            if e != "sp" and cnt[e] > 0:
                nc.sync.wait_ge(esem[e], cnt[e])


class Arena:
    def __init__(self, t32, words):
        self.t = t32
        self.words = words
        self.top = 0

    def alloc(self, shape, dtype):
        n = 1
        for s in shape:
            n *= s
        nbytes = n * (4 if dtype == F32 else 2)
        w = (nbytes + 3) // 4
        w = (w + 7) // 8 * 8
        off = self.top
        self.top += w
        assert self.top <= self.words, ("arena overflow", self.top, self.words)
        ap = self.t[:, off:off + w]
        if dtype != F32:
            ap = ap.bitcast(dtype)[:, 0:n]
        else:
            ap = ap[:, 0:n]
        if len(shape) == 2:
            return ap.rearrange("p (a b) -> p a b", a=shape[0])
        if len(shape) == 3:
            return ap.rearrange("p (a b c) -> p a b c", a=shape[0], b=shape[1])
        return ap

    def alloc_top(self, shape, dtype):
        n = 1
        for s in shape:
            n *= s
        nbytes = n * (4 if dtype == F32 else 2)
        w = (nbytes + 3) // 4
        w = (w + 7) // 8 * 8
        self.words -= w
        assert self.top <= self.words, ("arena overflow(top)", self.top, self.words)
        off = self.words
        ap = self.t[:, off:off + w]
        ap = ap.bitcast(dtype)[:, 0:n] if dtype != F32 else ap[:, 0:n]
        if len(shape) == 2:
            return ap.rearrange("p (a b) -> p a b", a=shape[0])
        if len(shape) == 3:
            return ap.rearrange("p (a b c) -> p a b c", a=shape[0], b=shape[1])
        return ap

    def mark(self):
        return self.top

    def release(self, m):
        self.top = m


NCONST = 128 * 4 + 2 * 16 * 8


def _const_pack():
    c = np.zeros((128, NCONST), np.float32)
    i = np.arange(128)
    c[:, 0:128] = np.eye(128, dtype=np.float32)
    c[:, 128:256] = (i[:, None] <= i[None, :]).astype(np.float32)
    c[:, 256:384] = (i[:, None] >= i[None, :]).astype(np.float32)
    c[:, 384:512] = 1.0
    pos = np.arange(S, dtype=np.float32)
    inv = np.power(np.float32(500000.0), -np.arange(0, 16, 2, dtype=np.float32) / np.float32(16)).astype(np.float32)
    ang = (pos[:, None] * inv[None, :]).astype(np.float32)
    cs = np.cos(ang).astype(np.float32).reshape(16, 128, 8).transpose(1, 0, 2).reshape(128, 128)
    sn = np.sin(ang).astype(np.float32).reshape(16, 128, 8).transpose(1, 0, 2).reshape(128, 128)
    c[:, 512:640] = cs
    c[:, 640:768] = sn
    return c


class Ctx:
    pass


def build_program(nseq, dbg=None, stages=("f1", "attn", "ssd", "o", "f2")):
    from contextlib import ExitStack
    nc = bass.Bass("TRN2", target_bir_lowering=False)
    T = nseq * S
    dr = lambda n, sh, dt=F32, kind="ExternalInput": nc.dram_tensor(n, sh, dt, kind=kind).ap()
    x = dr("x", [T, D])
    out = dr("out", [T, D], kind="ExternalOutput")
    consts = dr("consts", [128, NCONST])
    wnames = {"ffn1_w_gate": [D, DFF], "ffn1_w_up": [D, DFF], "ffn1_w_down": [DFF, D],
              "ffn2_w_gate": [D, DFF], "ffn2_w_up": [D, DFF], "ffn2_w_down": [DFF, D],
              "w_in": [D, DIN], "w_out": [2048, D]}
    wf = {n: dr(n, sh) for n, sh in wnames.items()}
    wb = {n: dr(n + "_bf", sh, BF16, kind="Internal") for n, sh in wnames.items()}
    vnames = {"ffn1_pre_g": 1024, "ffn1_post_g": 1024, "mix_pre_g": 1024, "mix_post_g": 1024,
              "ffn2_pre_g": 1024, "ffn2_post_g": 1024, "final_g": 1024, "ssd_norm_g": 1024,
              "attn_subln_g": 128, "lambda_q1": 64, "lambda_k1": 64, "lambda_q2": 64, "lambda_k2": 64,
              "a_log_fwd": 16, "a_log_bwd": 16, "dt_bias_fwd": 16, "dt_bias_bwd": 16,
              "d_skip": 16}
    vf = {n: dr(n, [1, k]) for n, k in vnames.items()}
    conv_w = dr("conv_w", [128, 60])
    conv_b = dr("conv_b", [128, 12])
    h1d = dr("h1_spill", [T, D], F32, kind="Internal")
    dbg_out = {}
    if dbg:
        for n, sh in dbg.items():
            if isinstance(sh, tuple):
                dbg_out[n] = dr("dbg_" + n, sh[0], sh[1], kind="ExternalOutput")
            else:
                dbg_out[n] = dr("dbg_" + n, sh, kind="ExternalOutput")

    with ExitStack() as st:
        AW = 53000
        arena_t = st.enter_context(nc.sbuf_tensor("arena", [128, AW], F32))
        ps = st.enter_context(nc.psum_tensor("ps", [128, 8, 512], F32))
        A = Arena(arena_t, AW)
        p = Prog(nc)
        C = Ctx()
        C.nc, C.p, C.A, C.ps = nc, p, A, ps
        uid = [0]

        def U(s):
            uid[0] += 1
            return "%s#%d" % (s, uid[0])

        live_regions = set()
        _orig_op = p.op

        def op(eng, fn, r=(), w=(), dma=None):
            for k in w:
                live_regions.add(k)
            for k in r:
                live_regions.add(k)
            return _orig_op(eng, fn, r=r, w=w, dma=dma)
        p.op = op

        def full_barrier():
            regs = list(live_regions)
            for e in ("pe", "act", "dve", "pool", "sp"):
                _orig_op(e, (lambda E: E.nop()), r=(), w=regs, strict=True)

        cst = A.alloc([NCONST], F32)
        p.op("sp", lambda e: e.dma_start(out=cst, in_=consts), w=["cst"], dma="cst")
        ident = A.alloc([128], BF16)
        p.op("dve", lambda e: e.tensor_copy(ident, cst[:, 0:128]), r=["cst"], w=["ident"])
        Uf32 = cst[:, 128:256]
        Ub32 = cst[:, 256:384]
        ones32 = cst[:, 384:512]
        maskf = A.alloc([128], BF16)
        maskb = A.alloc([128], BF16)
        p.op("dve", lambda e: e.tensor_copy(maskf, Uf32), r=["cst"], w=["maskf"])
        p.op("dve", lambda e: e.tensor_copy(maskb, Ub32), r=["cst"], w=["maskb"])
        cosT = cst[:, 512:640].rearrange("p (t f) -> p t f", t=16)
        sinT = cst[:, 640:768].rearrange("p (t f) -> p t f", t=16)
        negh = A.alloc([16], F32)
        p.op("pool", lambda e: e.memset(negh, -0.5), w=["negh"])

        def bcast_load(name, n, key=None):
            t = A.alloc([n], F32)
            k = key or ("v_" + name)
            p.op("sp", lambda e: e.dma_start(out=t, in_=vf[name].partition_broadcast(128)), w=[k], dma=k)
            return t, k

        gsub, k_gsub = bcast_load("attn_subln_g", 128)
        p.op("dve", lambda e: e.tensor_scalar(gsub, gsub, 1.0 - (0.8 - 0.6), None, op0=ALU.mult), r=[k_gsub], w=[k_gsub])
        lq1, k1 = bcast_load("lambda_q1", 64)
        lk1, k2 = bcast_load("lambda_k1", 64)
        lq2, k3 = bcast_load("lambda_q2", 64)
        lk2, k4 = bcast_load("lambda_k2", 64)
        lamt = A.alloc([8], F32)
        ljunk = A.alloc([64], F32)
        p.op("dve", lambda e: e.tensor_tensor(ljunk, lq1, lk1, op=ALU.mult), r=[k1, k2], w=["ljunk"])
        p.op("dve", lambda e: e.reduce_sum(lamt[:, 0:1], ljunk, axis=AX.X), r=["ljunk"], w=["lamt"])
        p.op("dve", lambda e: e.tensor_tensor(ljunk, lq2, lk2, op=ALU.mult), r=[k3, k4, "lamt"], w=["ljunk"])
        p.op("dve", lambda e: e.reduce_sum(lamt[:, 1:2], ljunk, axis=AX.X), r=["ljunk"], w=["lamt"])
        p.op("act", lambda e: e.activation(lamt[:, 0:2], lamt[:, 0:2], AF.Exp), r=["lamt"], w=["lamt"])
        p.op("dve", lambda e: e.tensor_tensor(lamt[:, 2:3], lamt[:, 0:1], lamt[:, 1:2], op=ALU.subtract), r=["lamt"], w=["lamt"])
        p.op("dve", lambda e: e.tensor_scalar(lamt[:, 2:3], lamt[:, 2:3], 0.8 - 0.6, None, op0=ALU.add), r=["lamt"], w=["lamt"])
        p.op("dve", lambda e: e.tensor_scalar(lamt[:, 3:4], lamt[:, 2:3], -1.0, None, op0=ALU.mult), r=["lamt"], w=["lamt"])
        alog = A.alloc([32], F32)
        p.op("sp", lambda e: e.dma_start(out=alog[:, 0:16], in_=vf["a_log_fwd"].partition_broadcast(128)), w=["alog"], dma="alog")
        p.op("sp", lambda e: e.dma_start(out=alog[:, 16:32], in_=vf["a_log_bwd"].partition_broadcast(128)), w=["alog"], dma="alog")
        aneg = A.alloc([32], F32)
        p.op("act", lambda e: e.activation(aneg, alog, AF.Exp), r=["alog"], w=["aneg"])
        p.op("dve", lambda e: e.tensor_scalar(aneg, aneg, -1.0, None, op0=ALU.mult), r=["aneg"], w=["aneg"])
        dtb = A.alloc([32], F32)
        p.op("sp", lambda e: e.dma_start(out=dtb[:, 0:16], in_=vf["dt_bias_fwd"].partition_broadcast(128)), w=["dtb"], dma="dtb")
        p.op("sp", lambda e: e.dma_start(out=dtb[:, 16:32], in_=vf["dt_bias_bwd"].partition_broadcast(128)), w=["dtb"], dma="dtb")
        dsk, k_dsk = bcast_load("d_skip", 16)
        cw = A.alloc([12, 5], F32)
        cb = A.alloc([12], F32)
        p.op("sp", lambda e: e.dma_start(out=cw.rearrange("p c k -> p (c k)"), in_=conv_w), w=["cw"], dma="cw")
        p.op("sp", lambda e: e.dma_start(out=cb, in_=conv_b), w=["cb"], dma="cb")
        cbh = A.alloc([12], F32)
        p.op("dve", lambda e: e.tensor_scalar(cbh, cb, 0.5, None, op0=ALU.mult), r=["cb"], w=["cbh"])

        def cast_weight(n):
            rows = wnames[n][0]
            nsplit = 4
            rs = rows // nsplit
            for i in range(nsplit):
                p.op("pool", lambda e, n=n, i=i, rs=rs: e.dma_start(out=wb[n][i * rs:(i + 1) * rs, :], in_=wf[n][i * rs:(i + 1) * rs, :]),
                     w=["wb_" + n], dma="wb_" + n)
        C.deferred_casts = []
        for n in wnames:
            if n.startswith("ffn1"):
                cast_weight(n)
            else:
                C.deferred_casts.append(n)
        C.cast_weight = cast_weight

        C.ident, C.negh = ident, negh
        junk1 = A.alloc([1024], BF16)
        base_mark = A.mark()

        def rms_rstd(ssq_ap, rstd_ap, n, kr, kw, width):
            p.op("dve", lambda e: e.tensor_scalar(rstd_ap, ssq_ap, 1.0 / width, EPS, op0=ALU.mult, op1=ALU.add), r=[kr], w=[kw])
            p.op("pool", lambda e: e.tensor_tensor(rstd_ap, rstd_ap, negh[:, 0:n], op=ALU.pow), r=[kw, "negh"], w=[kw])

        tr_ctr = [0]

        def norm_transpose(src, src_key, g_b, g_key, dstT, dst_key, dst_cols, bufs):
            i = tr_ctr[0] % 2
            tr_ctr[0] += 1
            junk, ssq, rstd, xn = bufs["junk"][i], bufs["ssq"][i], bufs["rstd"][i], bufs["xn"][i]
            kj, ks, kr, kx = "ntj%d" % i, "nts%d" % i, "ntr%d" % i, "ntx%d" % i
            bank = 0 + i
            kb = "ps%d" % bank
            p.op("act", lambda e: e.activation(junk, src, AF.Square, accum_out=ssq[:, 0:1]), r=[src_key], w=[ks])
            rms_rstd(ssq[:, 0:1], rstd[:, 0:1], 1, ks, kr, 1024.0)
            p.op("dve", lambda e: e.scalar_tensor_tensor(xn, src, rstd[:, 0:1], g_b, op0=ALU.mult, op1=ALU.mult), r=[src_key, kr, g_key], w=[kx])
            psT = ps[:, bank, :].bitcast(BF16).rearrange("p (c t) -> p c t", c=8)
            for c in range(8):
                p.op("pe", lambda e, c=c: e.transpose(psT[:, c, :], xn[:, c * 128:(c + 1) * 128], ident), r=[kx, "ident"], w=[kb])
            p.op("act", lambda e: e.activation(dstT[:, :, dst_cols], psT, AF.Copy), r=[kb], w=[dst_key])

        def alloc_norm_bufs():
            return {"junk": [junk1, junk1],
                    "ssq": [A.alloc([8], F32) for _ in range(2)],
                    "rstd": [A.alloc([8], F32) for _ in range(2)],
                    "xn": [A.alloc([1024], BF16) for _ in range(2)]}

        def ffn_phase(which, seq, get_src, epilogue):
            m = A.mark()
            wg, wu, wd = wb[which + "_w_gate"], wb[which + "_w_up"], wb[which + "_w_down"]
            gpre, kpre = bcast_load(which + "_pre_g", 1024, key="gpre")
            gpost, kpost = bcast_load(which + "_post_g", 1024, key="gpost")
            p.op("dve", lambda e: e.tensor_scalar(gpost, gpost, 0.5, None, op0=ALU.mult), r=[kpost], w=[kpost])
            nb = alloc_norm_bufs()
            hnT = [A.alloc([8, G], BF16) for _ in range(2)]
            actT = A.alloc([NJ, G], BF16)
            Wd = A.alloc([NJ, 1024], BF16)
            Wgu = [A.alloc([2, 8, 256], BF16) for _ in range(2)]
            th = [A.alloc([512], F32) for _ in range(3)]
            aa = [A.alloc([512], F32) for _ in range(2)]
            t1x = A.alloc([1024], F32)
            t1 = [t1x, t1x]
            fj = [junk1, junk1]
            fs = [A.alloc([8], F32) for _ in range(2)]
            fr = [A.alloc([8], F32) for _ in range(2)]
            NJB = 11
            wctr = 0
            srcs = {}
            pend_epi = []
            srcs[0] = get_src(0)
            for t in range(4):
                norm_transpose(srcs[0][0][:, t, :], srcs[0][1], gpre, kpre, hnT[0], "hnT0", slice(t * 128, (t + 1) * 128), nb)
            for g in range(NG):
                src, skey = srcs[g]
                hT = hnT[g % 2]
                khT = "hnT%d" % (g % 2)
                for jb in range(NJB):
                    ncols = 256
                    slot = wctr % 2
                    wctr += 1
                    W = Wgu[slot]
                    kW = "Wgu%d" % slot
                    for wi, wsrc in enumerate((wg, wu)):
                        p.op("sp", lambda e, wi=wi, wsrc=wsrc, jb=jb, ncols=ncols, W=W: e.dma_start(
                            out=W[:, wi, :, 0:ncols], in_=wsrc.rearrange("(c p) n -> p c n", p=128)[:, :, jb * 256: jb * 256 + ncols]),
                            r=["wb_" + which + ("_w_gate" if wi == 0 else "_w_up")], w=[kW], dma=kW)
                    if jb == 2:
                        for hh in range(2):
                            p.op("sp", lambda e, hh=hh: e.dma_start(out=Wd[:, hh * 11:(hh + 1) * 11, :], in_=wd.rearrange("(j p) n -> p j n", p=128)[:, hh * 11:(hh + 1) * 11, :]),
                                 r=["wb_" + which + "_w_down"], w=["Wd"], dma="Wd")
                    if jb < 4 and pend_epi:
                        epilogue(*pend_epi.pop(0))
                    if g + 1 < NG and jb == 4:
                        srcs[g + 1] = get_src(g + 1)
                    if g + 1 < NG and jb in (5, 6, 7, 8):
                        tn = jb - 5
                        nsrc, nkey = srcs[g + 1]
                        norm_transpose(nsrc[:, tn, :], nkey, gpre, kpre, hnT[(g + 1) % 2], "hnT%d" % ((g + 1) % 2), slice(tn * 128, (tn + 1) * 128), nb)
                    for jj in range(ncols // 128):
                        j = jb * 2 + jj
                        pb = 2 + 2 * (j % 3)
                        for wi in range(2):
                            for c in range(8):
                                p.op("pe", lambda e, wi=wi, c=c, jj=jj, pb=pb, W=W, hT=hT: e.matmul(ps[:, pb + wi, :], W[:, wi, c, jj * 128:(jj + 1) * 128], hT[:, c, :], start=(c == 0), stop=(c == 7)),
                                     r=[kW, khT], w=["ps%d" % (pb + wi)])
                        i2 = j % 3
                        p.op("act", lambda e, pb=pb, i2=i2: e.activation(th[i2], ps[:, pb, :], AF.Tanh, scale=0.5), r=["ps%d" % pb], w=["th%d" % i2])
                        i3 = j % 2
                        p.op("dve", lambda e, pb=pb, i2=i2, i3=i3: e.scalar_tensor_tensor(aa[i3], th[i2], 1.0, ps[:, pb, :], op0=ALU.add, op1=ALU.mult), r=["th%d" % i2, "ps%d" % pb], w=["aa%d" % i3])
                        p.op("dve", lambda e, pb=pb, i3=i3, j=j: e.scalar_tensor_tensor(actT[:, j, :], aa[i3], 0.5, ps[:, pb + 1, :], op0=ALU.mult, op1=ALU.mult), r=["aa%d" % i3, "ps%d" % (pb + 1)], w=["actT"])
                if C.deferred_casts and g == 0:
                    for n_ in C.deferred_casts:
                        C.cast_weight(n_)
                    C.deferred_casts = []
                for t in range(4):
                    pb = 2 + 2 * (t % 3)
                    i2 = t % 2
                    for n in range(2):
                        for j in range(NJ):
                            p.op("pe", lambda e, n=n, j=j, t=t, pb=pb: e.matmul(ps[:, pb + n, :], actT[:, j, t * 128:(t + 1) * 128], Wd[:, j, n * 512:(n + 1) * 512], start=(j == 0), stop=(j == NJ - 1)),
                                 r=["actT", "Wd"], w=["ps%d" % (pb + n)])
                    fps = ps[:, pb:pb + 2, :].rearrange("p a b -> p (a b)")
                    kps = ["ps%d" % pb, "ps%d" % (pb + 1)]
                    p.op("act", lambda e, fps=fps, i2=i2: e.activation(fj[i2], fps, AF.Square, accum_out=fs[i2][:, 0:1]), r=kps, w=["fs%d" % i2])
                    rms_rstd(fs[i2][:, 0:1], fr[i2][:, 0:1], 1, "fs%d" % i2, "fr%d" % i2, 1024.0)
                    p.op("dve", lambda e, fps=fps, i2=i2: e.scalar_tensor_tensor(t1[i2], fps, fr[i2][:, 0:1], gpost, op0=ALU.mult, op1=ALU.mult), r=kps + ["fr%d" % i2, kpost], w=["t1"])
                    p.op("pool", lambda e, i2=i2, t=t, src=src: e.tensor_tensor(src[:, t, :], src[:, t, :], t1[i2], op=ALU.add), r=["t1", skey], w=[skey])
                    pend_epi.append((g, t, src[:, t, :], skey))
            while pend_epi:
                epilogue(*pend_epi.pop(0))
            full_barrier()
            A.release(m)

        def dump(name, ap, keys):
            if name in dbg_out:
                p.op("sp", lambda e: e.dma_start(out=dbg_out[name], in_=ap), r=list(keys), w=["dbgo_" + name], dma="dbg")
        C.dump = dump
        C.ffn_phase = ffn_phase
        C.norm_transpose = norm_transpose
        C.alloc_norm_bufs = alloc_norm_bufs
        C.bcast_load = bcast_load
        C.full_barrier = full_barrier
        C.rms_rstd = rms_rstd
        C.U = U
        C.x, C.out, C.h1d, C.wb, C.vf, C.dbg_out = x, out, h1d, wb, vf, dbg_out
        C.cst, C.cosT, C.sinT, C.maskf, C.maskb, C.Uf32, C.Ub32, C.ones32 = cst, cosT, sinT, maskf, maskb, Uf32, Ub32, ones32
        C.gsub, C.k_gsub, C.lamt, C.aneg, C.dtb, C.dsk, C.k_dsk, C.cw, C.cb, C.cbh = gsub, k_gsub, lamt, aneg, dtb, dsk, k_dsk, cw, cb, cbh
        C.stages = stages

        for seq in range(nseq):
            run_sequence(C, seq)

        p.emit(st)
    return nc


def run_sequence(C, seq):
    p, A, ps = C.p, C.A, C.ps
    x, out, h1d, wb = C.x, C.out, C.h1d, C.wb
    U = C.U
    stages = C.stages
    tok0 = seq * S
    words_save = A.words
    seq_mark = A.mark()
    hnTm = A.alloc_top([8, S], BF16)
    kmix = "mixedT"
    khm = "hnTm"

    if "f1" in stages:
        m = A.mark()
        xt = [A.alloc([4, 1024], F32) for _ in range(2)]
        gmix, kgmix = C.bcast_load("mix_pre_g", 1024, key="gmix")
        nb2 = C.alloc_norm_bufs()

        def get_src(g):
            slot = g % 2
            k = "xt%d" % slot
            p.op("sp", lambda e: e.dma_start(out=xt[slot], in_=x[tok0 + g * G: tok0 + (g + 1) * G, :].rearrange("(t p) d -> p t d", p=128)), w=[k], dma=k)
            return xt[slot], k

        def epi(g, t, res, rkey):
            r0 = tok0 + g * G + t * 128
            p.op("sp", lambda e: e.dma_start(out=h1d[r0:r0 + 128, :], in_=res), r=[rkey], w=["h1d"], dma="h1st%d" % (g % 2))
            C.norm_transpose(res, rkey, gmix, kgmix, hnTm, khm, slice(g * G + t * 128, g * G + (t + 1) * 128), nb2)

        C.ffn_phase("ffn1", seq, get_src, epi)
        A.release(m)
    C.seq_mark = seq_mark
    C.words_save = words_save
    attnT = A.alloc([16, 8, 128], BF16)
    mixedT = attnT
    if "hn" in C.dbg_out and seq == 0:
        m = A.mark()
        tmp = A.alloc([8 * S // 4], F32)
        for q in range(4):
            p.op("dve", lambda e, q=q: e.tensor_copy(tmp, hnTm.rearrange("p c t -> p (c t)")[:, q * 4096:(q + 1) * 4096]), r=[khm], w=["dbgtmp"])
            p.op("sp", lambda e, q=q: e.dma_start(out=C.dbg_out["hn"][:, q * 4096:(q + 1) * 4096], in_=tmp), r=["dbgtmp"], w=["dbgo"], dma="dbg")
        C.full_barrier()
        A.release(m)

    if "attn" in stages:
        attention_phase(C, seq, mixedT, kmix, hnTm, khm)
    ssdT = A.alloc([16, 8, 128], BF16)
    C.attnT, C.ssdT = attnT, ssdT
    if "ssd" in stages:
        ssd_phase(C, seq, ssdT, kmix, hnTm, khm)
    for nm, tt in (("attnT", attnT), ("ssdT", ssdT)):
        if nm in C.dbg_out and seq == 0:
            m = A.mark()
            tmp = A.alloc([4096], F32)
            mf = tt.rearrange("p a b c -> p (a b c)")
            for q in range(4):
                p.op("dve", lambda e, q=q, mf=mf: e.tensor_copy(tmp, mf[:, q * 4096:(q + 1) * 4096]), r=[kmix + "x"] + [kmix + "a%d" % t for t in range(16)], w=["dbgtmp"])
                p.op("sp", lambda e, q=q, nm=nm: e.dma_start(out=C.dbg_out[nm][:, q * 4096:(q + 1) * 4096], in_=tmp), r=["dbgtmp"], w=["dbgo"], dma="dbg")
            C.full_barrier()
            A.release(m)
    A.words = words_save
    if "o" in stages:
        oproj_ffn2_phase(C, seq, mixedT, kmix)
    C.full_barrier()
    A.release(seq_mark)


def attention_phase(C, seq, mixedT, kmix, hnTm, khm):
    p, A, ps = C.p, C.A, C.ps
    ident = C.ident
    win = C.wb["w_in"].rearrange("(c p) n -> p c n", p=128)
    m = A.mark()
    QT = A.alloc([8, S], BF16)
    KT = A.alloc([8, S], BF16)
    Vp = A.alloc([16, 8, 130], BF16)
    m_in = A.mark()
    Wqkv = A.alloc([8, 1024], BF16)
    p.op("pool", lambda e: e.memset(Vp.rearrange("p a b c -> p (a b c)"), 1.0), w=["Vp"])
    qtok = [A.alloc([16, 64], BF16) for _ in range(2)]
    rta = [A.alloc([16, 8], F32) for _ in range(4)]
    rtb = [A.alloc([16, 8], F32) for _ in range(4)]
    for i in range(3):
        p.op("sp", lambda e, i=i: e.dma_start(out=Wqkv, in_=win[:, :, i * 1024:(i + 1) * 1024]), r=["wb_w_in"], w=["Wqkv"], dma="Wqkv")
        for t in range(16):
            cs_ = C.cosT[:, t, :].unsqueeze(1).to_broadcast([128, 16, 8])
            sn_ = C.sinT[:, t, :].unsqueeze(1).to_broadcast([128, 16, 8])
            b0 = 2 + 2 * (t % 3)
            for n in range(2):
                for c in range(8):
                    p.op("pe", lambda e, i=i, n=n, c=c, b0=b0, t=t: e.matmul(ps[:, b0 + n, :], hnTm[:, c, t * 128:(t + 1) * 128], Wqkv[:, c, n * 512:(n + 1) * 512], start=(c == 0), stop=(c == 7)),
                         r=[khm, "Wqkv"], w=["ps%d" % (b0 + n)])
            kps = ["ps%d" % b0, "ps%d" % (b0 + 1)]
            pv = ps[:, b0:b0 + 2, :].rearrange("p a b -> p (a b)")
            if i == 2:
                p.op("act", lambda e, pv=pv, t=t: e.activation(Vp[:, t, :, 0:128], pv.rearrange("p (h d) -> p h d", d=128), AF.Copy), r=kps, w=["Vp"])
                continue
            qv = pv.rearrange("p (h d) -> p h d", d=64)
            qt_ = qtok[i]
            kq = "qtok%d" % i
            ra, rb = rta[2 * i], rta[2 * i + 1]
            rc, rd = rtb[2 * i], rtb[2 * i + 1]
            p.op("dve", lambda e, qv=qv, ra=ra, cs_=cs_: e.tensor_tensor(ra, qv[:, :, 0:8], cs_, op=ALU.mult), r=kps + ["cst"], w=["ra%d" % i])
            p.op("dve", lambda e, qv=qv, rb=rb, sn_=sn_: e.tensor_tensor(rb, qv[:, :, 8:16], sn_, op=ALU.mult), r=kps + ["cst"], w=["rb%d" % i])
            p.op("dve", lambda e, qv=qv, rc=rc, cs_=cs_: e.tensor_tensor(rc, qv[:, :, 8:16], cs_, op=ALU.mult), r=kps + ["cst"], w=["rc%d" % i])
            p.op("dve", lambda e, qv=qv, rd=rd, sn_=sn_: e.tensor_tensor(rd, qv[:, :, 0:8], sn_, op=ALU.mult), r=kps + ["cst"], w=["rd%d" % i])
            p.op("pool", lambda e, qt_=qt_, ra=ra, rb=rb: e.tensor_tensor(qt_[:, :, 0:8], ra, rb, op=ALU.subtract), r=["ra%d" % i, "rb%d" % i], w=[kq])
            p.op("pool", lambda e, qt_=qt_, rc=rc, rd=rd: e.tensor_tensor(qt_[:, :, 8:16], rc, rd, op=ALU.add), r=["rc%d" % i, "rd%d" % i], w=[kq])
            p.op("act", lambda e, qt_=qt_, qv=qv: e.activation(qt_[:, :, 16:64], qv[:, :, 16:64], AF.Copy), r=kps, w=[kq])
            tb = t % 2
            psT = ps[:, tb, :].bitcast(BF16).rearrange("p (c t) -> p c t", c=8)
            qf = qt_.rearrange("p a b -> p (a b)")
            for h in range(8):
                p.op("pe", lambda e, h=h, psT=psT, qf=qf: e.transpose(psT[:, h, :], qf[:, h * 128:(h + 1) * 128], ident), r=[kq, "ident"], w=["ps%d" % tb])
            dst = QT if i == 0 else KT
            p.op("act", lambda e, dst=dst, psT=psT, t=t: e.activation(dst[:, :, t * 128:(t + 1) * 128], psT, AF.Copy), r=["ps%d" % tb], w=["QT" if i == 0 else "KT"])
    C.full_barrier()
    A.release(m_in)
    Qpad = [A.alloc([2, S], BF16) for _ in range(2)]
    for i_ in range(2):
        p.op("pool", lambda e, i_=i_: e.memset(Qpad[i_].rearrange("p a b -> p (a b)"), 0.0), w=["Qpad%d" % i_])
    NSB = 3
    SB = (0, 1, 6)
    PF = 2
    E = [A.alloc([512], BF16) for _ in range(NSB)]
    rr = [A.alloc([8], F32) for _ in range(2)]
    o1 = [A.alloc([128], F32) for _ in range(2)]
    o2 = [A.alloc([128], F32) for _ in range(2)]
    ss = [A.alloc([8], F32) for _ in range(2)]
    rs = [A.alloc([8], F32) for _ in range(2)]
    at = [A.alloc([128], BF16) for _ in range(2)]
    junk = A.alloc([128], BF16)
    import os as _os
    _nh = int(_os.environ.get("ATTN_HEADS_DBG", "8"))
    steps = [(h, qg, kt) for h in range(_nh) for qg in range(8) for kt in range(16)]

    def load_qpad(h):
        qp = Qpad[h % 2]
        kqp = "Qpad%d" % (h % 2)
        p.op("act", lambda e, qp=qp, h=h: e.activation(qp[0:64, 0, :], QT[0:64, h, :], AF.Copy), r=["QT"], w=[kqp])
        p.op("pool", lambda e, qp=qp, h=h: e.tensor_copy(qp[64:128, 1, :], QT[64:128, h, :]), r=["QT"], w=[kqp])

    def issue_qk(i):
        h, qg, kt = steps[i]
        sb = SB[i % NSB]
        qp = Qpad[h % 2]
        kqp = "Qpad%d" % (h % 2)
        for cm in range(2):
            p.op("pe", lambda e, sb=sb, cm=cm, h=h, kt=kt, qg=qg, qp=qp: e.matmul(ps[:, sb, cm * 256:(cm + 1) * 256], KT[:, h, kt * 128:(kt + 1) * 128], qp[:, cm, qg * 256:(qg + 1) * 256], start=True, stop=True),
                 r=["KT", kqp], w=["ps%d" % sb])

    if _nh > 0:
        load_qpad(0)
        if _nh > 1:
            load_qpad(1)
    for i in range(min(PF, len(steps))):
        issue_qk(i)
    ectr = 0
    deferred = []
    for i, (h, qg, kt) in enumerate(steps):
        accs = (2, 3) if ((h * 8 + qg) % 2 == 0) else (4, 5)
        sl = i % NSB
        sb = SB[sl]
        p.op("act", lambda e, sb=sb, sl=sl: e.activation(E[sl], ps[:, sb, :], AF.Exp, scale=0.125), r=["ps%d" % sb], w=["E%d" % sl])
        if i + PF < len(steps):
            issue_qk(i + PF)
        for cm in range(2):
            for qt in range(2):
                p.op("pe", lambda e, sl=sl, cm=cm, qt=qt, kt=kt, h=h, accs=accs: e.matmul(ps[:, accs[cm], qt * 130:(qt + 1) * 130], E[sl][:, cm * 256 + qt * 128: cm * 256 + (qt + 1) * 128], Vp[:, kt, h, 0:130], start=(kt == 0 and qt == 0), stop=(kt == 15), skip_group_check=True),
                     r=["E%d" % sl, "Vp"], w=["ps%d" % accs[cm]])
        if kt == 15 and qg == 7 and h + 2 < _nh:
            load_qpad(h + 2)
        if kt == 15:
            for qt in range(2):
                t = qg * 2 + qt
                i2 = ectr % 2
                ectr += 1
                ka = ["ps%d" % accs[0], "ps%d" % accs[1]]
                O1 = ps[:, accs[0], qt * 130: qt * 130 + 128]
                O2 = ps[:, accs[1], qt * 130: qt * 130 + 128]
                s1 = ps[:, accs[0], qt * 130 + 128: qt * 130 + 129]
                s2 = ps[:, accs[1], qt * 130 + 128: qt * 130 + 129]
                r_ = rr[i2]
                p.op("dve", lambda e, r_=r_, s1=s1: e.reciprocal(r_[:, 0:1], s1), r=ka, w=["rr%d" % i2])
                p.op("dve", lambda e, r_=r_, s2=s2: e.reciprocal(r_[:, 1:2], s2), r=ka, w=["rr%d" % i2])
                p.op("dve", lambda e, r_=r_: e.tensor_tensor(r_[:, 1:2], r_[:, 1:2], C.lamt[:, 3:4], op=ALU.mult), r=["rr%d" % i2, "lamt"], w=["rr%d" % i2])
                p.op("dve", lambda e, r_=r_, O1=O1, i2=i2: e.tensor_scalar(o1[i2], O1, r_[:, 0:1], None, op0=ALU.mult), r=ka + ["rr%d" % i2], w=["o1%d" % i2])
                p.op("dve", lambda e, r_=r_, O2=O2, i2=i2: e.scalar_tensor_tensor(o2[i2], O2, r_[:, 1:2], o1[i2], op0=ALU.mult, op1=ALU.add), r=ka + ["rr%d" % i2, "o1%d" % i2], w=["o2%d" % i2])

                def partB(i2=i2):
                    p.op("act", lambda e, i2=i2: e.activation(junk, o2[i2], AF.Square, accum_out=ss[i2][:, 0:1]), r=["o2%d" % i2], w=["ss%d" % i2])

                def partC(i2=i2):
                    C.rms_rstd(ss[i2][:, 0:1], rs[i2][:, 0:1], 1, "ss%d" % i2, "rs%d" % i2, 128.0)
                    p.op("dve", lambda e, i2=i2: e.scalar_tensor_tensor(at[i2], o2[i2], rs[i2][:, 0:1], C.gsub, op0=ALU.mult, op1=ALU.mult), r=["o2%d" % i2, "rs%d" % i2, C.k_gsub], w=["at%d" % i2])

                def partD(i2=i2):
                    psT = ps[:, 7, i2 * 64:(i2 + 1) * 64].bitcast(BF16)
                    p.op("pe", lambda e, psT=psT, i2=i2: e.transpose(psT, at[i2], ident), r=["at%d" % i2, "ident"], w=["ps7_%d" % i2])

                def partE(i2=i2, t=t, h=h):
                    psT = ps[:, 7, i2 * 64:(i2 + 1) * 64].bitcast(BF16)
                    p.op("act", lambda e, psT=psT, t=t, h=h: e.activation(mixedT[:, t, h, :], psT, AF.Copy), r=["ps7_%d" % i2], w=[kmix + "a%d" % t])

                deferred.append((i + 4, partB))
                deferred.append((i + 6, partC))
                deferred.append((i + 9, partD))
                deferred.append((i + 11, partE))
                deferred.sort(key=lambda x_: x_[0])
        while deferred and deferred[0][0] <= i:
            deferred.pop(0)[1]()
    while deferred:
        deferred.pop(0)[1]()
    C.full_barrier()
    A.release(m)


def ssd_phase(C, seq, mixedT, kmix, hnTm, khm):
    p, A, ps = C.p, C.A, C.ps
    ident = C.ident
    win = C.wb["w_in"].rearrange("(c p) n -> p c n", p=128)
    m0 = A.mark()
    zs = A.alloc([16, 1024], BF16)
    BCT = A.alloc([4, S], BF16)
    dtt = A.alloc([16, 32], F32)
    dat = A.alloc([16, 32], F32)
    m1 = A.mark()
    Wz = A.alloc([8, 1024], BF16)
    p.op("sp", lambda e: e.dma_start(out=Wz, in_=win[:, :, 3072:4096]), r=["wb_w_in"], w=["Wz"], dma="Wz")
    th = [A.alloc([1024], F32) for _ in range(2)]
    for t in range(16):
        b0 = 2 + 2 * (t % 2)
        i2 = t % 2
        for n in range(2):
            for c in range(8):
                p.op("pe", lambda e, n=n, c=c, b0=b0, t=t: e.matmul(ps[:, b0 + n, :], hnTm[:, c, t * 128:(t + 1) * 128], Wz[:, c, n * 512:(n + 1) * 512], start=(c == 0), stop=(c == 7)),
                     r=[khm, "Wz"], w=["ps%d" % (b0 + n)])
        kps = ["ps%d" % b0, "ps%d" % (b0 + 1)]
        pv = ps[:, b0:b0 + 2, :].rearrange("p a b -> p (a b)")
        p.op("act", lambda e, pv=pv, i2=i2: e.activation(th[i2], pv, AF.Tanh, scale=0.5), r=kps, w=["zth%d" % i2])
        p.op("dve", lambda e, pv=pv, i2=i2, t=t: e.scalar_tensor_tensor(zs[:, t, :], th[i2], 1.0, pv, op0=ALU.add, op1=ALU.mult), r=kps + ["zth%d" % i2], w=["zs"])
    C.full_barrier()
    A.release(m1)
    import os as _os
    _lvl = int(_os.environ.get("SSD_DBG", "9"))
    if _lvl <= 0:
        C.full_barrier()
        A.release(m0)
        return
    Wxs = [A.alloc([8, 128], BF16) for _ in range(2)]
    Wdt = A.alloc([8, 32], BF16)
    p.op("sp", lambda e: e.dma_start(out=Wdt, in_=win[:, :, 5632:5664]), r=["wb_w_in"], w=["Wdt"], dma="Wdt")
    dps = ps[:, 6, :].rearrange("p (t k) -> p t k", k=32)
    for t in range(16):
        for c in range(8):
            p.op("pe", lambda e, c=c, t=t: e.matmul(dps[:, t, :], hnTm[:, c, t * 128:(t + 1) * 128], Wdt[:, c, :], start=(c == 0), stop=(c == 7)), r=[khm, "Wdt"], w=["ps6"])
    p.op("dve", lambda e: e.tensor_tensor(dtt, dps, C.dtb.unsqueeze(1).to_broadcast([128, 16, 32]), op=ALU.add), r=["ps6", "dtb"], w=["dtt"])
    dflat = dtt.rearrange("p a b -> p (a b)")
    _dtl = int(_os.environ.get("SSD_DT", "9"))
    if _dtl >= 2:
        p.op("act", lambda e, dflat=dflat: e.activation(dflat, dflat, AF.Exp), r=["dtt"], w=["dtt"])
    if _dtl >= 3:
        p.op("act", lambda e, dflat=dflat: e.activation(dflat, dflat, AF.Ln, bias=C.ones32[:, 0:1]), r=["dtt", "cst"], w=["dtt"])
    if _dtl >= 4:
        p.op("dve", lambda e: e.tensor_tensor(dat, dtt, C.aneg.unsqueeze(1).to_broadcast([128, 16, 32]), op=ALU.mult), r=["dtt", "aneg"], w=["dat"])
    C.dump("dtt", dtt.rearrange("p a b -> p (a b)"), ["dtt"])
    C.dump("dat", dat.rearrange("p a b -> p (a b)"), ["dat"])
    C.dump("zs", zs.rearrange("p a b -> p (a b)"), ["zs"])
    if _lvl <= 1:
        C.full_barrier()
        A.release(m0)
        return
    xpad = A.alloc([S + 4], F32)
    acc = A.alloc([S], F32)
    cth = A.alloc([S], F32)
    xsT = A.alloc([S], BF16)
    p.op("pool", lambda e: e.memset(xpad, 0.0), w=["xpad"])
    for k in range(12):
        Wx = Wxs[k % 2]
        kWx = "Wx%d" % (k % 2)
        p.op("sp", lambda e, k=k, Wx=Wx: e.dma_start(out=Wx, in_=win[:, :, 4096 + k * 128: 4096 + (k + 1) * 128]), r=["wb_w_in"], w=[kWx], dma=kWx)
        for tg in range(4):
            for c in range(8):
                p.op("pe", lambda e, Wx=Wx, tg=tg, c=c: e.matmul(ps[:, 2 + tg, :], Wx[:, c, :], hnTm[:, c, tg * 512:(tg + 1) * 512], start=(c == 0), stop=(c == 7)),
                     r=[khm, kWx], w=["ps%d" % (2 + tg)])
            p.op("act", lambda e, tg=tg: e.activation(xpad[:, 2 + tg * 512: 2 + (tg + 1) * 512], ps[:, 2 + tg, :], AF.Copy), r=["ps%d" % (2 + tg)], w=["xpad"])
        p.op("dve", lambda e, k=k: e.tensor_scalar(acc, xpad[:, 0:S], C.cw[:, k, 0:1], None, op0=ALU.mult), r=["xpad", "cw"], w=["acc"])
        for j in range(1, 5):
            p.op("dve", lambda e, k=k, j=j: e.scalar_tensor_tensor(acc, xpad[:, j:j + S], C.cw[:, k, j:j + 1], acc, op0=ALU.mult, op1=ALU.add), r=["xpad", "cw", "acc"], w=["acc"])
        p.op("act", lambda e, k=k: e.activation(cth, acc, AF.Tanh, bias=C.cbh[:, k:k + 1], scale=0.5), r=["acc", "cbh"], w=["cth"])
        p.op("dve", lambda e, k=k: e.tensor_scalar(acc, acc, C.cb[:, k:k + 1], 0.5, op0=ALU.add, op1=ALU.mult), r=["acc", "cb", "cth"], w=["acc"])
        if k < 8:
            p.op("dve", lambda e: e.scalar_tensor_tensor(xsT, cth, 1.0, acc, op0=ALU.add, op1=ALU.mult), r=["cth", "acc"], w=["xsT"])
            for tq in range(2):
                pb = tq
                psT = ps[:, pb, :].bitcast(BF16).rearrange("p (t c) -> p t c", t=8)
                for tt in range(8):
                    t = tq * 8 + tt
                    p.op("pe", lambda e, psT=psT, tt=tt, t=t: e.transpose(psT[:, tt, :], xsT[:, t * 128:(t + 1) * 128], ident), r=["xsT", "ident"], w=["ps%d" % pb])
                p.op("act", lambda e, psT=psT, tq=tq, k=k: e.activation(mixedT[:, tq * 8:(tq + 1) * 8, k, :], psT, AF.Copy), r=["ps%d" % pb], w=[kmix + "x"])
        else:
            p.op("dve", lambda e, k=k: e.scalar_tensor_tensor(BCT[:, k - 8, :], cth, 1.0, acc, op0=ALU.add, op1=ALU.mult), r=["cth", "acc"], w=["BCT"])
    C.full_barrier()
    A.release(m1)
    C.dump("BCT", BCT.rearrange("p a b -> p (a b)"), ["BCT"])
    C.dump("xs", mixedT.rearrange("p a b c -> p (a b c)"), [kmix + "x"])
    C.full_barrier()
    if _lvl <= 2:
        A.release(m0)
        return
    A.words = C.words_save
    ysum = A.alloc([16, 1024], BF16)
    csc = A.alloc([16], F32)
    ecol = A.alloc([16], F32)
    dend = A.alloc([16], F32)
    cd = A.alloc([16], F32)
    Ah = A.alloc([16, 128], F32)
    dec = A.alloc([16, 128], F32)
    MT = A.alloc([16, 128], BF16)
    CBm = A.alloc([2, 128], BF16)
    xdt = A.alloc([16, 64], BF16)
    xdw = A.alloc([16, 64], BF16)
    Btok = A.alloc([2, 128], BF16)
    prev = A.alloc([16, 64], F32)
    pbf = A.alloc([16, 64], BF16)
    yc = A.alloc([16, 64], F32)
    y2 = A.alloc([16, 64], F32)
    ssq = A.alloc([8], F32)
    rst = A.alloc([8], F32)
    ynb = A.alloc([1024], BF16)
    junk = A.alloc([1024], BF16)
    gssd, kgssd = C.bcast_load("ssd_norm_g", 1024, key="gssd")

    def xs_tile(c):
        return mixedT[:, c, :, :].rearrange("p k (a b) -> p (k a) b", a=2)

    for d in (1, 0):
        Ud = C.Ub32 if d == 1 else C.Uf32
        maskd = C.maskb if d == 1 else C.maskf
        p.op("pool", lambda e: e.memset(prev.rearrange("p a b -> p (a b)"), 0.0), w=["prev"])
        p.op("pool", lambda e: e.memset(pbf.rearrange("p a b -> p (a b)"), 0.0), w=["pbf"])
        order = range(15, -1, -1) if d == 1 else range(16)
        if _lvl <= 3:
            order = list(order)[:1]
        _nch = int(_os.environ.get("SSD_NCH", "16"))
        order = list(order)[:_nch]
        if _lvl <= 4 and d == 0:
            break
        for c in order:
            da_c = dat[:, c, 16 * d:16 * d + 16]
            dt_c = dtt[:, c, 16 * d:16 * d + 16]
            xs = xs_tile(c)
            kx = kmix + "x"
            cols = slice(c * 128, (c + 1) * 128)
            p.op("pe", lambda e, da_c=da_c, Ud=Ud: e.matmul(ps[:, 0, 0:16], Ud, da_c, start=True, stop=True), r=["dat", "cst"], w=["ps0"])
            p.op("pe", lambda e, da_c=da_c: e.matmul(ps[:, 0, 16:32], C.ones32, da_c, start=True, stop=True), r=["dat", "cst"], w=["ps0"])
            p.op("dve", lambda e: e.tensor_copy(csc, ps[:, 0, 0:16]), r=["ps0"], w=["csc"])
            p.op("act", lambda e: e.activation(ecol, ps[:, 0, 0:16], AF.Exp), r=["ps0"], w=["ecol"])
            p.op("act", lambda e: e.activation(cd, ps[:, 0, 16:32], AF.Exp), r=["ps0"], w=["cd"])
            p.op("dve", lambda e: e.tensor_tensor(dend, ps[:, 0, 16:32], csc, op=ALU.subtract), r=["ps0", "csc"], w=["dend"])
            p.op("act", lambda e: e.activation(dend, dend, AF.Exp), r=["dend"], w=["dend"])
            p.op("pool", lambda e, da_c=da_c, Ud=Ud: e.tensor_tensor(Ah, Ud.unsqueeze(1).to_broadcast([128, 16, 128]), da_c.unsqueeze(2).to_broadcast([128, 16, 128]), op=ALU.mult), r=["dat", "cst"], w=["Ah"])
            for q in range(4):
                p.op("pe", lambda e, q=q: e.matmul(ps[:, 2 + q, :], C.ones32, Ah[:, 4 * q:4 * q + 4, :].rearrange("p a b -> p (a b)"), start=True, stop=True), r=["Ah", "cst"], w=["ps%d" % (2 + q)])
                p.op("dve", lambda e, q=q: e.tensor_tensor(dec[:, 4 * q:4 * q + 4, :], ps[:, 2 + q, :].rearrange("p (a b) -> p a b", a=4), csc[:, 4 * q:4 * q + 4].unsqueeze(2).to_broadcast([128, 4, 128]), op=ALU.subtract), r=["ps%d" % (2 + q), "csc"], w=["dec"])
            dflat = dec.rearrange("p a b -> p (a b)")
            p.op("dve", lambda e, dflat=dflat: e.tensor_scalar_min(dflat, dflat, 0.0), r=["dec"], w=["dec"])
            p.op("act", lambda e, dflat=dflat: e.activation(dflat, dflat, AF.Exp), r=["dec"], w=["dec"])
            for g in range(2):
                p.op("pe", lambda e, g=g, cols=cols: e.matmul(ps[:, 1, g * 128:(g + 1) * 128], BCT[:, g, cols], BCT[:, 2 + g, cols], start=True, stop=True), r=["BCT"], w=["ps1"])
            p.op("dve", lambda e, maskd=maskd: e.tensor_tensor(CBm, ps[:, 1, 0:256].rearrange("p (a b) -> p a b", a=2), maskd.unsqueeze(1).to_broadcast([128, 2, 128]), op=ALU.mult), r=["ps1", "maskf", "maskb"], w=["CBm"])
            for g in range(2):
                p.op("pool", lambda e, g=g: e.tensor_tensor(MT[:, 8 * g:8 * g + 8, :], dec[:, 8 * g:8 * g + 8, :], CBm[:, g, :].unsqueeze(1).to_broadcast([128, 8, 128]), op=ALU.mult), r=["dec", "CBm"], w=["MT"])
            p.op("dve", lambda e, xs=xs, dt_c=dt_c: e.tensor_tensor(xdt, xs, dt_c.unsqueeze(2).to_broadcast([128, 16, 64]), op=ALU.mult), r=[kx, "dtt"], w=["xdt"])
            p.op("pool", lambda e: e.tensor_tensor(xdw, xdt, dend.unsqueeze(2).to_broadcast([128, 16, 64]), op=ALU.mult), r=["xdt", "dend"], w=["xdw"])
            xdf = xdt.rearrange("p a b -> p (a b)")
            for h in range(16):
                p.op("pe", lambda e, h=h, xdf=xdf: e.matmul(ps[:, 6 + h // 8, (h % 8) * 64:(h % 8 + 1) * 64], MT[:, h, :], xdf[:, h * 64:(h + 1) * 64], start=True, stop=True), r=["MT", "xdt"], w=["ps%d" % (6 + h // 8)])
            pbff = pbf.rearrange("p a b -> p (a b)")
            for g in range(2):
                p.op("pe", lambda e, g=g, cols=cols, pbff=pbff: e.matmul(ps[:, 2 + g, :], BCT[:, 2 + g, cols], pbff[:, g * 512:(g + 1) * 512], start=True, stop=True), r=["BCT", "pbf"], w=["ps%d" % (2 + g)])
            yo = ps[:, 2:4, :].rearrange("p a b -> p (a b)").rearrange("p (h d) -> p h d", d=64)
            yd = ps[:, 6:8, :].rearrange("p a b -> p (a b)").rearrange("p (h d) -> p h d", d=64)
            p.op("dve", lambda e, yo=yo: e.tensor_tensor(yc, yo, ecol.unsqueeze(2).to_broadcast([128, 16, 64]), op=ALU.mult), r=["ps2", "ps3", "ecol"], w=["yc"])
            p.op("dve", lambda e, yd=yd: e.tensor_tensor(yc, yd, yc, op=ALU.add), r=["ps6", "ps7", "yc"], w=["yc"])
            for g in range(2):
                psB = ps[:, 1, 256 + g * 64: 256 + (g + 1) * 64].bitcast(BF16)
                p.op("pe", lambda e, g=g, cols=cols, psB=psB: e.transpose(psB, BCT[:, g, cols], ident), r=["BCT", "ident"], w=["ps1"])
            p.op("act", lambda e: e.activation(Btok, ps[:, 1, 256:384].bitcast(BF16).rearrange("p (a b) -> p a b", a=2), AF.Copy), r=["ps1"], w=["Btok"])
            xwf = xdw.rearrange("p a b -> p (a b)")
            for g in range(2):
                p.op("pe", lambda e, g=g, xwf=xwf: e.matmul(ps[:, 4 + g, :], Btok[:, g, :], xwf[:, g * 512:(g + 1) * 512], start=True, stop=True), r=["Btok", "xdw"], w=["ps%d" % (4 + g)])
            st_ = ps[:, 4:6, :].rearrange("p a b -> p (a b)").rearrange("p (h d) -> p h d", d=64)
            p.op("pool", lambda e: e.tensor_tensor(prev, prev, cd.unsqueeze(2).to_broadcast([128, 16, 64]), op=ALU.mult), r=["prev", "cd"], w=["prev"])
            p.op("dve", lambda e, st_=st_: e.tensor_tensor(prev, prev, st_, op=ALU.add), r=["prev", "ps4", "ps5"], w=["prev"])
            p.op("act", lambda e: e.activation(pbf, prev, AF.Copy), r=["prev"], w=["pbf"])
            if d == 1 and c == 15:
                C.dump("csc", csc, ["csc"]); C.dump("ecol", ecol, ["ecol"]); C.dump("dend", dend, ["dend"]); C.dump("cd", cd, ["cd"])
                C.dump("dec", dec.rearrange("p a b -> p (a b)"), ["dec"]); C.dump("MT", MT.rearrange("p a b -> p (a b)"), ["MT"])
                C.dump("yc", yc.rearrange("p a b -> p (a b)"), ["yc"]); C.dump("prev", prev.rearrange("p a b -> p (a b)"), ["prev"])
                C.dump("xdt", xdt.rearrange("p a b -> p (a b)"), ["xdt"]); C.dump("CBm", CBm.rearrange("p a b -> p (a b)"), ["CBm"])
            if d == 1:
                p.op("act", lambda e, c=c: e.activation(ysum[:, c, :], yc.rearrange("p a b -> p (a b)"), AF.Copy), r=["yc"], w=["ysum"])
            else:
                _ol = int(_os.environ.get("SSD_OUT", "9"))
                if _ol >= 1:
                    p.op("pool", lambda e, c=c: e.tensor_tensor(yc, yc, ysum[:, c, :].rearrange("p (a b) -> p a b", a=16), op=ALU.add), r=["yc", "ysum"], w=["yc"])
                    p.op("pool", lambda e, xs=xs: e.tensor_tensor(y2, xs, C.dsk.unsqueeze(2).to_broadcast([128, 16, 64]), op=ALU.mult), r=[kx, C.k_dsk], w=["y2"])
                    p.op("pool", lambda e: e.tensor_tensor(yc, yc, y2, op=ALU.add), r=["yc", "y2"], w=["yc"])
                ycf = yc.rearrange("p a b -> p (a b)")
                if _ol >= 2:
                    p.op("dve", lambda e, c=c, ycf=ycf: e.tensor_tensor(ycf, ycf, zs[:, c, :], op=ALU.mult), r=["yc", "zs"], w=["yc"])
                if _ol >= 3:
                    p.op("act", lambda e, ycf=ycf: e.activation(junk, ycf, AF.Square, accum_out=ssq[:, 0:1]), r=["yc"], w=["ssq"])
                    p.op("dve", lambda e: e.tensor_scalar(rst[:, 0:1], ssq[:, 0:1], 1.0 / 1024, 4.0 * EPS, op0=ALU.mult, op1=ALU.add), r=["ssq"], w=["rst"])
                    p.op("pool", lambda e: e.tensor_tensor(rst[:, 0:1], rst[:, 0:1], C.negh[:, 0:1], op=ALU.pow), r=["rst", "negh"], w=["rst"])
                if _ol >= 4:
                    p.op("dve", lambda e, ycf=ycf, c=c: e.scalar_tensor_tensor(ysum[:, c, :], ycf, rst[:, 0:1], gssd, op0=ALU.mult, op1=ALU.mult), r=["yc", "rst", kgssd, "ysum"], w=["ysum"])
    C.full_barrier()
    for c in range(16):
        bank = c % 2
        psT = ps[:, bank, :].bitcast(BF16).rearrange("p (k t) -> p k t", k=8)
        for k in range(8):
            p.op("pe", lambda e, k=k, psT=psT, c=c: e.transpose(psT[:, k, :], ysum[:, c, k * 128:(k + 1) * 128], ident), r=["ysum", "ident"], w=["ps%d" % bank])
        p.op("act", lambda e, c=c, psT=psT: e.activation(mixedT[:, c, :, :], psT, AF.Copy), r=["ps%d" % bank], w=[kmix + "x"])
    C.full_barrier()
    A.release(m0)


def oproj_ffn2_phase(C, seq, mixedT, kmix):
    p, A, ps = C.p, C.A, C.ps
    tok0 = seq * S
    words_save = A.words
    h2 = A.alloc_top([16, 1024], F32)
    m = A.mark()
    Wout = A.alloc([16, 1024], BF16)
    p.op("sp", lambda e: e.dma_start(out=Wout, in_=C.wb["w_out"].rearrange("(k p) n -> p k n", p=128)), r=["wb_w_out"], w=["Wout"], dma="Wout")
    gmp, kgmp = C.bcast_load("mix_post_g", 1024, key="gmp")
    t1 = A.alloc([1024], F32)
    junk = A.alloc([1024], BF16)
    ss = A.alloc([8], F32)
    rs = A.alloc([8], F32)
    allmix = [kmix + "x"] + [kmix + "a%d" % t for t in range(16)]
    for t in range(16):
        r0 = tok0 + t * 128
        p.op("sp", lambda e, t=t, r0=r0: e.dma_start(out=h2[:, t, :], in_=C.h1d[r0:r0 + 128, :]), r=["h1d"], w=["h2_%d" % (t // 4)], dma="h2ld")
        b0 = 2 + 2 * (t % 3)
        for n in range(2):
            for kc in range(16):
                p.op("pe", lambda e, t=t, n=n, kc=kc, b0=b0: e.matmul(ps[:, b0 + n, :], (C.attnT[:, t, kc, :] if kc < 8 else C.ssdT[:, t, kc - 8, :]), Wout[:, kc, n * 512:(n + 1) * 512], start=(kc == 0), stop=(kc == 15)),
                     r=allmix + ["Wout"], w=["ps%d" % (b0 + n)])
        kps = ["ps%d" % b0, "ps%d" % (b0 + 1)]
        fps = ps[:, b0:b0 + 2, :].rearrange("p a b -> p (a b)")
        p.op("act", lambda e, fps=fps: e.activation(junk, fps, AF.Square, accum_out=ss[:, 0:1]), r=kps, w=["oss"])
        C.rms_rstd(ss[:, 0:1], rs[:, 0:1], 1, "oss", "ors", 1024.0)
        p.op("dve", lambda e, fps=fps: e.scalar_tensor_tensor(t1, fps, rs[:, 0:1], gmp, op0=ALU.mult, op1=ALU.mult), r=kps + ["ors", kgmp], w=["ot1"])
        p.op("pool", lambda e, t=t: e.tensor_tensor(h2[:, t, :], h2[:, t, :], t1, op=ALU.add), r=["ot1", "h2_%d" % (t // 4)], w=["h2_%d" % (t // 4)])
    C.full_barrier()
    A.release(C.seq_mark)
    if "f2" in C.stages:
        gfin, kgfin = C.bcast_load("final_g", 1024, key="gfin")
        ot1 = A.alloc([1024], F32)
        ot = [ot1, ot1]
        fss = A.alloc([8], F32)
        frs = A.alloc([8], F32)
        junk2 = A.alloc([1024], BF16)

        def get_src(g):
            return h2[:, 4 * g:4 * g + 4, :], "h2_%d" % g

        def epi(g, t, res, rkey):
            i2 = t % 2
            r0 = tok0 + g * G + t * 128
            p.op("act", lambda e: e.activation(junk2, res, AF.Square, accum_out=fss[:, 0:1]), r=[rkey], w=["fss"])
            C.rms_rstd(fss[:, 0:1], frs[:, 0:1], 1, "fss", "frs", 1024.0)
            p.op("dve", lambda e: e.scalar_tensor_tensor(ot[i2], res, frs[:, 0:1], gfin, op0=ALU.mult, op1=ALU.mult), r=[rkey, "frs", kgfin], w=["otf"])
            p.op("sp", lambda e: e.dma_start(out=C.out[r0:r0 + 128, :], in_=ot[i2]), r=["otf"], w=["outd"], dma="ost")

        C.ffn_phase("ffn2", seq, get_src, epi)
    A.words = words_save


_WNAMES = ["ffn1_w_gate", "ffn1_w_up", "ffn1_w_down", "ffn2_w_gate", "ffn2_w_up", "ffn2_w_down", "w_in", "w_out"]
_VNAMES = ["ffn1_pre_g", "ffn1_post_g", "mix_pre_g", "mix_post_g", "ffn2_pre_g", "ffn2_post_g", "final_g", "ssd_norm_g",
           "attn_subln_g", "lambda_q1", "lambda_k1", "lambda_q2", "lambda_k2", "a_log_fwd", "a_log_bwd",
           "dt_bias_fwd", "dt_bias_bwd", "d_skip"]


def make_in_map(inp, xs):
    m = {"x": np.ascontiguousarray(xs, dtype=np.float32), "consts": _const_pack()}
    for n in _WNAMES:
        m[n] = np.ascontiguousarray(np.asarray(inp[n])[0], dtype=np.float32)
    for n in _VNAMES:
        m[n] = np.ascontiguousarray(np.asarray(inp[n]).reshape(1, -1), dtype=np.float32)
    cwt = np.asarray(inp["conv_w"])[0].T.reshape(12, 128, 5).transpose(1, 0, 2).reshape(128, 60)
    m["conv_w"] = np.ascontiguousarray(cwt, dtype=np.float32)
    m["conv_b"] = np.ascontiguousarray(np.asarray(inp["conv_b"]).reshape(12, 128).T, dtype=np.float32)
    return m


_CACHE = {}


def kernel(**inputs):
    x = np.asarray(inputs["x"], dtype=np.float32)
    B = x.shape[0]
    per = B // NCORES
    if "nc" not in _CACHE:
        _CACHE["nc"] = build_program(per)
    nc = _CACHE["nc"]
    in_maps = [make_in_map(inputs, x[i * per:(i + 1) * per].reshape(per * S, D)) for i in range(NCORES)]
    res = run_bass_kernel_spmd(nc, in_maps, core_ids=list(range(NCORES)))
    outs = [np.asarray(r["out"]).reshape(per, S, D) for r in res.results]
    return np.concatenate(outs, axis=0).astype(np.float32)
```

```python
import numpy as np
import concourse.bass as bass
import concourse.mybir as mybir
from concourse.bass_utils import run_bass_kernel_spmd

F32 = mybir.dt.float32
BF16 = mybir.dt.bfloat16
AF = mybir.ActivationFunctionType
ALU = mybir.AluOpType
AX = mybir.AxisListType

D = 1024
S = 2048
DFF = 2816
NJ = DFF // 128
DIN = 5664
EPS = 1e-6
NCORES = 8
G = 512
NG = S // G

SAME_ENGINE_SYNC = True


class _Op:
    __slots__ = ("eng", "fn", "deps", "dma", "dmacnt", "sig", "seq", "dmadeps", "strict")


class Prog:
    def __init__(self, nc):
        self.nc = nc
        self.ops = []
        self.last_w = {}
        self.readers = {}
        self.dma_cnt = {}

    def op(self, eng, fn, r=(), w=(), dma=None, strict=False):
        idx = len(self.ops)
        o = _Op()
        o.strict = strict
        o.eng = eng
        o.fn = fn
        o.dma = dma
        deps = set()
        for k in r:
            d = self.last_w.get(k)
            if d is not None:
                deps.add(d)
        for k in w:
            d = self.last_w.get(k)
            if d is not None:
                deps.add(d)
            for x in self.readers.get(k, ()):
                deps.add(x)
        deps.discard(idx)
        o.deps = []
        o.dmadeps = []
        for d in deps:
            od = self.ops[d]
            if od.dma is not None:
                o.dmadeps.append((od.dma, od.dmacnt))
            else:
                o.deps.append(d)
        if dma is not None:
            self.dma_cnt[dma] = self.dma_cnt.get(dma, 0) + 16
            o.dmacnt = self.dma_cnt[dma]
        else:
            o.dmacnt = 0
        for k in w:
            self.last_w[k] = idx
            self.readers[k] = []
        for k in r:
            lst = self.readers.setdefault(k, [])
            if dma is None:
                lst[:] = [x for x in lst if not (self.ops[x].eng == eng and self.ops[x].dma is None)]
            lst.append(idx)
        o.sig = False
        o.seq = 0
        self.ops.append(o)
        return idx

    def emit(self, stack):
        nc = self.nc
        ops = self.ops
        engs = {"pe": nc.tensor, "act": nc.scalar, "dve": nc.vector, "pool": nc.gpsimd, "sp": nc.sync}
        for o in ops:
            for d in o.deps:
                od = ops[d]
                if od.eng == o.eng and not o.strict:
                    if od.eng == "pe" or od.eng == "sp" or not SAME_ENGINE_SYNC:
                        continue
                od.sig = True
        cnt = {e: 0 for e in engs}
        for o in ops:
            if o.dma is None and o.sig:
                cnt[o.eng] += 1
                o.seq = cnt[o.eng]
        print("SEMCNT", cnt, "nops", len(ops), "dma", {k: v // 16 for k, v in self.dma_cnt.items() if v > 16 * 100})
        esem = {e: stack.enter_context(nc.semaphore("s_" + e)) for e in engs}
        dsem = {k: stack.enter_context(nc.semaphore("d_" + str(k))) for k in self.dma_cnt}
        waited = {e: {} for e in engs}
        for o in ops:
            E = engs[o.eng]
            wt = waited[o.eng]
            need = {}
            for d in o.deps:
                od = ops[d]
                if not od.sig:
                    continue
                if od.eng == o.eng and not o.strict and (od.eng in ("pe", "sp") or not SAME_ENGINE_SYNC):
                    continue
                if od.seq > need.get(od.eng, 0):
                    need[od.eng] = od.seq
            for e2, v in need.items():
                if wt.get(e2, 0) < v:
                    E.wait_ge(esem[e2], v)
                    wt[e2] = v
            for (k, c) in o.dmadeps:
                kk = ("dma", k)
                if wt.get(kk, 0) < c:
                    E.wait_ge(dsem[k], c)
                    wt[kk] = c
            ins = o.fn(E)
            if o.dma is not None:
                ins.then_inc(dsem[o.dma], 16)
            elif o.sig:
                ins.then_inc(esem[o.eng], 1)
        for k, c in self.dma_cnt.items():
            nc.sync.wait_ge(dsem[k], c)
        for e in engs:
            if e != "sp" and cnt[e] > 0:
                nc.sync.wait_ge(esem[e], cnt[e])


class Arena:
    def __init__(self, t32, words):
        self.t = t32
        self.words = words
        self.top = 0

    def alloc(self, shape, dtype):
        n = 1
        for s in shape:
            n *= s
        nbytes = n * (4 if dtype == F32 else 2)
        w = (nbytes + 3) // 4
        w = (w + 7) // 8 * 8
        off = self.top
        self.top += w
        assert self.top <= self.words, ("arena overflow", self.top, self.words)
        ap = self.t[:, off:off + w]
        if dtype != F32:
            ap = ap.bitcast(dtype)[:, 0:n]
        else:
            ap = ap[:, 0:n]
        if len(shape) == 2:
            return ap.rearrange("p (a b) -> p a b", a=shape[0])
        if len(shape) == 3:
            return ap.rearrange("p (a b c) -> p a b c", a=shape[0], b=shape[1])
        return ap

    def alloc_top(self, shape, dtype):
        n = 1
        for s in shape:
            n *= s
        nbytes = n * (4 if dtype == F32 else 2)
        w = (nbytes + 3) // 4
        w = (w + 7) // 8 * 8
        self.words -= w
        assert self.top <= self.words, ("arena overflow(top)", self.top, self.words)
        off = self.words
        ap = self.t[:, off:off + w]
        ap = ap.bitcast(dtype)[:, 0:n] if dtype != F32 else ap[:, 0:n]
        if len(shape) == 2:
            return ap.rearrange("p (a b) -> p a b", a=shape[0])
        if len(shape) == 3:
            return ap.rearrange("p (a b c) -> p a b c", a=shape[0], b=shape[1])
        return ap

    def mark(self):
        return self.top

    def release(self, m):
        self.top = m


NCONST = 128 * 4 + 2 * 16 * 8


def _const_pack():
    c = np.zeros((128, NCONST), np.float32)
    i = np.arange(128)
    c[:, 0:128] = np.eye(128, dtype=np.float32)
    c[:, 128:256] = (i[:, None] <= i[None, :]).astype(np.float32)
    c[:, 256:384] = (i[:, None] >= i[None, :]).astype(np.float32)
    c[:, 384:512] = 1.0
    pos = np.arange(S, dtype=np.float32)
    inv = np.power(np.float32(500000.0), -np.arange(0, 16, 2, dtype=np.float32) / np.float32(16)).astype(np.float32)
    ang = (pos[:, None] * inv[None, :]).astype(np.float32)
    cs = np.cos(ang).astype(np.float32).reshape(16, 128, 8).transpose(1, 0, 2).reshape(128, 128)
    sn = np.sin(ang).astype(np.float32).reshape(16, 128, 8).transpose(1, 0, 2).reshape(128, 128)
    c[:, 512:640] = cs
    c[:, 640:768] = sn
    return c


class Ctx:
    pass


def build_program(nseq, dbg=None, stages=("f1", "attn", "ssd", "o", "f2")):
    from contextlib import ExitStack
    nc = bass.Bass("TRN2", target_bir_lowering=False)
    T = nseq * S
    dr = lambda n, sh, dt=F32, kind="ExternalInput": nc.dram_tensor(n, sh, dt, kind=kind).ap()
    x = dr("x", [T, D])
    out = dr("out", [T, D], kind="ExternalOutput")
    consts = dr("consts", [128, NCONST])
    wnames = {"ffn1_w_gate": [D, DFF], "ffn1_w_up": [D, DFF], "ffn1_w_down": [DFF, D],
              "ffn2_w_gate": [D, DFF], "ffn2_w_up": [D, DFF], "ffn2_w_down": [DFF, D],
              "w_in": [D, DIN], "w_out": [2048, D]}
    wf = {n: dr(n, sh) for n, sh in wnames.items()}
    wb = {n: dr(n + "_bf", sh, BF16, kind="Internal") for n, sh in wnames.items()}
    vnames = {"ffn1_pre_g": 1024, "ffn1_post_g": 1024, "mix_pre_g": 1024, "mix_post_g": 1024,
              "ffn2_pre_g": 1024, "ffn2_post_g": 1024, "final_g": 1024, "ssd_norm_g": 1024,
              "attn_subln_g": 128, "lambda_q1": 64, "lambda_k1": 64, "lambda_q2": 64, "lambda_k2": 64,
              "a_log_fwd": 16, "a_log_bwd": 16, "dt_bias_fwd": 16, "dt_bias_bwd": 16,
              "d_skip": 16}
    vf = {n: dr(n, [1, k]) for n, k in vnames.items()}
    conv_w = dr("conv_w", [128, 60])
    conv_b = dr("conv_b", [128, 12])
    h1d = dr("h1_spill", [T, D], F32, kind="Internal")
    dbg_out = {}
    if dbg:
        for n, sh in dbg.items():
            if isinstance(sh, tuple):
                dbg_out[n] = dr("dbg_" + n, sh[0], sh[1], kind="ExternalOutput")
            else:
                dbg_out[n] = dr("dbg_" + n, sh, kind="ExternalOutput")

    with ExitStack() as st:
        AW = 53000
        arena_t = st.enter_context(nc.sbuf_tensor("arena", [128, AW], F32))
        ps = st.enter_context(nc.psum_tensor("ps", [128, 8, 512], F32))
        A = Arena(arena_t, AW)
        p = Prog(nc)
        C = Ctx()
        C.nc, C.p, C.A, C.ps = nc, p, A, ps
        uid = [0]

        def U(s):
            uid[0] += 1
            return "%s#%d" % (s, uid[0])

        live_regions = set()
        _orig_op = p.op

        def op(eng, fn, r=(), w=(), dma=None):
            for k in w:
                live_regions.add(k)
            for k in r:
                live_regions.add(k)
            return _orig_op(eng, fn, r=r, w=w, dma=dma)
        p.op = op

        def full_barrier():
            regs = list(live_regions)
            for e in ("pe", "act", "dve", "pool", "sp"):
                _orig_op(e, (lambda E: E.nop()), r=(), w=regs, strict=True)

        cst = A.alloc([NCONST], F32)
        p.op("sp", lambda e: e.dma_start(out=cst, in_=consts), w=["cst"], dma="cst")
        ident = A.alloc([128], BF16)
        p.op("dve", lambda e: e.tensor_copy(ident, cst[:, 0:128]), r=["cst"], w=["ident"])
        Uf32 = cst[:, 128:256]
        Ub32 = cst[:, 256:384]
        ones32 = cst[:, 384:512]
        maskf = A.alloc([128], BF16)
        maskb = A.alloc([128], BF16)
        p.op("dve", lambda e: e.tensor_copy(maskf, Uf32), r=["cst"], w=["maskf"])
        p.op("dve", lambda e: e.tensor_copy(maskb, Ub32), r=["cst"], w=["maskb"])
        cosT = cst[:, 512:640].rearrange("p (t f) -> p t f", t=16)
        sinT = cst[:, 640:768].rearrange("p (t f) -> p t f", t=16)
        negh = A.alloc([16], F32)
        p.op("pool", lambda e: e.memset(negh, -0.5), w=["negh"])

        def bcast_load(name, n, key=None):
            t = A.alloc([n], F32)
            k = key or ("v_" + name)
            p.op("sp", lambda e: e.dma_start(out=t, in_=vf[name].partition_broadcast(128)), w=[k], dma=k)
            return t, k

        gsub, k_gsub = bcast_load("attn_subln_g", 128)
        p.op("dve", lambda e: e.tensor_scalar(gsub, gsub, 1.0 - (0.8 - 0.6), None, op0=ALU.mult), r=[k_gsub], w=[k_gsub])
        lq1, k1 = bcast_load("lambda_q1", 64)
        lk1, k2 = bcast_load("lambda_k1", 64)
        lq2, k3 = bcast_load("lambda_q2", 64)
        lk2, k4 = bcast_load("lambda_k2", 64)
        lamt = A.alloc([8], F32)
        ljunk = A.alloc([64], F32)
        p.op("dve", lambda e: e.tensor_tensor(ljunk, lq1, lk1, op=ALU.mult), r=[k1, k2], w=["ljunk"])
        p.op("dve", lambda e: e.reduce_sum(lamt[:, 0:1], ljunk, axis=AX.X), r=["ljunk"], w=["lamt"])
        p.op("dve", lambda e: e.tensor_tensor(ljunk, lq2, lk2, op=ALU.mult), r=[k3, k4, "lamt"], w=["ljunk"])
        p.op("dve", lambda e: e.reduce_sum(lamt[:, 1:2], ljunk, axis=AX.X), r=["ljunk"], w=["lamt"])
        p.op("act", lambda e: e.activation(lamt[:, 0:2], lamt[:, 0:2], AF.Exp), r=["lamt"], w=["lamt"])
        p.op("dve", lambda e: e.tensor_tensor(lamt[:, 2:3], lamt[:, 0:1], lamt[:, 1:2], op=ALU.subtract), r=["lamt"], w=["lamt"])
        p.op("dve", lambda e: e.tensor_scalar(lamt[:, 2:3], lamt[:, 2:3], 0.8 - 0.6, None, op0=ALU.add), r=["lamt"], w=["lamt"])
        p.op("dve", lambda e: e.tensor_scalar(lamt[:, 3:4], lamt[:, 2:3], -1.0, None, op0=ALU.mult), r=["lamt"], w=["lamt"])
        alog = A.alloc([32], F32)
        p.op("sp", lambda e: e.dma_start(out=alog[:, 0:16], in_=vf["a_log_fwd"].partition_broadcast(128)), w=["alog"], dma="alog")
        p.op("sp", lambda e: e.dma_start(out=alog[:, 16:32], in_=vf["a_log_bwd"].partition_broadcast(128)), w=["alog"], dma="alog")
        aneg = A.alloc([32], F32)
        p.op("act", lambda e: e.activation(aneg, alog, AF.Exp), r=["alog"], w=["aneg"])
        p.op("dve", lambda e: e.tensor_scalar(aneg, aneg, -1.0, None, op0=ALU.mult), r=["aneg"], w=["aneg"])
        dtb = A.alloc([32], F32)
        p.op("sp", lambda e: e.dma_start(out=dtb[:, 0:16], in_=vf["dt_bias_fwd"].partition_broadcast(128)), w=["dtb"], dma="dtb")
        p.op("sp", lambda e: e.dma_start(out=dtb[:, 16:32], in_=vf["dt_bias_bwd"].partition_broadcast(128)), w=["dtb"], dma="dtb")
        dsk, k_dsk = bcast_load("d_skip", 16)
        cw = A.alloc([12, 5], F32)
        cb = A.alloc([12], F32)
        p.op("sp", lambda e: e.dma_start(out=cw.rearrange("p c k -> p (c k)"), in_=conv_w), w=["cw"], dma="cw")
        p.op("sp", lambda e: e.dma_start(out=cb, in_=conv_b), w=["cb"], dma="cb")
        cbh = A.alloc([12], F32)
        p.op("dve", lambda e: e.tensor_scalar(cbh, cb, 0.5, None, op0=ALU.mult), r=["cb"], w=["cbh"])

        def cast_weight(n):
            rows = wnames[n][0]
            nsplit = 4
            rs = rows // nsplit
            for i in range(nsplit):
                p.op("pool", lambda e, n=n, i=i, rs=rs: e.dma_start(out=wb[n][i * rs:(i + 1) * rs, :], in_=wf[n][i * rs:(i + 1) * rs, :]),
                     w=["wb_" + n], dma="wb_" + n)
        C.deferred_casts = []
        for n in wnames:
            if n.startswith("ffn1"):
                cast_weight(n)
            else:
                C.deferred_casts.append(n)
        C.cast_weight = cast_weight

        C.ident, C.negh = ident, negh
        junk1 = A.alloc([1024], BF16)
        base_mark = A.mark()

        def rms_rstd(ssq_ap, rstd_ap, n, kr, kw, width):
            p.op("dve", lambda e: e.tensor_scalar(rstd_ap, ssq_ap, 1.0 / width, EPS, op0=ALU.mult, op1=ALU.add), r=[kr], w=[kw])
            p.op("pool", lambda e: e.tensor_tensor(rstd_ap, rstd_ap, negh[:, 0:n], op=ALU.pow), r=[kw, "negh"], w=[kw])

        tr_ctr = [0]

        def norm_transpose(src, src_key, g_b, g_key, dstT, dst_key, dst_cols, bufs):
            i = tr_ctr[0] % 2
            tr_ctr[0] += 1
            junk, ssq, rstd, xn = bufs["junk"][i], bufs["ssq"][i], bufs["rstd"][i], bufs["xn"][i]
            kj, ks, kr, kx = "ntj%d" % i, "nts%d" % i, "ntr%d" % i, "ntx%d" % i
            bank = 0 + i
            kb = "ps%d" % bank
            p.op("act", lambda e: e.activation(junk, src, AF.Square, accum_out=ssq[:, 0:1]), r=[src_key], w=[ks])
            rms_rstd(ssq[:, 0:1], rstd[:, 0:1], 1, ks, kr, 1024.0)
            p.op("dve", lambda e: e.scalar_tensor_tensor(xn, src, rstd[:, 0:1], g_b, op0=ALU.mult, op1=ALU.mult), r=[src_key, kr, g_key], w=[kx])
            psT = ps[:, bank, :].bitcast(BF16).rearrange("p (c t) -> p c t", c=8)
            for c in range(8):
                p.op("pe", lambda e, c=c: e.transpose(psT[:, c, :], xn[:, c * 128:(c + 1) * 128], ident), r=[kx, "ident"], w=[kb])
            p.op("act", lambda e: e.activation(dstT[:, :, dst_cols], psT, AF.Copy), r=[kb], w=[dst_key])

        def alloc_norm_bufs():
            return {"junk": [junk1, junk1],
                    "ssq": [A.alloc([8], F32) for _ in range(2)],
                    "rstd": [A.alloc([8], F32) for _ in range(2)],
                    "xn": [A.alloc([1024], BF16) for _ in range(2)]}

        def ffn_phase(which, seq, get_src, epilogue):
            m = A.mark()
            wg, wu, wd = wb[which + "_w_gate"], wb[which + "_w_up"], wb[which + "_w_down"]
            gpre, kpre = bcast_load(which + "_pre_g", 1024, key="gpre")
            gpost, kpost = bcast_load(which + "_post_g", 1024, key="gpost")
            p.op("dve", lambda e: e.tensor_scalar(gpost, gpost, 0.5, None, op0=ALU.mult), r=[kpost], w=[kpost])
            nb = alloc_norm_bufs()
            hnT = [A.alloc([8, G], BF16) for _ in range(2)]
            actT = A.alloc([NJ, G], BF16)
            Wd = A.alloc([NJ, 1024], BF16)
            Wgu = [A.alloc([2, 8, 256], BF16) for _ in range(2)]
            th = [A.alloc([512], F32) for _ in range(3)]
            aa = [A.alloc([512], F32) for _ in range(2)]
            t1x = A.alloc([1024], F32)
            t1 = [t1x, t1x]
            fj = [junk1, junk1]
            fs = [A.alloc([8], F32) for _ in range(2)]
            fr = [A.alloc([8], F32) for _ in range(2)]
            NJB = 11
            wctr = 0
            srcs = {}
            pend_epi = []
            srcs[0] = get_src(0)
            for t in range(4):
                norm_transpose(srcs[0][0][:, t, :], srcs[0][1], gpre, kpre, hnT[0], "hnT0", slice(t * 128, (t + 1) * 128), nb)
            for g in range(NG):
                src, skey = srcs[g]
                hT = hnT[g % 2]
                khT = "hnT%d" % (g % 2)
                for jb in range(NJB):
                    ncols = 256
                    slot = wctr % 2
                    wctr += 1
                    W = Wgu[slot]
                    kW = "Wgu%d" % slot
                    for wi, wsrc in enumerate((wg, wu)):
                        p.op("sp", lambda e, wi=wi, wsrc=wsrc, jb=jb, ncols=ncols, W=W: e.dma_start(
                            out=W[:, wi, :, 0:ncols], in_=wsrc.rearrange("(c p) n -> p c n", p=128)[:, :, jb * 256: jb * 256 + ncols]),
                            r=["wb_" + which + ("_w_gate" if wi == 0 else "_w_up")], w=[kW], dma=kW)
                    p.op("sp", lambda e, jb=jb: e.dma_start(out=Wd[:, 2 * jb:2 * jb + 2, :], in_=wd.rearrange("(j p) n -> p j n", p=128)[:, 2 * jb:2 * jb + 2, :]),
                         r=["wb_" + which + "_w_down"], w=["Wd"], dma="Wd")
                    if jb < 4 and pend_epi:
                        epilogue(*pend_epi.pop(0))
                    if g + 1 < NG and jb == 4:
                        srcs[g + 1] = get_src(g + 1)
                    if g + 1 < NG and jb in (5, 6, 7, 8):
                        tn = jb - 5
                        nsrc, nkey = srcs[g + 1]
                        norm_transpose(nsrc[:, tn, :], nkey, gpre, kpre, hnT[(g + 1) % 2], "hnT%d" % ((g + 1) % 2), slice(tn * 128, (tn + 1) * 128), nb)
                    for jj in range(ncols // 128):
                        j = jb * 2 + jj
                        pb = 2 + 2 * (j % 3)
                        for wi in range(2):
                            for c in range(8):
                                p.op("pe", lambda e, wi=wi, c=c, jj=jj, pb=pb, W=W, hT=hT: e.matmul(ps[:, pb + wi, :], W[:, wi, c, jj * 128:(jj + 1) * 128], hT[:, c, :], start=(c == 0), stop=(c == 7)),
                                     r=[kW, khT], w=["ps%d" % (pb + wi)])
                        i2 = j % 3
                        p.op("act", lambda e, pb=pb, i2=i2: e.activation(th[i2], ps[:, pb, :], AF.Tanh, scale=0.5), r=["ps%d" % pb], w=["th%d" % i2])
                        i3 = j % 2
                        p.op("dve", lambda e, pb=pb, i2=i2, i3=i3: e.scalar_tensor_tensor(aa[i3], th[i2], 1.0, ps[:, pb, :], op0=ALU.add, op1=ALU.mult), r=["th%d" % i2, "ps%d" % pb], w=["aa%d" % i3])
                        p.op("dve", lambda e, pb=pb, i3=i3, j=j: e.scalar_tensor_tensor(actT[:, j, :], aa[i3], 0.5, ps[:, pb + 1, :], op0=ALU.mult, op1=ALU.mult), r=["aa%d" % i3, "ps%d" % (pb + 1)], w=["actT"])
                if C.deferred_casts and g == 0:
                    for n_ in C.deferred_casts:
                        C.cast_weight(n_)
                    C.deferred_casts = []
                for t in range(4):
                    pb = 2 + 2 * (t % 3)
                    i2 = t % 2
                    for n in range(2):
                        for j in range(NJ):
                            p.op("pe", lambda e, n=n, j=j, t=t, pb=pb: e.matmul(ps[:, pb + n, :], actT[:, j, t * 128:(t + 1) * 128], Wd[:, j, n * 512:(n + 1) * 512], start=(j == 0), stop=(j == NJ - 1)),
                                 r=["actT", "Wd"], w=["ps%d" % (pb + n)])
                    fps = ps[:, pb:pb + 2, :].rearrange("p a b -> p (a b)")
                    kps = ["ps%d" % pb, "ps%d" % (pb + 1)]
                    p.op("act", lambda e, fps=fps, i2=i2: e.activation(fj[i2], fps, AF.Square, accum_out=fs[i2][:, 0:1]), r=kps, w=["fs%d" % i2])
                    rms_rstd(fs[i2][:, 0:1], fr[i2][:, 0:1], 1, "fs%d" % i2, "fr%d" % i2, 1024.0)
                    p.op("dve", lambda e, fps=fps, i2=i2: e.scalar_tensor_tensor(t1[i2], fps, fr[i2][:, 0:1], gpost, op0=ALU.mult, op1=ALU.mult), r=kps + ["fr%d" % i2, kpost], w=["t1"])
                    p.op("pool", lambda e, i2=i2, t=t, src=src: e.tensor_tensor(src[:, t, :], src[:, t, :], t1[i2], op=ALU.add), r=["t1", skey], w=[skey])
                    pend_epi.append((g, t, src[:, t, :], skey))
            while pend_epi:
                epilogue(*pend_epi.pop(0))
            full_barrier()
            A.release(m)

        def dump(name, ap, keys):
            if name in dbg_out:
                p.op("sp", lambda e: e.dma_start(out=dbg_out[name], in_=ap), r=list(keys), w=["dbgo_" + name], dma="dbg")
        C.dump = dump
        C.ffn_phase = ffn_phase
        C.norm_transpose = norm_transpose
        C.alloc_norm_bufs = alloc_norm_bufs
        C.bcast_load = bcast_load
        C.full_barrier = full_barrier
        C.rms_rstd = rms_rstd
        C.U = U
        C.x, C.out, C.h1d, C.wb, C.vf, C.dbg_out = x, out, h1d, wb, vf, dbg_out
        C.cst, C.cosT, C.sinT, C.maskf, C.maskb, C.Uf32, C.Ub32, C.ones32 = cst, cosT, sinT, maskf, maskb, Uf32, Ub32, ones32
        C.gsub, C.k_gsub, C.lamt, C.aneg, C.dtb, C.dsk, C.k_dsk, C.cw, C.cb, C.cbh = gsub, k_gsub, lamt, aneg, dtb, dsk, k_dsk, cw, cb, cbh
        C.stages = stages

        for seq in range(nseq):
            run_sequence(C, seq)

        p.emit(st)
    return nc


def run_sequence(C, seq):
    p, A, ps = C.p, C.A, C.ps
    x, out, h1d, wb = C.x, C.out, C.h1d, C.wb
    U = C.U
    stages = C.stages
    tok0 = seq * S
    words_save = A.words
    seq_mark = A.mark()
    hnTm = A.alloc_top([8, S], BF16)
    kmix = "mixedT"
    khm = "hnTm"

    if "f1" in stages:
        m = A.mark()
        xt = [A.alloc([4, 1024], F32) for _ in range(2)]
        gmix, kgmix = C.bcast_load("mix_pre_g", 1024, key="gmix")
        nb2 = C.alloc_norm_bufs()

        def get_src(g):
            slot = g % 2
            k = "xt%d" % slot
            p.op("sp", lambda e: e.dma_start(out=xt[slot], in_=x[tok0 + g * G: tok0 + (g + 1) * G, :].rearrange("(t p) d -> p t d", p=128)), w=[k], dma=k)
            return xt[slot], k

        def epi(g, t, res, rkey):
            r0 = tok0 + g * G + t * 128
            p.op("sp", lambda e: e.dma_start(out=h1d[r0:r0 + 128, :], in_=res), r=[rkey], w=["h1d"], dma="h1st%d" % (g % 2))
            C.norm_transpose(res, rkey, gmix, kgmix, hnTm, khm, slice(g * G + t * 128, g * G + (t + 1) * 128), nb2)

        C.ffn_phase("ffn1", seq, get_src, epi)
        A.release(m)
    C.seq_mark = seq_mark
    C.words_save = words_save
    attnT = A.alloc([16, 8, 128], BF16)
    mixedT = attnT
    if "hn" in C.dbg_out and seq == 0:
        m = A.mark()
        tmp = A.alloc([8 * S // 4], F32)
        for q in range(4):
            p.op("dve", lambda e, q=q: e.tensor_copy(tmp, hnTm.rearrange("p c t -> p (c t)")[:, q * 4096:(q + 1) * 4096]), r=[khm], w=["dbgtmp"])
            p.op("sp", lambda e, q=q: e.dma_start(out=C.dbg_out["hn"][:, q * 4096:(q + 1) * 4096], in_=tmp), r=["dbgtmp"], w=["dbgo"], dma="dbg")
        C.full_barrier()
        A.release(m)

    if "attn" in stages:
        attention_phase(C, seq, mixedT, kmix, hnTm, khm)
    ssdT = A.alloc([16, 8, 128], BF16)
    C.attnT, C.ssdT = attnT, ssdT
    if "ssd" in stages:
        ssd_phase(C, seq, ssdT, kmix, hnTm, khm)
    for nm, tt in (("attnT", attnT), ("ssdT", ssdT)):
        if nm in C.dbg_out and seq == 0:
            m = A.mark()
            tmp = A.alloc([4096], F32)
            mf = tt.rearrange("p a b c -> p (a b c)")
            for q in range(4):
                p.op("dve", lambda e, q=q, mf=mf: e.tensor_copy(tmp, mf[:, q * 4096:(q + 1) * 4096]), r=[kmix + "x"] + [kmix + "a%d" % t for t in range(16)], w=["dbgtmp"])
                p.op("sp", lambda e, q=q, nm=nm: e.dma_start(out=C.dbg_out[nm][:, q * 4096:(q + 1) * 4096], in_=tmp), r=["dbgtmp"], w=["dbgo"], dma="dbg")
            C.full_barrier()
            A.release(m)
    A.words = words_save
    if "o" in stages:
        oproj_ffn2_phase(C, seq, mixedT, kmix)
    C.full_barrier()
    A.release(seq_mark)


def attention_phase(C, seq, mixedT, kmix, hnTm, khm):
    p, A, ps = C.p, C.A, C.ps
    ident = C.ident
    win = C.wb["w_in"].rearrange("(c p) n -> p c n", p=128)
    m = A.mark()
    QT = A.alloc([8, S], BF16)
    KT = A.alloc([8, S], BF16)
    Vp = A.alloc([16, 8, 130], BF16)
    m_in = A.mark()
    Wqkv = A.alloc([8, 1024], BF16)
    p.op("pool", lambda e: e.memset(Vp.rearrange("p a b c -> p (a b c)"), 1.0), w=["Vp"])
    qtok = [A.alloc([16, 64], BF16) for _ in range(2)]
    rta = [A.alloc([16, 8], F32) for _ in range(4)]
    rtb = [A.alloc([16, 8], F32) for _ in range(4)]
    for i in range(3):
        p.op("sp", lambda e, i=i: e.dma_start(out=Wqkv, in_=win[:, :, i * 1024:(i + 1) * 1024]), r=["wb_w_in"], w=["Wqkv"], dma="Wqkv")
        for t in range(16):
            cs_ = C.cosT[:, t, :].unsqueeze(1).to_broadcast([128, 16, 8])
            sn_ = C.sinT[:, t, :].unsqueeze(1).to_broadcast([128, 16, 8])
            b0 = 2 + 2 * (t % 3)
            for n in range(2):
                for c in range(8):
                    p.op("pe", lambda e, i=i, n=n, c=c, b0=b0, t=t: e.matmul(ps[:, b0 + n, :], hnTm[:, c, t * 128:(t + 1) * 128], Wqkv[:, c, n * 512:(n + 1) * 512], start=(c == 0), stop=(c == 7)),
                         r=[khm, "Wqkv"], w=["ps%d" % (b0 + n)])
            kps = ["ps%d" % b0, "ps%d" % (b0 + 1)]
            pv = ps[:, b0:b0 + 2, :].rearrange("p a b -> p (a b)")
            if i == 2:
                p.op("act", lambda e, pv=pv, t=t: e.activation(Vp[:, t, :, 0:128], pv.rearrange("p (h d) -> p h d", d=128), AF.Copy), r=kps, w=["Vp"])
                continue
            qv = pv.rearrange("p (h d) -> p h d", d=64)
            qt_ = qtok[i]
            kq = "qtok%d" % i
            ra, rb = rta[2 * i], rta[2 * i + 1]
            rc, rd = rtb[2 * i], rtb[2 * i + 1]
            p.op("dve", lambda e, qv=qv, ra=ra, cs_=cs_: e.tensor_tensor(ra, qv[:, :, 0:8], cs_, op=ALU.mult), r=kps + ["cst"], w=["ra%d" % i])
            p.op("dve", lambda e, qv=qv, rb=rb, sn_=sn_: e.tensor_tensor(rb, qv[:, :, 8:16], sn_, op=ALU.mult), r=kps + ["cst"], w=["rb%d" % i])
            p.op("dve", lambda e, qv=qv, rc=rc, cs_=cs_: e.tensor_tensor(rc, qv[:, :, 8:16], cs_, op=ALU.mult), r=kps + ["cst"], w=["rc%d" % i])
            p.op("dve", lambda e, qv=qv, rd=rd, sn_=sn_: e.tensor_tensor(rd, qv[:, :, 0:8], sn_, op=ALU.mult), r=kps + ["cst"], w=["rd%d" % i])
            p.op("pool", lambda e, qt_=qt_, ra=ra, rb=rb: e.tensor_tensor(qt_[:, :, 0:8], ra, rb, op=ALU.subtract), r=["ra%d" % i, "rb%d" % i], w=[kq])
            p.op("pool", lambda e, qt_=qt_, rc=rc, rd=rd: e.tensor_tensor(qt_[:, :, 8:16], rc, rd, op=ALU.add), r=["rc%d" % i, "rd%d" % i], w=[kq])
            p.op("act", lambda e, qt_=qt_, qv=qv: e.activation(qt_[:, :, 16:64], qv[:, :, 16:64], AF.Copy), r=kps, w=[kq])
            tb = t % 2
            psT = ps[:, tb, :].bitcast(BF16).rearrange("p (c t) -> p c t", c=8)
            qf = qt_.rearrange("p a b -> p (a b)")
            for h in range(8):
                p.op("pe", lambda e, h=h, psT=psT, qf=qf: e.transpose(psT[:, h, :], qf[:, h * 128:(h + 1) * 128], ident), r=[kq, "ident"], w=["ps%d" % tb])
            dst = QT if i == 0 else KT
            p.op("act", lambda e, dst=dst, psT=psT, t=t: e.activation(dst[:, :, t * 128:(t + 1) * 128], psT, AF.Copy), r=["ps%d" % tb], w=["QT" if i == 0 else "KT"])
    C.full_barrier()
    A.release(m_in)
    Qpad = [A.alloc([2, S], BF16) for _ in range(2)]
    for i_ in range(2):
        p.op("pool", lambda e, i_=i_: e.memset(Qpad[i_].rearrange("p a b -> p (a b)"), 0.0), w=["Qpad%d" % i_])
    NSB = 3
    SB = (0, 1, 6)
    PF = 2
    E = [A.alloc([512], BF16) for _ in range(NSB)]
    rr = [A.alloc([8], F32) for _ in range(2)]
    o1 = [A.alloc([128], F32) for _ in range(2)]
    o2 = [A.alloc([128], F32) for _ in range(2)]
    ss = [A.alloc([8], F32) for _ in range(2)]
    rs = [A.alloc([8], F32) for _ in range(2)]
    at = [A.alloc([128], BF16) for _ in range(2)]
    junk = A.alloc([128], BF16)
    import os as _os
    _nh = int(_os.environ.get("ATTN_HEADS_DBG", "8"))
    steps = [(h, qg, kt) for h in range(_nh) for qg in range(8) for kt in range(16)]

    def load_qpad(h):
        qp = Qpad[h % 2]
        kqp = "Qpad%d" % (h % 2)
        p.op("act", lambda e, qp=qp, h=h: e.activation(qp[0:64, 0, :], QT[0:64, h, :], AF.Copy), r=["QT"], w=[kqp])
        p.op("pool", lambda e, qp=qp, h=h: e.tensor_copy(qp[64:128, 1, :], QT[64:128, h, :]), r=["QT"], w=[kqp])

    def issue_qk(i):
        h, qg, kt = steps[i]
        sb = SB[i % NSB]
        qp = Qpad[h % 2]
        kqp = "Qpad%d" % (h % 2)
        for cm in range(2):
            p.op("pe", lambda e, sb=sb, cm=cm, h=h, kt=kt, qg=qg, qp=qp: e.matmul(ps[:, sb, cm * 256:(cm + 1) * 256], KT[:, h, kt * 128:(kt + 1) * 128], qp[:, cm, qg * 256:(qg + 1) * 256], start=True, stop=True),
                 r=["KT", kqp], w=["ps%d" % sb])

    if _nh > 0:
        load_qpad(0)
        if _nh > 1:
            load_qpad(1)
    for i in range(min(PF, len(steps))):
        issue_qk(i)
    ectr = 0
    deferred = []
    for i, (h, qg, kt) in enumerate(steps):
        accs = (2, 3) if ((h * 8 + qg) % 2 == 0) else (4, 5)
        sl = i % NSB
        sb = SB[sl]
        p.op("act", lambda e, sb=sb, sl=sl: e.activation(E[sl], ps[:, sb, :], AF.Exp, scale=0.125), r=["ps%d" % sb], w=["E%d" % sl])
        if i + PF < len(steps):
            issue_qk(i + PF)
        for cm in range(2):
            for qt in range(2):
                p.op("pe", lambda e, sl=sl, cm=cm, qt=qt, kt=kt, h=h, accs=accs: e.matmul(ps[:, accs[cm], qt * 130:(qt + 1) * 130], E[sl][:, cm * 256 + qt * 128: cm * 256 + (qt + 1) * 128], Vp[:, kt, h, 0:130], start=(kt == 0 and qt == 0), stop=(kt == 15), skip_group_check=True),
                     r=["E%d" % sl, "Vp"], w=["ps%d" % accs[cm]])
        if kt == 15 and qg == 7 and h + 2 < _nh:
            load_qpad(h + 2)
        if kt == 15:
            for qt in range(2):
                t = qg * 2 + qt
                i2 = ectr % 2
                ectr += 1
                ka = ["ps%d" % accs[0], "ps%d" % accs[1]]
                O1 = ps[:, accs[0], qt * 130: qt * 130 + 128]
                O2 = ps[:, accs[1], qt * 130: qt * 130 + 128]
                s1 = ps[:, accs[0], qt * 130 + 128: qt * 130 + 129]
                s2 = ps[:, accs[1], qt * 130 + 128: qt * 130 + 129]
                r_ = rr[i2]
                p.op("dve", lambda e, r_=r_, s1=s1: e.reciprocal(r_[:, 0:1], s1), r=ka, w=["rr%d" % i2])
                p.op("dve", lambda e, r_=r_, s2=s2: e.reciprocal(r_[:, 1:2], s2), r=ka, w=["rr%d" % i2])
                p.op("dve", lambda e, r_=r_: e.tensor_tensor(r_[:, 1:2], r_[:, 1:2], C.lamt[:, 3:4], op=ALU.mult), r=["rr%d" % i2, "lamt"], w=["rr%d" % i2])
                p.op("dve", lambda e, r_=r_, O1=O1, i2=i2: e.tensor_scalar(o1[i2], O1, r_[:, 0:1], None, op0=ALU.mult), r=ka + ["rr%d" % i2], w=["o1%d" % i2])
                p.op("dve", lambda e, r_=r_, O2=O2, i2=i2: e.scalar_tensor_tensor(o2[i2], O2, r_[:, 1:2], o1[i2], op0=ALU.mult, op1=ALU.add), r=ka + ["rr%d" % i2, "o1%d" % i2], w=["o2%d" % i2])

                def partB(i2=i2):
                    p.op("act", lambda e, i2=i2: e.activation(junk, o2[i2], AF.Square, accum_out=ss[i2][:, 0:1]), r=["o2%d" % i2], w=["ss%d" % i2])

                def partC(i2=i2):
                    C.rms_rstd(ss[i2][:, 0:1], rs[i2][:, 0:1], 1, "ss%d" % i2, "rs%d" % i2, 128.0)
                    p.op("dve", lambda e, i2=i2: e.scalar_tensor_tensor(at[i2], o2[i2], rs[i2][:, 0:1], C.gsub, op0=ALU.mult, op1=ALU.mult), r=["o2%d" % i2, "rs%d" % i2, C.k_gsub], w=["at%d" % i2])

                def partD(i2=i2):
                    psT = ps[:, 7, i2 * 64:(i2 + 1) * 64].bitcast(BF16)
                    p.op("pe", lambda e, psT=psT, i2=i2: e.transpose(psT, at[i2], ident), r=["at%d" % i2, "ident"], w=["ps7_%d" % i2])

                def partE(i2=i2, t=t, h=h):
                    psT = ps[:, 7, i2 * 64:(i2 + 1) * 64].bitcast(BF16)
                    p.op("act", lambda e, psT=psT, t=t, h=h: e.activation(mixedT[:, t, h, :], psT, AF.Copy), r=["ps7_%d" % i2], w=[kmix + "a%d" % t])

                deferred.append((i + 4, partB))
                deferred.append((i + 6, partC))
                deferred.append((i + 9, partD))
                deferred.append((i + 11, partE))
                deferred.sort(key=lambda x_: x_[0])
        while deferred and deferred[0][0] <= i:
            deferred.pop(0)[1]()
    while deferred:
        deferred.pop(0)[1]()
    C.full_barrier()
    A.release(m)


def ssd_phase(C, seq, mixedT, kmix, hnTm, khm):
    p, A, ps = C.p, C.A, C.ps
    ident = C.ident
    win = C.wb["w_in"].rearrange("(c p) n -> p c n", p=128)
    m0 = A.mark()
    zs = A.alloc([16, 1024], BF16)
    BCT = A.alloc([4, S], BF16)
    dtt = A.alloc([16, 32], F32)
    dat = A.alloc([16, 32], F32)
    m1 = A.mark()
    Wz = A.alloc([8, 1024], BF16)
    p.op("sp", lambda e: e.dma_start(out=Wz, in_=win[:, :, 3072:4096]), r=["wb_w_in"], w=["Wz"], dma="Wz")
    th = [A.alloc([1024], F32) for _ in range(2)]
    for t in range(16):
        b0 = 2 + 2 * (t % 2)
        i2 = t % 2
        for n in range(2):
            for c in range(8):
                p.op("pe", lambda e, n=n, c=c, b0=b0, t=t: e.matmul(ps[:, b0 + n, :], hnTm[:, c, t * 128:(t + 1) * 128], Wz[:, c, n * 512:(n + 1) * 512], start=(c == 0), stop=(c == 7)),
                     r=[khm, "Wz"], w=["ps%d" % (b0 + n)])
        kps = ["ps%d" % b0, "ps%d" % (b0 + 1)]
        pv = ps[:, b0:b0 + 2, :].rearrange("p a b -> p (a b)")
        p.op("act", lambda e, pv=pv, i2=i2: e.activation(th[i2], pv, AF.Tanh, scale=0.5), r=kps, w=["zth%d" % i2])
        p.op("dve", lambda e, pv=pv, i2=i2, t=t: e.scalar_tensor_tensor(zs[:, t, :], th[i2], 1.0, pv, op0=ALU.add, op1=ALU.mult), r=kps + ["zth%d" % i2], w=["zs"])
    C.full_barrier()
    A.release(m1)
    import os as _os
    _lvl = int(_os.environ.get("SSD_DBG", "9"))
    if _lvl <= 0:
        C.full_barrier()
        A.release(m0)
        return
    Wxs = [A.alloc([8, 128], BF16) for _ in range(2)]
    Wdt = A.alloc([8, 32], BF16)
    p.op("sp", lambda e: e.dma_start(out=Wdt, in_=win[:, :, 5632:5664]), r=["wb_w_in"], w=["Wdt"], dma="Wdt")
    dps = ps[:, 6, :].rearrange("p (t k) -> p t k", k=32)
    for t in range(16):
        for c in range(8):
            p.op("pe", lambda e, c=c, t=t: e.matmul(dps[:, t, :], hnTm[:, c, t * 128:(t + 1) * 128], Wdt[:, c, :], start=(c == 0), stop=(c == 7)), r=[khm, "Wdt"], w=["ps6"])
    p.op("dve", lambda e: e.tensor_tensor(dtt, dps, C.dtb.unsqueeze(1).to_broadcast([128, 16, 32]), op=ALU.add), r=["ps6", "dtb"], w=["dtt"])
    dflat = dtt.rearrange("p a b -> p (a b)")
    _dtl = int(_os.environ.get("SSD_DT", "9"))
    if _dtl >= 2:
        p.op("act", lambda e, dflat=dflat: e.activation(dflat, dflat, AF.Exp), r=["dtt"], w=["dtt"])
    if _dtl >= 3:
        p.op("act", lambda e, dflat=dflat: e.activation(dflat, dflat, AF.Ln, bias=C.ones32[:, 0:1]), r=["dtt", "cst"], w=["dtt"])
    if _dtl >= 4:
        p.op("dve", lambda e: e.tensor_tensor(dat, dtt, C.aneg.unsqueeze(1).to_broadcast([128, 16, 32]), op=ALU.mult), r=["dtt", "aneg"], w=["dat"])
    C.dump("dtt", dtt.rearrange("p a b -> p (a b)"), ["dtt"])
    C.dump("dat", dat.rearrange("p a b -> p (a b)"), ["dat"])
    C.dump("zs", zs.rearrange("p a b -> p (a b)"), ["zs"])
    if _lvl <= 1:
        C.full_barrier()
        A.release(m0)
        return
    xpad = A.alloc([S + 4], F32)
    acc = A.alloc([S], F32)
    cth = A.alloc([S], F32)
    xsT = A.alloc([S], BF16)
    p.op("pool", lambda e: e.memset(xpad, 0.0), w=["xpad"])
    for k in range(12):
        Wx = Wxs[k % 2]
        kWx = "Wx%d" % (k % 2)
        p.op("sp", lambda e, k=k, Wx=Wx: e.dma_start(out=Wx, in_=win[:, :, 4096 + k * 128: 4096 + (k + 1) * 128]), r=["wb_w_in"], w=[kWx], dma=kWx)
        for tg in range(4):
            for c in range(8):
                p.op("pe", lambda e, Wx=Wx, tg=tg, c=c: e.matmul(ps[:, 2 + tg, :], Wx[:, c, :], hnTm[:, c, tg * 512:(tg + 1) * 512], start=(c == 0), stop=(c == 7)),
                     r=[khm, kWx], w=["ps%d" % (2 + tg)])
            p.op("act", lambda e, tg=tg: e.activation(xpad[:, 2 + tg * 512: 2 + (tg + 1) * 512], ps[:, 2 + tg, :], AF.Copy), r=["ps%d" % (2 + tg)], w=["xpad"])
        p.op("dve", lambda e, k=k: e.tensor_scalar(acc, xpad[:, 0:S], C.cw[:, k, 0:1], None, op0=ALU.mult), r=["xpad", "cw"], w=["acc"])
        for j in range(1, 5):
            p.op("dve", lambda e, k=k, j=j: e.scalar_tensor_tensor(acc, xpad[:, j:j + S], C.cw[:, k, j:j + 1], acc, op0=ALU.mult, op1=ALU.add), r=["xpad", "cw", "acc"], w=["acc"])
        p.op("act", lambda e, k=k: e.activation(cth, acc, AF.Tanh, bias=C.cbh[:, k:k + 1], scale=0.5), r=["acc", "cbh"], w=["cth"])
        p.op("dve", lambda e, k=k: e.tensor_scalar(acc, acc, C.cb[:, k:k + 1], 0.5, op0=ALU.add, op1=ALU.mult), r=["acc", "cb", "cth"], w=["acc"])
        if k < 8:
            p.op("dve", lambda e: e.scalar_tensor_tensor(xsT, cth, 1.0, acc, op0=ALU.add, op1=ALU.mult), r=["cth", "acc"], w=["xsT"])
            for tq in range(2):
                pb = tq
                psT = ps[:, pb, :].bitcast(BF16).rearrange("p (t c) -> p t c", t=8)
                for tt in range(8):
                    t = tq * 8 + tt
                    p.op("pe", lambda e, psT=psT, tt=tt, t=t: e.transpose(psT[:, tt, :], xsT[:, t * 128:(t + 1) * 128], ident), r=["xsT", "ident"], w=["ps%d" % pb])
                p.op("act", lambda e, psT=psT, tq=tq, k=k: e.activation(mixedT[:, tq * 8:(tq + 1) * 8, k, :], psT, AF.Copy), r=["ps%d" % pb], w=[kmix + "x"])
        else:
            p.op("dve", lambda e, k=k: e.scalar_tensor_tensor(BCT[:, k - 8, :], cth, 1.0, acc, op0=ALU.add, op1=ALU.mult), r=["cth", "acc"], w=["BCT"])
    C.full_barrier()
    A.release(m1)
    C.dump("BCT", BCT.rearrange("p a b -> p (a b)"), ["BCT"])
    C.dump("xs", mixedT.rearrange("p a b c -> p (a b c)"), [kmix + "x"])
    C.full_barrier()
    if _lvl <= 2:
        A.release(m0)
        return
    A.words = C.words_save
    ysum = A.alloc([16, 1024], BF16)
    csc = A.alloc([16], F32)
    ecol = A.alloc([16], F32)
    dend = A.alloc([16], F32)
    cd = A.alloc([16], F32)
    Ah = A.alloc([16, 128], F32)
    dec = A.alloc([16, 128], F32)
    MT = A.alloc([16, 128], BF16)
    CBm = A.alloc([2, 128], BF16)
    xdt = A.alloc([16, 64], BF16)
    xdw = A.alloc([16, 64], BF16)
    Btok = A.alloc([2, 128], BF16)
    prev = A.alloc([16, 64], F32)
    pbf = A.alloc([16, 64], BF16)
    yc = A.alloc([16, 64], F32)
    y2 = A.alloc([16, 64], F32)
    ssq = A.alloc([8], F32)
    rst = A.alloc([8], F32)
    ynb = A.alloc([1024], BF16)
    junk = A.alloc([1024], BF16)
    gssd, kgssd = C.bcast_load("ssd_norm_g", 1024, key="gssd")

    def xs_tile(c):
        return mixedT[:, c, :, :].rearrange("p k (a b) -> p (k a) b", a=2)

    for d in (1, 0):
        Ud = C.Ub32 if d == 1 else C.Uf32
        maskd = C.maskb if d == 1 else C.maskf
        p.op("pool", lambda e: e.memset(prev.rearrange("p a b -> p (a b)"), 0.0), w=["prev"])
        p.op("pool", lambda e: e.memset(pbf.rearrange("p a b -> p (a b)"), 0.0), w=["pbf"])
        order = range(15, -1, -1) if d == 1 else range(16)
        if _lvl <= 3:
            order = list(order)[:1]
        _nch = int(_os.environ.get("SSD_NCH", "16"))
        order = list(order)[:_nch]
        if _lvl <= 4 and d == 0:
            break
        order = list(order)

        def emit_Ah(c_, d=d, Ud=Ud):
            da_n = dat[:, c_, 16 * d:16 * d + 16]
            p.op("pool", lambda e, da_n=da_n, Ud=Ud: e.tensor_tensor(Ah, Ud.unsqueeze(1).to_broadcast([128, 16, 128]), da_n.unsqueeze(2).to_broadcast([128, 16, 128]), op=ALU.mult), r=["dat", "cst"], w=["Ah"])

        for ci, c in enumerate(order):
            da_c = dat[:, c, 16 * d:16 * d + 16]
            dt_c = dtt[:, c, 16 * d:16 * d + 16]
            xs = xs_tile(c)
            kx = kmix + "x"
            cols = slice(c * 128, (c + 1) * 128)
            p.op("pe", lambda e, da_c=da_c, Ud=Ud: e.matmul(ps[:, 0, 0:16], Ud, da_c, start=True, stop=True), r=["dat", "cst"], w=["ps0"])
            p.op("pe", lambda e, da_c=da_c: e.matmul(ps[:, 0, 16:32], C.ones32, da_c, start=True, stop=True), r=["dat", "cst"], w=["ps0"])
            p.op("dve", lambda e: e.tensor_copy(csc, ps[:, 0, 0:16]), r=["ps0"], w=["csc"])
            p.op("act", lambda e: e.activation(ecol, ps[:, 0, 0:16], AF.Exp), r=["ps0"], w=["ecol"])
            p.op("act", lambda e: e.activation(cd, ps[:, 0, 16:32], AF.Exp), r=["ps0"], w=["cd"])
            p.op("dve", lambda e: e.tensor_tensor(dend, ps[:, 0, 16:32], csc, op=ALU.subtract), r=["ps0", "csc"], w=["dend"])
            p.op("act", lambda e: e.activation(dend, dend, AF.Exp), r=["dend"], w=["dend"])
            if ci == 0:
                emit_Ah(c)
            for q in range(4):
                p.op("pe", lambda e, q=q: e.matmul(ps[:, 2 + q, :], C.ones32, Ah[:, 4 * q:4 * q + 4, :].rearrange("p a b -> p (a b)"), start=True, stop=True), r=["Ah", "cst"], w=["ps%d" % (2 + q)])
                p.op("dve", lambda e, q=q: e.tensor_tensor(dec[:, 4 * q:4 * q + 4, :], ps[:, 2 + q, :].rearrange("p (a b) -> p a b", a=4), csc[:, 4 * q:4 * q + 4].unsqueeze(2).to_broadcast([128, 4, 128]), op=ALU.subtract), r=["ps%d" % (2 + q), "csc"], w=["dec"])
            dflat = dec.rearrange("p a b -> p (a b)")
            p.op("dve", lambda e, dflat=dflat: e.tensor_scalar_min(dflat, dflat, 0.0), r=["dec"], w=["dec"])
            p.op("act", lambda e, dflat=dflat: e.activation(dflat, dflat, AF.Exp), r=["dec"], w=["dec"])
            for g in range(2):
                p.op("pe", lambda e, g=g, cols=cols: e.matmul(ps[:, 1, g * 128:(g + 1) * 128], BCT[:, g, cols], BCT[:, 2 + g, cols], start=True, stop=True), r=["BCT"], w=["ps1"])
            p.op("dve", lambda e, maskd=maskd: e.tensor_tensor(CBm, ps[:, 1, 0:256].rearrange("p (a b) -> p a b", a=2), maskd.unsqueeze(1).to_broadcast([128, 2, 128]), op=ALU.mult), r=["ps1", "maskf", "maskb"], w=["CBm"])
            for g in range(2):
                p.op("dve" if g == 0 else "pool", lambda e, g=g: e.tensor_tensor(MT[:, 8 * g:8 * g + 8, :], dec[:, 8 * g:8 * g + 8, :], CBm[:, g, :].unsqueeze(1).to_broadcast([128, 8, 128]), op=ALU.mult), r=["dec", "CBm"], w=["MT%d" % g])
            if ci + 1 < len(order):
                emit_Ah(order[ci + 1])
            p.op("dve", lambda e, xs=xs, dt_c=dt_c: e.tensor_tensor(xdt, xs, dt_c.unsqueeze(2).to_broadcast([128, 16, 64]), op=ALU.mult), r=[kx, "dtt"], w=["xdt"])
            p.op("pool", lambda e: e.tensor_tensor(xdw, xdt, dend.unsqueeze(2).to_broadcast([128, 16, 64]), op=ALU.mult), r=["xdt", "dend"], w=["xdw"])
            xdf = xdt.rearrange("p a b -> p (a b)")
            for h in range(16):
                p.op("pe", lambda e, h=h, xdf=xdf: e.matmul(ps[:, 6 + h // 8, (h % 8) * 64:(h % 8 + 1) * 64], MT[:, h, :], xdf[:, h * 64:(h + 1) * 64], start=True, stop=True), r=["MT%d" % (h // 8), "xdt"], w=["ps%d" % (6 + h // 8)])
            pbff = pbf.rearrange("p a b -> p (a b)")
            for g in range(2):
                p.op("pe", lambda e, g=g, cols=cols, pbff=pbff: e.matmul(ps[:, 2 + g, :], BCT[:, 2 + g, cols], pbff[:, g * 512:(g + 1) * 512], start=True, stop=True), r=["BCT", "pbf"], w=["ps%d" % (2 + g)])
            yo = ps[:, 2:4, :].rearrange("p a b -> p (a b)").rearrange("p (h d) -> p h d", d=64)
            yd = ps[:, 6:8, :].rearrange("p a b -> p (a b)").rearrange("p (h d) -> p h d", d=64)
            p.op("dve", lambda e, yo=yo: e.tensor_tensor(yc, yo, ecol.unsqueeze(2).to_broadcast([128, 16, 64]), op=ALU.mult), r=["ps2", "ps3", "ecol"], w=["yc"])
            p.op("dve", lambda e, yd=yd: e.tensor_tensor(yc, yd, yc, op=ALU.add), r=["ps6", "ps7", "yc"], w=["yc"])
            for g in range(2):
                psB = ps[:, 1, 256 + g * 64: 256 + (g + 1) * 64].bitcast(BF16)
                p.op("pe", lambda e, g=g, cols=cols, psB=psB: e.transpose(psB, BCT[:, g, cols], ident), r=["BCT", "ident"], w=["ps1"])
            p.op("act", lambda e: e.activation(Btok, ps[:, 1, 256:384].bitcast(BF16).rearrange("p (a b) -> p a b", a=2), AF.Copy), r=["ps1"], w=["Btok"])
            xwf = xdw.rearrange("p a b -> p (a b)")
            for g in range(2):
                p.op("pe", lambda e, g=g, xwf=xwf: e.matmul(ps[:, 4 + g, :], Btok[:, g, :], xwf[:, g * 512:(g + 1) * 512], start=True, stop=True), r=["Btok", "xdw"], w=["ps%d" % (4 + g)])
            st_ = ps[:, 4:6, :].rearrange("p a b -> p (a b)").rearrange("p (h d) -> p h d", d=64)
            p.op("pool", lambda e: e.tensor_tensor(prev, prev, cd.unsqueeze(2).to_broadcast([128, 16, 64]), op=ALU.mult), r=["prev", "cd"], w=["prev"])
            p.op("dve", lambda e, st_=st_: e.tensor_tensor(prev, prev, st_, op=ALU.add), r=["prev", "ps4", "ps5"], w=["prev"])
            p.op("act", lambda e: e.activation(pbf, prev, AF.Copy), r=["prev"], w=["pbf"])
            if d == 1 and c == 15:
                C.dump("csc", csc, ["csc"]); C.dump("ecol", ecol, ["ecol"]); C.dump("dend", dend, ["dend"]); C.dump("cd", cd, ["cd"])
                C.dump("dec", dec.rearrange("p a b -> p (a b)"), ["dec"]); C.dump("MT", MT.rearrange("p a b -> p (a b)"), ["MT0", "MT1"])
                C.dump("yc", yc.rearrange("p a b -> p (a b)"), ["yc"]); C.dump("prev", prev.rearrange("p a b -> p (a b)"), ["prev"])
                C.dump("xdt", xdt.rearrange("p a b -> p (a b)"), ["xdt"]); C.dump("CBm", CBm.rearrange("p a b -> p (a b)"), ["CBm"])
            if d == 1:
                p.op("act", lambda e, c=c: e.activation(ysum[:, c, :], yc.rearrange("p a b -> p (a b)"), AF.Copy), r=["yc"], w=["ysum"])
            else:
                _ol = int(_os.environ.get("SSD_OUT", "9"))
                if _ol >= 1:
                    p.op("pool", lambda e, c=c: e.tensor_tensor(yc, yc, ysum[:, c, :].rearrange("p (a b) -> p a b", a=16), op=ALU.add), r=["yc", "ysum"], w=["yc"])
                    p.op("pool", lambda e, xs=xs: e.tensor_tensor(y2, xs, C.dsk.unsqueeze(2).to_broadcast([128, 16, 64]), op=ALU.mult), r=[kx, C.k_dsk], w=["y2"])
                    p.op("pool", lambda e: e.tensor_tensor(yc, yc, y2, op=ALU.add), r=["yc", "y2"], w=["yc"])
                ycf = yc.rearrange("p a b -> p (a b)")
                if _ol >= 2:
                    p.op("dve", lambda e, c=c, ycf=ycf: e.tensor_tensor(ycf, ycf, zs[:, c, :], op=ALU.mult), r=["yc", "zs"], w=["yc"])
                if _ol >= 3:
                    p.op("act", lambda e, ycf=ycf: e.activation(junk, ycf, AF.Square, accum_out=ssq[:, 0:1]), r=["yc"], w=["ssq"])
                    p.op("dve", lambda e: e.tensor_scalar(rst[:, 0:1], ssq[:, 0:1], 1.0 / 1024, 4.0 * EPS, op0=ALU.mult, op1=ALU.add), r=["ssq"], w=["rst"])
                    p.op("pool", lambda e: e.tensor_tensor(rst[:, 0:1], rst[:, 0:1], C.negh[:, 0:1], op=ALU.pow), r=["rst", "negh"], w=["rst"])
                if _ol >= 4:
                    p.op("dve", lambda e, ycf=ycf, c=c: e.scalar_tensor_tensor(ysum[:, c, :], ycf, rst[:, 0:1], gssd, op0=ALU.mult, op1=ALU.mult), r=["yc", "rst", kgssd, "ysum"], w=["ysum"])
    C.full_barrier()
    for c in range(16):
        bank = c % 2
        psT = ps[:, bank, :].bitcast(BF16).rearrange("p (k t) -> p k t", k=8)
        for k in range(8):
            p.op("pe", lambda e, k=k, psT=psT, c=c: e.transpose(psT[:, k, :], ysum[:, c, k * 128:(k + 1) * 128], ident), r=["ysum", "ident"], w=["ps%d" % bank])
        p.op("act", lambda e, c=c, psT=psT: e.activation(mixedT[:, c, :, :], psT, AF.Copy), r=["ps%d" % bank], w=[kmix + "x"])
    C.full_barrier()
    A.release(m0)


def oproj_ffn2_phase(C, seq, mixedT, kmix):
    p, A, ps = C.p, C.A, C.ps
    tok0 = seq * S
    words_save = A.words
    h2 = A.alloc_top([16, 1024], F32)
    m = A.mark()
    Wout = A.alloc([16, 1024], BF16)
    p.op("sp", lambda e: e.dma_start(out=Wout, in_=C.wb["w_out"].rearrange("(k p) n -> p k n", p=128)), r=["wb_w_out"], w=["Wout"], dma="Wout")
    gmp, kgmp = C.bcast_load("mix_post_g", 1024, key="gmp")
    t1 = A.alloc([1024], F32)
    junk = A.alloc([1024], BF16)
    ss = A.alloc([8], F32)
    rs = A.alloc([8], F32)
    allmix = [kmix + "x"] + [kmix + "a%d" % t for t in range(16)]
    for t in range(16):
        r0 = tok0 + t * 128
        p.op("sp", lambda e, t=t, r0=r0: e.dma_start(out=h2[:, t, :], in_=C.h1d[r0:r0 + 128, :]), r=["h1d"], w=["h2_%d" % (t // 4)], dma="h2ld")
        b0 = 2 + 2 * (t % 3)
        for n in range(2):
            for kc in range(16):
                p.op("pe", lambda e, t=t, n=n, kc=kc, b0=b0: e.matmul(ps[:, b0 + n, :], (C.attnT[:, t, kc, :] if kc < 8 else C.ssdT[:, t, kc - 8, :]), Wout[:, kc, n * 512:(n + 1) * 512], start=(kc == 0), stop=(kc == 15)),
                     r=allmix + ["Wout"], w=["ps%d" % (b0 + n)])
        kps = ["ps%d" % b0, "ps%d" % (b0 + 1)]
        fps = ps[:, b0:b0 + 2, :].rearrange("p a b -> p (a b)")
        p.op("act", lambda e, fps=fps: e.activation(junk, fps, AF.Square, accum_out=ss[:, 0:1]), r=kps, w=["oss"])
        C.rms_rstd(ss[:, 0:1], rs[:, 0:1], 1, "oss", "ors", 1024.0)
        p.op("dve", lambda e, fps=fps: e.scalar_tensor_tensor(t1, fps, rs[:, 0:1], gmp, op0=ALU.mult, op1=ALU.mult), r=kps + ["ors", kgmp], w=["ot1"])
        p.op("pool", lambda e, t=t: e.tensor_tensor(h2[:, t, :], h2[:, t, :], t1, op=ALU.add), r=["ot1", "h2_%d" % (t // 4)], w=["h2_%d" % (t // 4)])
    C.full_barrier()
    A.release(C.seq_mark)
    if "f2" in C.stages:
        gfin, kgfin = C.bcast_load("final_g", 1024, key="gfin")
        ot1 = A.alloc([1024], F32)
        ot = [ot1, ot1]
        fss = A.alloc([8], F32)
        frs = A.alloc([8], F32)
        junk2 = A.alloc([1024], BF16)

        def get_src(g):
            return h2[:, 4 * g:4 * g + 4, :], "h2_%d" % g

        def epi(g, t, res, rkey):
            i2 = t % 2
            r0 = tok0 + g * G + t * 128
            p.op("act", lambda e: e.activation(junk2, res, AF.Square, accum_out=fss[:, 0:1]), r=[rkey], w=["fss"])
            C.rms_rstd(fss[:, 0:1], frs[:, 0:1], 1, "fss", "frs", 1024.0)
            p.op("dve", lambda e: e.scalar_tensor_tensor(ot[i2], res, frs[:, 0:1], gfin, op0=ALU.mult, op1=ALU.mult), r=[rkey, "frs", kgfin], w=["otf"])
            p.op("sp", lambda e: e.dma_start(out=C.out[r0:r0 + 128, :], in_=ot[i2]), r=["otf"], w=["outd"], dma="ost")

        C.ffn_phase("ffn2", seq, get_src, epi)
    A.words = words_save


_WNAMES = ["ffn1_w_gate", "ffn1_w_up", "ffn1_w_down", "ffn2_w_gate", "ffn2_w_up", "ffn2_w_down", "w_in", "w_out"]
_VNAMES = ["ffn1_pre_g", "ffn1_post_g", "mix_pre_g", "mix_post_g", "ffn2_pre_g", "ffn2_post_g", "final_g", "ssd_norm_g",
           "attn_subln_g", "lambda_q1", "lambda_k1", "lambda_q2", "lambda_k2", "a_log_fwd", "a_log_bwd",
           "dt_bias_fwd", "dt_bias_bwd", "d_skip"]


def make_in_map(inp, xs):
    m = {"x": np.ascontiguousarray(xs, dtype=np.float32), "consts": _const_pack()}
    for n in _WNAMES:
        m[n] = np.ascontiguousarray(np.asarray(inp[n])[0], dtype=np.float32)
    for n in _VNAMES:
        m[n] = np.ascontiguousarray(np.asarray(inp[n]).reshape(1, -1), dtype=np.float32)
    cwt = np.asarray(inp["conv_w"])[0].T.reshape(12, 128, 5).transpose(1, 0, 2).reshape(128, 60)
    m["conv_w"] = np.ascontiguousarray(cwt, dtype=np.float32)
    m["conv_b"] = np.ascontiguousarray(np.asarray(inp["conv_b"]).reshape(12, 128).T, dtype=np.float32)
    return m


_CACHE = {}


def kernel(**inputs):
    x = np.asarray(inputs["x"], dtype=np.float32)
    B = x.shape[0]
    per = B // NCORES
    if "nc" not in _CACHE:
        _CACHE["nc"] = build_program(per)
    nc = _CACHE["nc"]
    in_maps = [make_in_map(inputs, x[i * per:(i + 1) * per].reshape(per * S, D)) for i in range(NCORES)]
    res = run_bass_kernel_spmd(nc, in_maps, core_ids=list(range(NCORES)))
    outs = [np.asarray(r["out"]).reshape(per, S, D) for r in res.results]
    return np.concatenate(outs, axis=0).astype(np.float32)
```

```python
import numpy as np
import concourse.bass as bass
import concourse.mybir as mybir
from concourse.bass_utils import run_bass_kernel_spmd

F32 = mybir.dt.float32
BF16 = mybir.dt.bfloat16
AF = mybir.ActivationFunctionType
ALU = mybir.AluOpType
AX = mybir.AxisListType

D = 1024
S = 2048
DFF = 2816
NJ = DFF // 128
DIN = 5664
EPS = 1e-6
NCORES = 8
G = 512
NG = S // G

SAME_ENGINE_SYNC = True


class _Op:
    __slots__ = ("eng", "fn", "deps", "dma", "dmacnt", "sig", "seq", "dmadeps", "strict")


class Prog:
    def __init__(self, nc):
        self.nc = nc
        self.ops = []
        self.last_w = {}
        self.readers = {}
        self.dma_cnt = {}

    def op(self, eng, fn, r=(), w=(), dma=None, strict=False):
        idx = len(self.ops)
        o = _Op()
        o.strict = strict
        o.eng = eng
        o.fn = fn
        o.dma = dma
        deps = set()
        for k in r:
            d = self.last_w.get(k)
            if d is not None:
                deps.add(d)
        for k in w:
            d = self.last_w.get(k)
            if d is not None:
                deps.add(d)
            for x in self.readers.get(k, ()):
                deps.add(x)
        deps.discard(idx)
        o.deps = []
        o.dmadeps = []
        for d in deps:
            od = self.ops[d]
            if od.dma is not None:
                o.dmadeps.append((od.dma, od.dmacnt))
            else:
                o.deps.append(d)
        if dma is not None:
            self.dma_cnt[dma] = self.dma_cnt.get(dma, 0) + 16
            o.dmacnt = self.dma_cnt[dma]
        else:
            o.dmacnt = 0
        for k in w:
            self.last_w[k] = idx
            self.readers[k] = []
        for k in r:
            lst = self.readers.setdefault(k, [])
            if dma is None:
                lst[:] = [x for x in lst if not (self.ops[x].eng == eng and self.ops[x].dma is None)]
            lst.append(idx)
        o.sig = False
        o.seq = 0
        self.ops.append(o)
        return idx

    def emit(self, stack):
        nc = self.nc
        ops = self.ops
        engs = {"pe": nc.tensor, "act": nc.scalar, "dve": nc.vector, "pool": nc.gpsimd, "sp": nc.sync}
        for o in ops:
            for d in o.deps:
                od = ops[d]
                if od.eng == o.eng and not o.strict:
                    if od.eng == "pe" or od.eng == "sp" or not SAME_ENGINE_SYNC:
                        continue
                od.sig = True
        cnt = {e: 0 for e in engs}
        for o in ops:
            if o.dma is None and o.sig:
                cnt[o.eng] += 1
                o.seq = cnt[o.eng]
        print("SEMCNT", cnt, "nops", len(ops), "dma", {k: v // 16 for k, v in self.dma_cnt.items() if v > 16 * 100})
        esem = {e: stack.enter_context(nc.semaphore("s_" + e)) for e in engs}
        dsem = {k: stack.enter_context(nc.semaphore("d_" + str(k))) for k in self.dma_cnt}
        waited = {e: {} for e in engs}
        for o in ops:
            E = engs[o.eng]
            wt = waited[o.eng]
            need = {}
            for d in o.deps:
                od = ops[d]
                if not od.sig:
                    continue
                if od.eng == o.eng and not o.strict and (od.eng in ("pe", "sp") or not SAME_ENGINE_SYNC):
                    continue
                if od.seq > need.get(od.eng, 0):
                    need[od.eng] = od.seq
            for e2, v in need.items():
                if wt.get(e2, 0) < v:
                    E.wait_ge(esem[e2], v)
                    wt[e2] = v
            for (k, c) in o.dmadeps:
                kk = ("dma", k)
                if wt.get(kk, 0) < c:
                    E.wait_ge(dsem[k], c)
                    wt[kk] = c
            ins = o.fn(E)
            if o.dma is not None:
                ins.then_inc(dsem[o.dma], 16)
            elif o.sig:
                ins.then_inc(esem[o.eng], 1)
        for k, c in self.dma_cnt.items():
            nc.sync.wait_ge(dsem[k], c)
        for e in engs:
            if e != "sp" and cnt[e] > 0:
                nc.sync.wait_ge(esem[e], cnt[e])


class Arena:
    def __init__(self, t32, words):
        self.t = t32
        self.words = words
        self.top = 0

    def alloc(self, shape, dtype):
        n = 1
        for s in shape:
            n *= s
        nbytes = n * (4 if dtype == F32 else 2)
        w = (nbytes + 3) // 4
        w = (w + 7) // 8 * 8
        off = self.top
        self.top += w
        assert self.top <= self.words, ("arena overflow", self.top, self.words)
        ap = self.t[:, off:off + w]
        if dtype != F32:
            ap = ap.bitcast(dtype)[:, 0:n]
        else:
            ap = ap[:, 0:n]
        if len(shape) == 2:
            return ap.rearrange("p (a b) -> p a b", a=shape[0])
        if len(shape) == 3:
            return ap.rearrange("p (a b c) -> p a b c", a=shape[0], b=shape[1])
        return ap

    def alloc_top(self, shape, dtype):
        n = 1
        for s in shape:
            n *= s
        nbytes = n * (4 if dtype == F32 else 2)
        w = (nbytes + 3) // 4
        w = (w + 7) // 8 * 8
        self.words -= w
        assert self.top <= self.words, ("arena overflow(top)", self.top, self.words)
        off = self.words
        ap = self.t[:, off:off + w]
        ap = ap.bitcast(dtype)[:, 0:n] if dtype != F32 else ap[:, 0:n]
        if len(shape) == 2:
            return ap.rearrange("p (a b) -> p a b", a=shape[0])
        if len(shape) == 3:
            return ap.rearrange("p (a b c) -> p a b c", a=shape[0], b=shape[1])
        return ap

    def mark(self):
        return self.top

    def release(self, m):
        self.top = m


NCONST = 128 * 4 + 2 * 16 * 8


def _const_pack():
    c = np.zeros((128, NCONST), np.float32)
    i = np.arange(128)
    c[:, 0:128] = np.eye(128, dtype=np.float32)
    c[:, 128:256] = (i[:, None] <= i[None, :]).astype(np.float32)
    c[:, 256:384] = (i[:, None] >= i[None, :]).astype(np.float32)
    c[:, 384:512] = 1.0
    pos = np.arange(S, dtype=np.float32)
    inv = np.power(np.float32(500000.0), -np.arange(0, 16, 2, dtype=np.float32) / np.float32(16)).astype(np.float32)
    ang = (pos[:, None] * inv[None, :]).astype(np.float32)
    cs = np.cos(ang).astype(np.float32).reshape(16, 128, 8).transpose(1, 0, 2).reshape(128, 128)
    sn = np.sin(ang).astype(np.float32).reshape(16, 128, 8).transpose(1, 0, 2).reshape(128, 128)
    c[:, 512:640] = cs
    c[:, 640:768] = sn
    return c


class Ctx:
    pass


def build_program(nseq, dbg=None, stages=("f1", "attn", "ssd", "o", "f2")):
    from contextlib import ExitStack
    nc = bass.Bass("TRN2", target_bir_lowering=False)
    T = nseq * S
    dr = lambda n, sh, dt=F32, kind="ExternalInput": nc.dram_tensor(n, sh, dt, kind=kind).ap()
    x = dr("x", [T, D])
    out = dr("out", [T, D], kind="ExternalOutput")
    consts = dr("consts", [128, NCONST])
    wnames = {"ffn1_w_gate": [D, DFF], "ffn1_w_up": [D, DFF], "ffn1_w_down": [DFF, D],
              "ffn2_w_gate": [D, DFF], "ffn2_w_up": [D, DFF], "ffn2_w_down": [DFF, D],
              "w_in": [D, DIN], "w_out": [2048, D]}
    wf = {n: dr(n, sh) for n, sh in wnames.items()}
    wb = {n: dr(n + "_bf", sh, BF16, kind="Internal") for n, sh in wnames.items()}
    vnames = {"ffn1_pre_g": 1024, "ffn1_post_g": 1024, "mix_pre_g": 1024, "mix_post_g": 1024,
              "ffn2_pre_g": 1024, "ffn2_post_g": 1024, "final_g": 1024, "ssd_norm_g": 1024,
              "attn_subln_g": 128, "lambda_q1": 64, "lambda_k1": 64, "lambda_q2": 64, "lambda_k2": 64,
              "a_log_fwd": 16, "a_log_bwd": 16, "dt_bias_fwd": 16, "dt_bias_bwd": 16,
              "d_skip": 16}
    vf = {n: dr(n, [1, k]) for n, k in vnames.items()}
    conv_w = dr("conv_w", [128, 60])
    conv_b = dr("conv_b", [128, 12])
    h1d = dr("h1_spill", [T, D], F32, kind="Internal")
    dbg_out = {}
    if dbg:
        for n, sh in dbg.items():
            if isinstance(sh, tuple):
                dbg_out[n] = dr("dbg_" + n, sh[0], sh[1], kind="ExternalOutput")
            else:
                dbg_out[n] = dr("dbg_" + n, sh, kind="ExternalOutput")

    with ExitStack() as st:
        AW = 53000
        arena_t = st.enter_context(nc.sbuf_tensor("arena", [128, AW], F32))
        ps = st.enter_context(nc.psum_tensor("ps", [128, 8, 512], F32))
        A = Arena(arena_t, AW)
        p = Prog(nc)
        C = Ctx()
        C.nc, C.p, C.A, C.ps = nc, p, A, ps
        uid = [0]

        def U(s):
            uid[0] += 1
            return "%s#%d" % (s, uid[0])

        live_regions = set()
        _orig_op = p.op

        def op(eng, fn, r=(), w=(), dma=None):
            for k in w:
                live_regions.add(k)
            for k in r:
                live_regions.add(k)
            return _orig_op(eng, fn, r=r, w=w, dma=dma)
        p.op = op

        def full_barrier():
            regs = list(live_regions)
            for e in ("pe", "act", "dve", "pool", "sp"):
                _orig_op(e, (lambda E: E.nop()), r=(), w=regs, strict=True)

        cst = A.alloc([NCONST], F32)
        p.op("sp", lambda e: e.dma_start(out=cst, in_=consts), w=["cst"], dma="cst")
        ident = A.alloc([128], BF16)
        p.op("dve", lambda e: e.tensor_copy(ident, cst[:, 0:128]), r=["cst"], w=["ident"])
        Uf32 = cst[:, 128:256]
        Ub32 = cst[:, 256:384]
        ones32 = cst[:, 384:512]
        maskf = A.alloc([128], BF16)
        maskb = A.alloc([128], BF16)
        p.op("dve", lambda e: e.tensor_copy(maskf, Uf32), r=["cst"], w=["maskf"])
        p.op("dve", lambda e: e.tensor_copy(maskb, Ub32), r=["cst"], w=["maskb"])
        cosT = cst[:, 512:640].rearrange("p (t f) -> p t f", t=16)
        sinT = cst[:, 640:768].rearrange("p (t f) -> p t f", t=16)
        negh = A.alloc([16], F32)
        p.op("pool", lambda e: e.memset(negh, -0.5), w=["negh"])

        def bcast_load(name, n, key=None):
            t = A.alloc([n], F32)
            k = key or ("v_" + name)
            p.op("sp", lambda e: e.dma_start(out=t, in_=vf[name].partition_broadcast(128)), w=[k], dma=k)
            return t, k

        gsub, k_gsub = bcast_load("attn_subln_g", 128)
        p.op("dve", lambda e: e.tensor_scalar(gsub, gsub, 1.0 - (0.8 - 0.6), None, op0=ALU.mult), r=[k_gsub], w=[k_gsub])
        lq1, k1 = bcast_load("lambda_q1", 64)
        lk1, k2 = bcast_load("lambda_k1", 64)
        lq2, k3 = bcast_load("lambda_q2", 64)
        lk2, k4 = bcast_load("lambda_k2", 64)
        lamt = A.alloc([8], F32)
        ljunk = A.alloc([64], F32)
        p.op("dve", lambda e: e.tensor_tensor(ljunk, lq1, lk1, op=ALU.mult), r=[k1, k2], w=["ljunk"])
        p.op("dve", lambda e: e.reduce_sum(lamt[:, 0:1], ljunk, axis=AX.X), r=["ljunk"], w=["lamt"])
        p.op("dve", lambda e: e.tensor_tensor(ljunk, lq2, lk2, op=ALU.mult), r=[k3, k4, "lamt"], w=["ljunk"])
        p.op("dve", lambda e: e.reduce_sum(lamt[:, 1:2], ljunk, axis=AX.X), r=["ljunk"], w=["lamt"])
        p.op("act", lambda e: e.activation(lamt[:, 0:2], lamt[:, 0:2], AF.Exp), r=["lamt"], w=["lamt"])
        p.op("dve", lambda e: e.tensor_tensor(lamt[:, 2:3], lamt[:, 0:1], lamt[:, 1:2], op=ALU.subtract), r=["lamt"], w=["lamt"])
        p.op("dve", lambda e: e.tensor_scalar(lamt[:, 2:3], lamt[:, 2:3], 0.8 - 0.6, None, op0=ALU.add), r=["lamt"], w=["lamt"])
        p.op("dve", lambda e: e.tensor_scalar(lamt[:, 3:4], lamt[:, 2:3], -1.0, None, op0=ALU.mult), r=["lamt"], w=["lamt"])
        alog = A.alloc([32], F32)
        p.op("sp", lambda e: e.dma_start(out=alog[:, 0:16], in_=vf["a_log_fwd"].partition_broadcast(128)), w=["alog"], dma="alog")
        p.op("sp", lambda e: e.dma_start(out=alog[:, 16:32], in_=vf["a_log_bwd"].partition_broadcast(128)), w=["alog"], dma="alog")
        aneg = A.alloc([32], F32)
        p.op("act", lambda e: e.activation(aneg, alog, AF.Exp), r=["alog"], w=["aneg"])
        p.op("dve", lambda e: e.tensor_scalar(aneg, aneg, -1.0, None, op0=ALU.mult), r=["aneg"], w=["aneg"])
        dtb = A.alloc([32], F32)
        p.op("sp", lambda e: e.dma_start(out=dtb[:, 0:16], in_=vf["dt_bias_fwd"].partition_broadcast(128)), w=["dtb"], dma="dtb")
        p.op("sp", lambda e: e.dma_start(out=dtb[:, 16:32], in_=vf["dt_bias_bwd"].partition_broadcast(128)), w=["dtb"], dma="dtb")
        dsk, k_dsk = bcast_load("d_skip", 16)
        cw = A.alloc([12, 5], F32)
        cb = A.alloc([12], F32)
        p.op("sp", lambda e: e.dma_start(out=cw.rearrange("p c k -> p (c k)"), in_=conv_w), w=["cw"], dma="cw")
        p.op("sp", lambda e: e.dma_start(out=cb, in_=conv_b), w=["cb"], dma="cb")
        cbh = A.alloc([12], F32)
        p.op("dve", lambda e: e.tensor_scalar(cbh, cb, 0.5, None, op0=ALU.mult), r=["cb"], w=["cbh"])

        def cast_weight(n):
            rows = wnames[n][0]
            nsplit = 4
            rs = rows // nsplit
            for i in range(nsplit):
                p.op("pool", lambda e, n=n, i=i, rs=rs: e.dma_start(out=wb[n][i * rs:(i + 1) * rs, :], in_=wf[n][i * rs:(i + 1) * rs, :]),
                     w=["wb_" + n], dma="wb_" + n)
        C.deferred_casts = []
        for n in wnames:
            if n.startswith("ffn1"):
                cast_weight(n)
            else:
                C.deferred_casts.append(n)
        C.cast_weight = cast_weight

        C.ident, C.negh = ident, negh
        junk1 = A.alloc([1024], BF16)
        base_mark = A.mark()

        def rms_rstd(ssq_ap, rstd_ap, n, kr, kw, width):
            p.op("dve", lambda e: e.tensor_scalar(rstd_ap, ssq_ap, 1.0 / width, EPS, op0=ALU.mult, op1=ALU.add), r=[kr], w=[kw])
            p.op("pool", lambda e: e.tensor_tensor(rstd_ap, rstd_ap, negh[:, 0:n], op=ALU.pow), r=[kw, "negh"], w=[kw])

        tr_ctr = [0]

        def norm_transpose(src, src_key, g_b, g_key, dstT, dst_key, dst_cols, bufs):
            i = tr_ctr[0] % 2
            tr_ctr[0] += 1
            junk, ssq, rstd, xn = bufs["junk"][i], bufs["ssq"][i], bufs["rstd"][i], bufs["xn"][i]
            kj, ks, kr, kx = "ntj%d" % i, "nts%d" % i, "ntr%d" % i, "ntx%d" % i
            bank = 0 + i
            kb = "ps%d" % bank
            p.op("act", lambda e: e.activation(junk, src, AF.Square, accum_out=ssq[:, 0:1]), r=[src_key], w=[ks, "junk1"])
            rms_rstd(ssq[:, 0:1], rstd[:, 0:1], 1, ks, kr, 1024.0)
            p.op("dve", lambda e: e.scalar_tensor_tensor(xn, src, rstd[:, 0:1], g_b, op0=ALU.mult, op1=ALU.mult), r=[src_key, kr, g_key], w=[kx])
            psT = ps[:, bank, :].bitcast(BF16).rearrange("p (c t) -> p c t", c=8)
            for c in range(8):
                p.op("pe", lambda e, c=c: e.transpose(psT[:, c, :], xn[:, c * 128:(c + 1) * 128], ident), r=[kx, "ident"], w=[kb])
            p.op("act", lambda e: e.activation(dstT[:, :, dst_cols], psT, AF.Copy), r=[kb], w=[dst_key])

        def alloc_norm_bufs():
            return {"junk": [junk1, junk1],
                    "ssq": [A.alloc([8], F32) for _ in range(2)],
                    "rstd": [A.alloc([8], F32) for _ in range(2)],
                    "xn": [A.alloc([1024], BF16) for _ in range(2)]}

        def ffn_phase(which, seq, get_src, epilogue):
            m = A.mark()
            wg, wu, wd = wb[which + "_w_gate"], wb[which + "_w_up"], wb[which + "_w_down"]
            gpre, kpre = bcast_load(which + "_pre_g", 1024, key="gpre")
            gpost, kpost = bcast_load(which + "_post_g", 1024, key="gpost")
            p.op("dve", lambda e: e.tensor_scalar(gpost, gpost, 0.5, None, op0=ALU.mult), r=[kpost], w=[kpost])
            nb = alloc_norm_bufs()
            hnT = [A.alloc([8, G], BF16) for _ in range(2)]
            actT = A.alloc([NJ, G], BF16)
            Wd = A.alloc([NJ, 1024], BF16)
            Wgu = [A.alloc([2, 8, 256], BF16) for _ in range(2)]
            th = [A.alloc([512], F32) for _ in range(3)]
            aa = [A.alloc([512], F32) for _ in range(2)]
            t1x = A.alloc([1024], F32)
            t1 = [t1x, t1x]
            fj = [junk1, junk1]
            fs = [A.alloc([8], F32) for _ in range(2)]
            fr = [A.alloc([8], F32) for _ in range(2)]
            NJB = 11
            wctr = 0
            srcs = {}
            pend_epi = []
            srcs[0] = get_src(0)
            for t in range(4):
                norm_transpose(srcs[0][0][:, t, :], srcs[0][1], gpre, kpre, hnT[0], "hnT0", slice(t * 128, (t + 1) * 128), nb)
            for g in range(NG):
                src, skey = srcs[g]
                hT = hnT[g % 2]
                khT = "hnT%d" % (g % 2)
                for jb in range(NJB):
                    ncols = 256
                    slot = wctr % 2
                    wctr += 1
                    W = Wgu[slot]
                    kW = "Wgu%d" % slot
                    for wi, wsrc in enumerate((wg, wu)):
                        p.op("sp", lambda e, wi=wi, wsrc=wsrc, jb=jb, ncols=ncols, W=W: e.dma_start(
                            out=W[:, wi, :, 0:ncols], in_=wsrc.rearrange("(c p) n -> p c n", p=128)[:, :, jb * 256: jb * 256 + ncols]),
                            r=["wb_" + which + ("_w_gate" if wi == 0 else "_w_up")], w=[kW], dma=kW)
                    p.op("sp", lambda e, jb=jb: e.dma_start(out=Wd[:, 2 * jb:2 * jb + 2, :], in_=wd.rearrange("(j p) n -> p j n", p=128)[:, 2 * jb:2 * jb + 2, :]),
                         r=["wb_" + which + "_w_down"], w=["Wd"], dma="Wd")
                    if jb < 4 and pend_epi:
                        epilogue(*pend_epi.pop(0))
                    if g + 1 < NG and jb == 4:
                        srcs[g + 1] = get_src(g + 1)
                    if g + 1 < NG and jb in (5, 6, 7, 8):
                        tn = jb - 5
                        nsrc, nkey = srcs[g + 1]
                        norm_transpose(nsrc[:, tn, :], nkey, gpre, kpre, hnT[(g + 1) % 2], "hnT%d" % ((g + 1) % 2), slice(tn * 128, (tn + 1) * 128), nb)
                    for jj in range(ncols // 128):
                        j = jb * 2 + jj
                        pb = 2 + 2 * (j % 3)
                        for wi in range(2):
                            for c in range(8):
                                p.op("pe", lambda e, wi=wi, c=c, jj=jj, pb=pb, W=W, hT=hT: e.matmul(ps[:, pb + wi, :], W[:, wi, c, jj * 128:(jj + 1) * 128], hT[:, c, :], start=(c == 0), stop=(c == 7)),
                                     r=[kW, khT], w=["ps%d" % (pb + wi)])
                        i2 = j % 3
                        p.op("act", lambda e, pb=pb, i2=i2: e.activation(th[i2], ps[:, pb, :], AF.Tanh, scale=0.5), r=["ps%d" % pb], w=["th%d" % i2])
                        i3 = j % 2
                        p.op("dve", lambda e, pb=pb, i2=i2, i3=i3: e.scalar_tensor_tensor(aa[i3], th[i2], 1.0, ps[:, pb, :], op0=ALU.add, op1=ALU.mult), r=["th%d" % i2, "ps%d" % pb], w=["aa%d" % i3])
                        p.op("dve", lambda e, pb=pb, i3=i3, j=j: e.scalar_tensor_tensor(actT[:, j, :], aa[i3], 0.5, ps[:, pb + 1, :], op0=ALU.mult, op1=ALU.mult), r=["aa%d" % i3, "ps%d" % (pb + 1)], w=["actT"])
                if C.deferred_casts and g == 0:
                    for n_ in C.deferred_casts:
                        C.cast_weight(n_)
                    C.deferred_casts = []
                for t in range(4):
                    pb = 2 + 2 * (t % 3)
                    i2 = t % 2
                    for n in range(2):
                        for j in range(NJ):
                            p.op("pe", lambda e, n=n, j=j, t=t, pb=pb: e.matmul(ps[:, pb + n, :], actT[:, j, t * 128:(t + 1) * 128], Wd[:, j, n * 512:(n + 1) * 512], start=(j == 0), stop=(j == NJ - 1)),
                                 r=["actT", "Wd"], w=["ps%d" % (pb + n)])
                    fps = ps[:, pb:pb + 2, :].rearrange("p a b -> p (a b)")
                    kps = ["ps%d" % pb, "ps%d" % (pb + 1)]
                    p.op("act", lambda e, fps=fps, i2=i2: e.activation(fj[i2], fps, AF.Square, accum_out=fs[i2][:, 0:1]), r=kps, w=["fs%d" % i2, "junk1"])
                    rms_rstd(fs[i2][:, 0:1], fr[i2][:, 0:1], 1, "fs%d" % i2, "fr%d" % i2, 1024.0)
                    p.op("dve", lambda e, fps=fps, i2=i2: e.scalar_tensor_tensor(t1[i2], fps, fr[i2][:, 0:1], gpost, op0=ALU.mult, op1=ALU.mult), r=kps + ["fr%d" % i2, kpost], w=["t1"])
                    p.op("pool", lambda e, i2=i2, t=t, src=src: e.tensor_tensor(src[:, t, :], src[:, t, :], t1[i2], op=ALU.add), r=["t1", skey], w=[skey])
                    pend_epi.append((g, t, src[:, t, :], skey))
            while pend_epi:
                epilogue(*pend_epi.pop(0))
            full_barrier()
            A.release(m)

        def dump(name, ap, keys):
            if name in dbg_out:
                p.op("sp", lambda e: e.dma_start(out=dbg_out[name], in_=ap), r=list(keys), w=["dbgo_" + name], dma="dbg")
        C.dump = dump
        C.ffn_phase = ffn_phase
        C.norm_transpose = norm_transpose
        C.alloc_norm_bufs = alloc_norm_bufs
        C.bcast_load = bcast_load
        C.full_barrier = full_barrier
        C.rms_rstd = rms_rstd
        C.U = U
        C.x, C.out, C.h1d, C.wb, C.vf, C.dbg_out = x, out, h1d, wb, vf, dbg_out
        C.cst, C.cosT, C.sinT, C.maskf, C.maskb, C.Uf32, C.Ub32, C.ones32 = cst, cosT, sinT, maskf, maskb, Uf32, Ub32, ones32
        C.gsub, C.k_gsub, C.lamt, C.aneg, C.dtb, C.dsk, C.k_dsk, C.cw, C.cb, C.cbh = gsub, k_gsub, lamt, aneg, dtb, dsk, k_dsk, cw, cb, cbh
        C.stages = stages

        for seq in range(nseq):
            run_sequence(C, seq)

        p.emit(st)
    return nc


def run_sequence(C, seq):
    p, A, ps = C.p, C.A, C.ps
    x, out, h1d, wb = C.x, C.out, C.h1d, C.wb
    U = C.U
    stages = C.stages
    tok0 = seq * S
    words_save = A.words
    seq_mark = A.mark()
    hnTm = A.alloc_top([8, S], BF16)
    kmix = "mixedT"
    khm = "hnTm"

    if "f1" in stages:
        m = A.mark()
        xt = [A.alloc([4, 1024], F32) for _ in range(2)]
        gmix, kgmix = C.bcast_load("mix_pre_g", 1024, key="gmix")
        nb2 = C.alloc_norm_bufs()

        def get_src(g):
            slot = g % 2
            k = "xt%d" % slot
            p.op("sp", lambda e: e.dma_start(out=xt[slot], in_=x[tok0 + g * G: tok0 + (g + 1) * G, :].rearrange("(t p) d -> p t d", p=128)), w=[k], dma=k)
            return xt[slot], k

        def epi(g, t, res, rkey):
            r0 = tok0 + g * G + t * 128
            p.op("sp", lambda e: e.dma_start(out=h1d[r0:r0 + 128, :], in_=res), r=[rkey], w=["h1d"], dma="h1st%d" % (g % 2))
            C.norm_transpose(res, rkey, gmix, kgmix, hnTm, khm, slice(g * G + t * 128, g * G + (t + 1) * 128), nb2)

        C.ffn_phase("ffn1", seq, get_src, epi)
        A.release(m)
    C.seq_mark = seq_mark
    C.words_save = words_save
    attnT = A.alloc([16, 8, 128], BF16)
    mixedT = attnT
    if "hn" in C.dbg_out and seq == 0:
        m = A.mark()
        tmp = A.alloc([8 * S // 4], F32)
        for q in range(4):
            p.op("dve", lambda e, q=q: e.tensor_copy(tmp, hnTm.rearrange("p c t -> p (c t)")[:, q * 4096:(q + 1) * 4096]), r=[khm], w=["dbgtmp"])
            p.op("sp", lambda e, q=q: e.dma_start(out=C.dbg_out["hn"][:, q * 4096:(q + 1) * 4096], in_=tmp), r=["dbgtmp"], w=["dbgo"], dma="dbg")
        C.full_barrier()
        A.release(m)

    if "attn" in stages:
        attention_phase(C, seq, mixedT, kmix, hnTm, khm)
    ssdT = A.alloc([16, 8, 128], BF16)
    C.attnT, C.ssdT = attnT, ssdT
    if "ssd" in stages:
        ssd_phase(C, seq, ssdT, kmix, hnTm, khm)
    for nm, tt in (("attnT", attnT), ("ssdT", ssdT)):
        if nm in C.dbg_out and seq == 0:
            m = A.mark()
            tmp = A.alloc([4096], F32)
            mf = tt.rearrange("p a b c -> p (a b c)")
            for q in range(4):
                p.op("dve", lambda e, q=q, mf=mf: e.tensor_copy(tmp, mf[:, q * 4096:(q + 1) * 4096]), r=[kmix + "x"] + [kmix + "a%d" % t for t in range(16)], w=["dbgtmp"])
                p.op("sp", lambda e, q=q, nm=nm: e.dma_start(out=C.dbg_out[nm][:, q * 4096:(q + 1) * 4096], in_=tmp), r=["dbgtmp"], w=["dbgo"], dma="dbg")
            C.full_barrier()
            A.release(m)
    A.words = words_save
    if "o" in stages:
        oproj_ffn2_phase(C, seq, mixedT, kmix)
    C.full_barrier()
    A.release(seq_mark)


def attention_phase(C, seq, mixedT, kmix, hnTm, khm):
    p, A, ps = C.p, C.A, C.ps
    ident = C.ident
    win = C.wb["w_in"].rearrange("(c p) n -> p c n", p=128)
    m = A.mark()
    QT = A.alloc([8, S], BF16)
    KT = A.alloc([8, S], BF16)
    Vp = A.alloc([16, 8, 130], BF16)
    m_in = A.mark()
    Wqkv = A.alloc([8, 1024], BF16)
    p.op("pool", lambda e: e.memset(Vp.rearrange("p a b c -> p (a b c)"), 1.0), w=["Vp"])
    qtok = [A.alloc([16, 64], BF16) for _ in range(2)]
    rta = [A.alloc([16, 8], F32) for _ in range(4)]
    rtb = [A.alloc([16, 8], F32) for _ in range(4)]
    for i in range(3):
        p.op("sp", lambda e, i=i: e.dma_start(out=Wqkv, in_=win[:, :, i * 1024:(i + 1) * 1024]), r=["wb_w_in"], w=["Wqkv"], dma="Wqkv")
        for t in range(16):
            cs_ = C.cosT[:, t, :].unsqueeze(1).to_broadcast([128, 16, 8])
            sn_ = C.sinT[:, t, :].unsqueeze(1).to_broadcast([128, 16, 8])
            b0 = 2 + 2 * (t % 3)
            for n in range(2):
                for c in range(8):
                    p.op("pe", lambda e, i=i, n=n, c=c, b0=b0, t=t: e.matmul(ps[:, b0 + n, :], hnTm[:, c, t * 128:(t + 1) * 128], Wqkv[:, c, n * 512:(n + 1) * 512], start=(c == 0), stop=(c == 7)),
                         r=[khm, "Wqkv"], w=["ps%d" % (b0 + n)])
            kps = ["ps%d" % b0, "ps%d" % (b0 + 1)]
            pv = ps[:, b0:b0 + 2, :].rearrange("p a b -> p (a b)")
            if i == 2:
                p.op("act", lambda e, pv=pv, t=t: e.activation(Vp[:, t, :, 0:128], pv.rearrange("p (h d) -> p h d", d=128), AF.Copy), r=kps, w=["Vp"])
                continue
            qv = pv.rearrange("p (h d) -> p h d", d=64)
            qt_ = qtok[i]
            kq = "qtok%d" % i
            ra, rb = rta[2 * i], rta[2 * i + 1]
            rc, rd = rtb[2 * i], rtb[2 * i + 1]
            p.op("dve", lambda e, qv=qv, ra=ra, cs_=cs_: e.tensor_tensor(ra, qv[:, :, 0:8], cs_, op=ALU.mult), r=kps + ["cst"], w=["ra%d" % i])
            p.op("dve", lambda e, qv=qv, rb=rb, sn_=sn_: e.tensor_tensor(rb, qv[:, :, 8:16], sn_, op=ALU.mult), r=kps + ["cst"], w=["rb%d" % i])
            p.op("dve", lambda e, qv=qv, rc=rc, cs_=cs_: e.tensor_tensor(rc, qv[:, :, 8:16], cs_, op=ALU.mult), r=kps + ["cst"], w=["rc%d" % i])
            p.op("dve", lambda e, qv=qv, rd=rd, sn_=sn_: e.tensor_tensor(rd, qv[:, :, 0:8], sn_, op=ALU.mult), r=kps + ["cst"], w=["rd%d" % i])
            p.op("pool", lambda e, qt_=qt_, ra=ra, rb=rb: e.tensor_tensor(qt_[:, :, 0:8], ra, rb, op=ALU.subtract), r=["ra%d" % i, "rb%d" % i], w=[kq])
            p.op("pool", lambda e, qt_=qt_, rc=rc, rd=rd: e.tensor_tensor(qt_[:, :, 8:16], rc, rd, op=ALU.add), r=["rc%d" % i, "rd%d" % i], w=[kq])
            p.op("act", lambda e, qt_=qt_, qv=qv: e.activation(qt_[:, :, 16:64], qv[:, :, 16:64], AF.Copy), r=kps, w=[kq])
            tb = t % 2
            psT = ps[:, tb, :].bitcast(BF16).rearrange("p (c t) -> p c t", c=8)
            qf = qt_.rearrange("p a b -> p (a b)")
            for h in range(8):
                p.op("pe", lambda e, h=h, psT=psT, qf=qf: e.transpose(psT[:, h, :], qf[:, h * 128:(h + 1) * 128], ident), r=[kq, "ident"], w=["ps%d" % tb])
            dst = QT if i == 0 else KT
            p.op("act", lambda e, dst=dst, psT=psT, t=t: e.activation(dst[:, :, t * 128:(t + 1) * 128], psT, AF.Copy), r=["ps%d" % tb], w=["QT" if i == 0 else "KT"])
    C.full_barrier()
    A.release(m_in)
    Qpad = [A.alloc([2, S], BF16) for _ in range(2)]
    for i_ in range(2):
        p.op("pool", lambda e, i_=i_: e.memset(Qpad[i_].rearrange("p a b -> p (a b)"), 0.0), w=["Qpad%d" % i_])
    NSB = 3
    SB = (0, 1, 6)
    PF = 2
    E = [A.alloc([512], BF16) for _ in range(NSB)]
    rr = [A.alloc([8], F32) for _ in range(2)]
    o1 = [A.alloc([128], F32) for _ in range(2)]
    o2 = [A.alloc([128], F32) for _ in range(2)]
    ss = [A.alloc([8], F32) for _ in range(2)]
    rs = [A.alloc([8], F32) for _ in range(2)]
    at = [A.alloc([128], BF16) for _ in range(2)]
    junk = A.alloc([128], BF16)
    import os as _os
    _nh = int(_os.environ.get("ATTN_HEADS_DBG", "8"))
    steps = [(h, qg, kt) for h in range(_nh) for qg in range(8) for kt in range(16)]

    def load_qpad(h):
        qp = Qpad[h % 2]
        kqp = "Qpad%d" % (h % 2)
        p.op("act", lambda e, qp=qp, h=h: e.activation(qp[0:64, 0, :], QT[0:64, h, :], AF.Copy), r=["QT"], w=[kqp])
        p.op("pool", lambda e, qp=qp, h=h: e.tensor_copy(qp[64:128, 1, :], QT[64:128, h, :]), r=["QT"], w=[kqp])

    def issue_qk(i):
        h, qg, kt = steps[i]
        sb = SB[i % NSB]
        qp = Qpad[h % 2]
        kqp = "Qpad%d" % (h % 2)
        for cm in range(2):
            p.op("pe", lambda e, sb=sb, cm=cm, h=h, kt=kt, qg=qg, qp=qp: e.matmul(ps[:, sb, cm * 256:(cm + 1) * 256], KT[:, h, kt * 128:(kt + 1) * 128], qp[:, cm, qg * 256:(qg + 1) * 256], start=True, stop=True),
                 r=["KT", kqp], w=["ps%d" % sb])

    if _nh > 0:
        load_qpad(0)
        if _nh > 1:
            load_qpad(1)
    for i in range(min(PF, len(steps))):
        issue_qk(i)
    ectr = 0
    deferred = []
    for i, (h, qg, kt) in enumerate(steps):
        accs = (2, 3) if ((h * 8 + qg) % 2 == 0) else (4, 5)
        sl = i % NSB
        sb = SB[sl]
        p.op("act", lambda e, sb=sb, sl=sl: e.activation(E[sl], ps[:, sb, :], AF.Exp, scale=0.125), r=["ps%d" % sb], w=["E%d" % sl])
        if i + PF < len(steps):
            issue_qk(i + PF)
        for cm in range(2):
            for qt in range(2):
                p.op("pe", lambda e, sl=sl, cm=cm, qt=qt, kt=kt, h=h, accs=accs: e.matmul(ps[:, accs[cm], qt * 130:(qt + 1) * 130], E[sl][:, cm * 256 + qt * 128: cm * 256 + (qt + 1) * 128], Vp[:, kt, h, 0:130], start=(kt == 0 and qt == 0), stop=(kt == 15), skip_group_check=True),
                     r=["E%d" % sl, "Vp"], w=["ps%d" % accs[cm]])
        if kt == 15 and qg == 7 and h + 2 < _nh:
            load_qpad(h + 2)
        if kt == 15:
            for qt in range(2):
                t = qg * 2 + qt
                i2 = ectr % 2
                ectr += 1
                ka = ["ps%d" % accs[0], "ps%d" % accs[1]]
                O1 = ps[:, accs[0], qt * 130: qt * 130 + 128]
                O2 = ps[:, accs[1], qt * 130: qt * 130 + 128]
                s1 = ps[:, accs[0], qt * 130 + 128: qt * 130 + 129]
                s2 = ps[:, accs[1], qt * 130 + 128: qt * 130 + 129]
                r_ = rr[i2]
                p.op("dve", lambda e, r_=r_, s1=s1: e.reciprocal(r_[:, 0:1], s1), r=ka, w=["rr%d" % i2])
                p.op("dve", lambda e, r_=r_, s2=s2: e.reciprocal(r_[:, 1:2], s2), r=ka, w=["rr%d" % i2])
                p.op("dve", lambda e, r_=r_: e.tensor_tensor(r_[:, 1:2], r_[:, 1:2], C.lamt[:, 3:4], op=ALU.mult), r=["rr%d" % i2, "lamt"], w=["rr%d" % i2])
                p.op("dve", lambda e, r_=r_, O1=O1, i2=i2: e.tensor_scalar(o1[i2], O1, r_[:, 0:1], None, op0=ALU.mult), r=ka + ["rr%d" % i2], w=["o1%d" % i2])
                p.op("dve", lambda e, r_=r_, O2=O2, i2=i2: e.scalar_tensor_tensor(o2[i2], O2, r_[:, 1:2], o1[i2], op0=ALU.mult, op1=ALU.add), r=ka + ["rr%d" % i2, "o1%d" % i2], w=["o2%d" % i2])

                def partB(i2=i2):
                    p.op("act", lambda e, i2=i2: e.activation(junk, o2[i2], AF.Square, accum_out=ss[i2][:, 0:1]), r=["o2%d" % i2], w=["ss%d" % i2, "junk_at"])

                def partC(i2=i2):
                    C.rms_rstd(ss[i2][:, 0:1], rs[i2][:, 0:1], 1, "ss%d" % i2, "rs%d" % i2, 128.0)
                    p.op("dve", lambda e, i2=i2: e.scalar_tensor_tensor(at[i2], o2[i2], rs[i2][:, 0:1], C.gsub, op0=ALU.mult, op1=ALU.mult), r=["o2%d" % i2, "rs%d" % i2, C.k_gsub], w=["at%d" % i2])

                def partD(i2=i2):
                    psT = ps[:, 7, i2 * 64:(i2 + 1) * 64].bitcast(BF16)
                    p.op("pe", lambda e, psT=psT, i2=i2: e.transpose(psT, at[i2], ident), r=["at%d" % i2, "ident"], w=["ps7"])

                def partE(i2=i2, t=t, h=h):
                    psT = ps[:, 7, i2 * 64:(i2 + 1) * 64].bitcast(BF16)
                    p.op("act", lambda e, psT=psT, t=t, h=h: e.activation(mixedT[:, t, h, :], psT, AF.Copy), r=["ps7"], w=[kmix + "a%d" % t])

                deferred.append((i + 4, partB))
                deferred.append((i + 6, partC))
                deferred.append((i + 9, partD))
                deferred.append((i + 11, partE))
                deferred.sort(key=lambda x_: x_[0])
        while deferred and deferred[0][0] <= i:
            deferred.pop(0)[1]()
    while deferred:
        deferred.pop(0)[1]()
    C.full_barrier()
    A.release(m)


def ssd_phase(C, seq, mixedT, kmix, hnTm, khm):
    p, A, ps = C.p, C.A, C.ps
    ident = C.ident
    win = C.wb["w_in"].rearrange("(c p) n -> p c n", p=128)
    m0 = A.mark()
    zs = A.alloc([16, 1024], BF16)
    BCT = A.alloc([4, S], BF16)
    dtt = A.alloc([16, 32], F32)
    dat = A.alloc([16, 32], F32)
    m1 = A.mark()
    Wz = A.alloc([8, 1024], BF16)
    p.op("sp", lambda e: e.dma_start(out=Wz, in_=win[:, :, 3072:4096]), r=["wb_w_in"], w=["Wz"], dma="Wz")
    th = [A.alloc([1024], F32) for _ in range(2)]
    for t in range(16):
        b0 = 2 + 2 * (t % 2)
        i2 = t % 2
        for n in range(2):
            for c in range(8):
                p.op("pe", lambda e, n=n, c=c, b0=b0, t=t: e.matmul(ps[:, b0 + n, :], hnTm[:, c, t * 128:(t + 1) * 128], Wz[:, c, n * 512:(n + 1) * 512], start=(c == 0), stop=(c == 7)),
                     r=[khm, "Wz"], w=["ps%d" % (b0 + n)])
        kps = ["ps%d" % b0, "ps%d" % (b0 + 1)]
        pv = ps[:, b0:b0 + 2, :].rearrange("p a b -> p (a b)")
        p.op("act", lambda e, pv=pv, i2=i2: e.activation(th[i2], pv, AF.Tanh, scale=0.5), r=kps, w=["zth%d" % i2])
        p.op("dve", lambda e, pv=pv, i2=i2, t=t: e.scalar_tensor_tensor(zs[:, t, :], th[i2], 1.0, pv, op0=ALU.add, op1=ALU.mult), r=kps + ["zth%d" % i2], w=["zs"])
    C.full_barrier()
    A.release(m1)
    import os as _os
    _lvl = int(_os.environ.get("SSD_DBG", "9"))
    if _lvl <= 0:
        C.full_barrier()
        A.release(m0)
        return
    Wxs = [A.alloc([8, 128], BF16) for _ in range(2)]
    Wdt = A.alloc([8, 32], BF16)
    p.op("sp", lambda e: e.dma_start(out=Wdt, in_=win[:, :, 5632:5664]), r=["wb_w_in"], w=["Wdt"], dma="Wdt")
    dps = ps[:, 6, :].rearrange("p (t k) -> p t k", k=32)
    for t in range(16):
        for c in range(8):
            p.op("pe", lambda e, c=c, t=t: e.matmul(dps[:, t, :], hnTm[:, c, t * 128:(t + 1) * 128], Wdt[:, c, :], start=(c == 0), stop=(c == 7)), r=[khm, "Wdt"], w=["ps6"])
    p.op("dve", lambda e: e.tensor_tensor(dtt, dps, C.dtb.unsqueeze(1).to_broadcast([128, 16, 32]), op=ALU.add), r=["ps6", "dtb"], w=["dtt"])
    dflat = dtt.rearrange("p a b -> p (a b)")
    _dtl = int(_os.environ.get("SSD_DT", "9"))
    if _dtl >= 2:
        p.op("act", lambda e, dflat=dflat: e.activation(dflat, dflat, AF.Exp), r=["dtt"], w=["dtt"])
    if _dtl >= 3:
        p.op("act", lambda e, dflat=dflat: e.activation(dflat, dflat, AF.Ln, bias=C.ones32[:, 0:1]), r=["dtt", "cst"], w=["dtt"])
    if _dtl >= 4:
        p.op("dve", lambda e: e.tensor_tensor(dat, dtt, C.aneg.unsqueeze(1).to_broadcast([128, 16, 32]), op=ALU.mult), r=["dtt", "aneg"], w=["dat"])
    C.dump("dtt", dtt.rearrange("p a b -> p (a b)"), ["dtt"])
    C.dump("dat", dat.rearrange("p a b -> p (a b)"), ["dat"])
    C.dump("zs", zs.rearrange("p a b -> p (a b)"), ["zs"])
    if _lvl <= 1:
        C.full_barrier()
        A.release(m0)
        return
    xpad = A.alloc([S + 4], F32)
    acc = A.alloc([S], F32)
    cth = A.alloc([S], F32)
    xsT = A.alloc([S], BF16)
    p.op("pool", lambda e: e.memset(xpad, 0.0), w=["xpad"])
    for k in range(12):
        Wx = Wxs[k % 2]
        kWx = "Wx%d" % (k % 2)
        p.op("sp", lambda e, k=k, Wx=Wx: e.dma_start(out=Wx, in_=win[:, :, 4096 + k * 128: 4096 + (k + 1) * 128]), r=["wb_w_in"], w=[kWx], dma=kWx)
        for tg in range(4):
            for c in range(8):
                p.op("pe", lambda e, Wx=Wx, tg=tg, c=c: e.matmul(ps[:, 2 + tg, :], Wx[:, c, :], hnTm[:, c, tg * 512:(tg + 1) * 512], start=(c == 0), stop=(c == 7)),
                     r=[khm, kWx], w=["ps%d" % (2 + tg)])
            p.op("act", lambda e, tg=tg: e.activation(xpad[:, 2 + tg * 512: 2 + (tg + 1) * 512], ps[:, 2 + tg, :], AF.Copy), r=["ps%d" % (2 + tg)], w=["xpad"])
        p.op("dve", lambda e, k=k: e.tensor_scalar(acc, xpad[:, 0:S], C.cw[:, k, 0:1], None, op0=ALU.mult), r=["xpad", "cw"], w=["acc"])
        for j in range(1, 5):
            p.op("dve", lambda e, k=k, j=j: e.scalar_tensor_tensor(acc, xpad[:, j:j + S], C.cw[:, k, j:j + 1], acc, op0=ALU.mult, op1=ALU.add), r=["xpad", "cw", "acc"], w=["acc"])
        p.op("act", lambda e, k=k: e.activation(cth, acc, AF.Tanh, bias=C.cbh[:, k:k + 1], scale=0.5), r=["acc", "cbh"], w=["cth"])
        p.op("dve", lambda e, k=k: e.tensor_scalar(acc, acc, C.cb[:, k:k + 1], 0.5, op0=ALU.add, op1=ALU.mult), r=["acc", "cb", "cth"], w=["acc"])
        if k < 8:
            p.op("dve", lambda e: e.scalar_tensor_tensor(xsT, cth, 1.0, acc, op0=ALU.add, op1=ALU.mult), r=["cth", "acc"], w=["xsT"])
            for tq in range(2):
                pb = tq
                psT = ps[:, pb, :].bitcast(BF16).rearrange("p (t c) -> p t c", t=8)
                for tt in range(8):
                    t = tq * 8 + tt
                    p.op("pe", lambda e, psT=psT, tt=tt, t=t: e.transpose(psT[:, tt, :], xsT[:, t * 128:(t + 1) * 128], ident), r=["xsT", "ident"], w=["ps%d" % pb])
                p.op("act", lambda e, psT=psT, tq=tq, k=k: e.activation(mixedT[:, tq * 8:(tq + 1) * 8, k, :], psT, AF.Copy), r=["ps%d" % pb], w=[kmix + "x"])
        else:
            p.op("dve", lambda e, k=k: e.scalar_tensor_tensor(BCT[:, k - 8, :], cth, 1.0, acc, op0=ALU.add, op1=ALU.mult), r=["cth", "acc"], w=["BCT"])
    C.full_barrier()
    A.release(m1)
    C.dump("BCT", BCT.rearrange("p a b -> p (a b)"), ["BCT"])
    C.dump("xs", mixedT.rearrange("p a b c -> p (a b c)"), [kmix + "x"])
    C.full_barrier()
    if _lvl <= 2:
        A.release(m0)
        return
    A.words = C.words_save
    ysum = A.alloc([16, 1024], BF16)
    csc = A.alloc([16], F32)
    ecol = A.alloc([16], F32)
    dend = A.alloc([16], F32)
    cd = A.alloc([16], F32)
    Ah = A.alloc([16, 128], F32)
    dec = A.alloc([16, 128], F32)
    MT = A.alloc([16, 128], BF16)
    CBm = A.alloc([2, 128], BF16)
    xdt = A.alloc([16, 64], BF16)
    xdw = A.alloc([16, 64], BF16)
    Btok = A.alloc([2, 128], BF16)
    prev = A.alloc([16, 64], F32)
    pbf = A.alloc([16, 64], BF16)
    yc = A.alloc([16, 64], F32)
    y2 = A.alloc([16, 64], F32)
    ssq = A.alloc([8], F32)
    rst = A.alloc([8], F32)
    ynb = A.alloc([1024], BF16)
    junk = A.alloc([1024], BF16)
    gssd, kgssd = C.bcast_load("ssd_norm_g", 1024, key="gssd")

    def xs_tile(c):
        return mixedT[:, c, :, :].rearrange("p k (a b) -> p (k a) b", a=2)

    for d in (1, 0):
        Ud = C.Ub32 if d == 1 else C.Uf32
        maskd = C.maskb if d == 1 else C.maskf
        p.op("pool", lambda e: e.memset(prev.rearrange("p a b -> p (a b)"), 0.0), w=["prev"])
        p.op("pool", lambda e: e.memset(pbf.rearrange("p a b -> p (a b)"), 0.0), w=["pbf"])
        order = range(15, -1, -1) if d == 1 else range(16)
        if _lvl <= 3:
            order = list(order)[:1]
        _nch = int(_os.environ.get("SSD_NCH", "16"))
        order = list(order)[:_nch]
        if _lvl <= 4 and d == 0:
            break
        order = list(order)

        def emit_Ah(c_, d=d, Ud=Ud):
            da_n = dat[:, c_, 16 * d:16 * d + 16]
            p.op("pool", lambda e, da_n=da_n, Ud=Ud: e.tensor_tensor(Ah, Ud.unsqueeze(1).to_broadcast([128, 16, 128]), da_n.unsqueeze(2).to_broadcast([128, 16, 128]), op=ALU.mult), r=["dat", "cst"], w=["Ah"])

        for ci, c in enumerate(order):
            da_c = dat[:, c, 16 * d:16 * d + 16]
            dt_c = dtt[:, c, 16 * d:16 * d + 16]
            xs = xs_tile(c)
            kx = kmix + "x"
            cols = slice(c * 128, (c + 1) * 128)
            p.op("pe", lambda e, da_c=da_c, Ud=Ud: e.matmul(ps[:, 0, 0:16], Ud, da_c, start=True, stop=True), r=["dat", "cst"], w=["ps0"])
            p.op("pe", lambda e, da_c=da_c: e.matmul(ps[:, 0, 16:32], C.ones32, da_c, start=True, stop=True), r=["dat", "cst"], w=["ps0"])
            p.op("dve", lambda e: e.tensor_copy(csc, ps[:, 0, 0:16]), r=["ps0"], w=["csc"])
            p.op("act", lambda e: e.activation(ecol, ps[:, 0, 0:16], AF.Exp), r=["ps0"], w=["ecol"])
            p.op("act", lambda e: e.activation(cd, ps[:, 0, 16:32], AF.Exp), r=["ps0"], w=["cd"])
            p.op("dve", lambda e: e.tensor_tensor(dend, ps[:, 0, 16:32], csc, op=ALU.subtract), r=["ps0", "csc"], w=["dend"])
            p.op("act", lambda e: e.activation(dend, dend, AF.Exp), r=["dend"], w=["dend"])
            if ci == 0:
                emit_Ah(c)
            for q in range(4):
                p.op("pe", lambda e, q=q: e.matmul(ps[:, 2 + q, :], C.ones32, Ah[:, 4 * q:4 * q + 4, :].rearrange("p a b -> p (a b)"), start=True, stop=True), r=["Ah", "cst"], w=["ps%d" % (2 + q)])
                p.op("dve", lambda e, q=q: e.tensor_tensor(dec[:, 4 * q:4 * q + 4, :], ps[:, 2 + q, :].rearrange("p (a b) -> p a b", a=4), csc[:, 4 * q:4 * q + 4].unsqueeze(2).to_broadcast([128, 4, 128]), op=ALU.subtract), r=["ps%d" % (2 + q), "csc"], w=["dec"])
            dflat = dec.rearrange("p a b -> p (a b)")
            p.op("dve", lambda e, dflat=dflat: e.tensor_scalar_min(dflat, dflat, 0.0), r=["dec"], w=["dec"])
            p.op("act", lambda e, dflat=dflat: e.activation(dflat, dflat, AF.Exp), r=["dec"], w=["dec"])
            for g in range(2):
                p.op("pe", lambda e, g=g, cols=cols: e.matmul(ps[:, 1, g * 128:(g + 1) * 128], BCT[:, g, cols], BCT[:, 2 + g, cols], start=True, stop=True), r=["BCT"], w=["ps1"])
            p.op("dve", lambda e, maskd=maskd: e.tensor_tensor(CBm, ps[:, 1, 0:256].rearrange("p (a b) -> p a b", a=2), maskd.unsqueeze(1).to_broadcast([128, 2, 128]), op=ALU.mult), r=["ps1", "maskf", "maskb"], w=["CBm"])
            for g in range(2):
                p.op("dve" if g == 0 else "pool", lambda e, g=g: e.tensor_tensor(MT[:, 8 * g:8 * g + 8, :], dec[:, 8 * g:8 * g + 8, :], CBm[:, g, :].unsqueeze(1).to_broadcast([128, 8, 128]), op=ALU.mult), r=["dec", "CBm"], w=["MT%d" % g])
            if ci + 1 < len(order):
                emit_Ah(order[ci + 1])
            p.op("dve", lambda e, xs=xs, dt_c=dt_c: e.tensor_tensor(xdt, xs, dt_c.unsqueeze(2).to_broadcast([128, 16, 64]), op=ALU.mult), r=[kx, "dtt"], w=["xdt"])
            p.op("pool", lambda e: e.tensor_tensor(xdw, xdt, dend.unsqueeze(2).to_broadcast([128, 16, 64]), op=ALU.mult), r=["xdt", "dend"], w=["xdw"])
            xdf = xdt.rearrange("p a b -> p (a b)")
            for h in range(16):
                p.op("pe", lambda e, h=h, xdf=xdf: e.matmul(ps[:, 6 + h // 8, (h % 8) * 64:(h % 8 + 1) * 64], MT[:, h, :], xdf[:, h * 64:(h + 1) * 64], start=True, stop=True), r=["MT%d" % (h // 8), "xdt"], w=["ps%d" % (6 + h // 8)])
            pbff = pbf.rearrange("p a b -> p (a b)")
            for g in range(2):
                p.op("pe", lambda e, g=g, cols=cols, pbff=pbff: e.matmul(ps[:, 2 + g, :], BCT[:, 2 + g, cols], pbff[:, g * 512:(g + 1) * 512], start=True, stop=True), r=["BCT", "pbf"], w=["ps%d" % (2 + g)])
            yo = ps[:, 2:4, :].rearrange("p a b -> p (a b)").rearrange("p (h d) -> p h d", d=64)
            yd = ps[:, 6:8, :].rearrange("p a b -> p (a b)").rearrange("p (h d) -> p h d", d=64)
            p.op("dve", lambda e, yo=yo: e.tensor_tensor(yc, yo, ecol.unsqueeze(2).to_broadcast([128, 16, 64]), op=ALU.mult), r=["ps2", "ps3", "ecol"], w=["yc"])
            p.op("dve", lambda e, yd=yd: e.tensor_tensor(yc, yd, yc, op=ALU.add), r=["ps6", "ps7", "yc"], w=["yc"])
            for g in range(2):
                psB = ps[:, 1, 256 + g * 64: 256 + (g + 1) * 64].bitcast(BF16)
                p.op("pe", lambda e, g=g, cols=cols, psB=psB: e.transpose(psB, BCT[:, g, cols], ident), r=["BCT", "ident"], w=["ps1"])
            p.op("act", lambda e: e.activation(Btok, ps[:, 1, 256:384].bitcast(BF16).rearrange("p (a b) -> p a b", a=2), AF.Copy), r=["ps1"], w=["Btok"])
            xwf = xdw.rearrange("p a b -> p (a b)")
            for g in range(2):
                p.op("pe", lambda e, g=g, xwf=xwf: e.matmul(ps[:, 4 + g, :], Btok[:, g, :], xwf[:, g * 512:(g + 1) * 512], start=True, stop=True), r=["Btok", "xdw"], w=["ps%d" % (4 + g)])
            st_ = ps[:, 4:6, :].rearrange("p a b -> p (a b)").rearrange("p (h d) -> p h d", d=64)
            p.op("pool", lambda e: e.tensor_tensor(prev, prev, cd.unsqueeze(2).to_broadcast([128, 16, 64]), op=ALU.mult), r=["prev", "cd"], w=["prev"])
            p.op("dve", lambda e, st_=st_: e.tensor_tensor(prev, prev, st_, op=ALU.add), r=["prev", "ps4", "ps5"], w=["prev"])
            p.op("act", lambda e: e.activation(pbf, prev, AF.Copy), r=["prev"], w=["pbf"])
            if d == 1 and c == 15:
                C.dump("csc", csc, ["csc"]); C.dump("ecol", ecol, ["ecol"]); C.dump("dend", dend, ["dend"]); C.dump("cd", cd, ["cd"])
                C.dump("dec", dec.rearrange("p a b -> p (a b)"), ["dec"]); C.dump("MT", MT.rearrange("p a b -> p (a b)"), ["MT0", "MT1"])
                C.dump("yc", yc.rearrange("p a b -> p (a b)"), ["yc"]); C.dump("prev", prev.rearrange("p a b -> p (a b)"), ["prev"])
                C.dump("xdt", xdt.rearrange("p a b -> p (a b)"), ["xdt"]); C.dump("CBm", CBm.rearrange("p a b -> p (a b)"), ["CBm"])
            if d == 1:
                p.op("act", lambda e, c=c: e.activation(ysum[:, c, :], yc.rearrange("p a b -> p (a b)"), AF.Copy), r=["yc"], w=["ysum"])
            else:
                _ol = int(_os.environ.get("SSD_OUT", "9"))
                if _ol >= 1:
                    p.op("pool", lambda e, c=c: e.tensor_tensor(yc, yc, ysum[:, c, :].rearrange("p (a b) -> p a b", a=16), op=ALU.add), r=["yc", "ysum"], w=["yc"])
                    p.op("pool", lambda e, xs=xs: e.tensor_tensor(y2, xs, C.dsk.unsqueeze(2).to_broadcast([128, 16, 64]), op=ALU.mult), r=[kx, C.k_dsk], w=["y2"])
                    p.op("pool", lambda e: e.tensor_tensor(yc, yc, y2, op=ALU.add), r=["yc", "y2"], w=["yc"])
                ycf = yc.rearrange("p a b -> p (a b)")
                if _ol >= 2:
                    p.op("dve", lambda e, c=c, ycf=ycf: e.tensor_tensor(ycf, ycf, zs[:, c, :], op=ALU.mult), r=["yc", "zs"], w=["yc"])
                if _ol >= 3:
                    p.op("act", lambda e, ycf=ycf: e.activation(junk, ycf, AF.Square, accum_out=ssq[:, 0:1]), r=["yc"], w=["ssq", "junk_ssd"])
                    p.op("dve", lambda e: e.tensor_scalar(rst[:, 0:1], ssq[:, 0:1], 1.0 / 1024, 4.0 * EPS, op0=ALU.mult, op1=ALU.add), r=["ssq"], w=["rst"])
                    p.op("pool", lambda e: e.tensor_tensor(rst[:, 0:1], rst[:, 0:1], C.negh[:, 0:1], op=ALU.pow), r=["rst", "negh"], w=["rst"])
                if _ol >= 4:
                    p.op("dve", lambda e, ycf=ycf, c=c: e.scalar_tensor_tensor(ysum[:, c, :], ycf, rst[:, 0:1], gssd, op0=ALU.mult, op1=ALU.mult), r=["yc", "rst", kgssd, "ysum"], w=["ysum"])
    C.full_barrier()
    for c in range(16):
        bank = c % 2
        psT = ps[:, bank, :].bitcast(BF16).rearrange("p (k t) -> p k t", k=8)
        for k in range(8):
            p.op("pe", lambda e, k=k, psT=psT, c=c: e.transpose(psT[:, k, :], ysum[:, c, k * 128:(k + 1) * 128], ident), r=["ysum", "ident"], w=["ps%d" % bank])
        p.op("act", lambda e, c=c, psT=psT: e.activation(mixedT[:, c, :, :], psT, AF.Copy), r=["ps%d" % bank], w=[kmix + "x"])
    C.full_barrier()
    A.release(m0)


def oproj_ffn2_phase(C, seq, mixedT, kmix):
    p, A, ps = C.p, C.A, C.ps
    tok0 = seq * S
    words_save = A.words
    h2 = A.alloc_top([16, 1024], F32)
    m = A.mark()
    Wout = A.alloc([16, 1024], BF16)
    p.op("sp", lambda e: e.dma_start(out=Wout, in_=C.wb["w_out"].rearrange("(k p) n -> p k n", p=128)), r=["wb_w_out"], w=["Wout"], dma="Wout")
    gmp, kgmp = C.bcast_load("mix_post_g", 1024, key="gmp")
    t1 = A.alloc([1024], F32)
    junk = A.alloc([1024], BF16)
    ss = A.alloc([8], F32)
    rs = A.alloc([8], F32)
    allmix = [kmix + "x"] + [kmix + "a%d" % t for t in range(16)]
    for t in range(16):
        r0 = tok0 + t * 128
        p.op("sp", lambda e, t=t, r0=r0: e.dma_start(out=h2[:, t, :], in_=C.h1d[r0:r0 + 128, :]), r=["h1d"], w=["h2_%d" % (t // 4)], dma="h2ld")
        b0 = 2 + 2 * (t % 3)
        for n in range(2):
            for kc in range(16):
                p.op("pe", lambda e, t=t, n=n, kc=kc, b0=b0: e.matmul(ps[:, b0 + n, :], (C.attnT[:, t, kc, :] if kc < 8 else C.ssdT[:, t, kc - 8, :]), Wout[:, kc, n * 512:(n + 1) * 512], start=(kc == 0), stop=(kc == 15)),
                     r=allmix + ["Wout"], w=["ps%d" % (b0 + n)])
        kps = ["ps%d" % b0, "ps%d" % (b0 + 1)]
        fps = ps[:, b0:b0 + 2, :].rearrange("p a b -> p (a b)")
        p.op("act", lambda e, fps=fps: e.activation(junk, fps, AF.Square, accum_out=ss[:, 0:1]), r=kps, w=["oss", "junk_o"])
        C.rms_rstd(ss[:, 0:1], rs[:, 0:1], 1, "oss", "ors", 1024.0)
        p.op("dve", lambda e, fps=fps: e.scalar_tensor_tensor(t1, fps, rs[:, 0:1], gmp, op0=ALU.mult, op1=ALU.mult), r=kps + ["ors", kgmp], w=["ot1"])
        p.op("pool", lambda e, t=t: e.tensor_tensor(h2[:, t, :], h2[:, t, :], t1, op=ALU.add), r=["ot1", "h2_%d" % (t // 4)], w=["h2_%d" % (t // 4)])
    C.full_barrier()
    A.release(C.seq_mark)
    if "f2" in C.stages:
        gfin, kgfin = C.bcast_load("final_g", 1024, key="gfin")
        ot1 = A.alloc([1024], F32)
        ot = [ot1, ot1]
        fss = A.alloc([8], F32)
        frs = A.alloc([8], F32)
        junk2 = A.alloc([1024], BF16)

        def get_src(g):
            return h2[:, 4 * g:4 * g + 4, :], "h2_%d" % g

        def epi(g, t, res, rkey):
            i2 = t % 2
            r0 = tok0 + g * G + t * 128
            p.op("act", lambda e: e.activation(junk2, res, AF.Square, accum_out=fss[:, 0:1]), r=[rkey], w=["fss", "junk_f"])
            C.rms_rstd(fss[:, 0:1], frs[:, 0:1], 1, "fss", "frs", 1024.0)
            p.op("dve", lambda e: e.scalar_tensor_tensor(ot[i2], res, frs[:, 0:1], gfin, op0=ALU.mult, op1=ALU.mult), r=[rkey, "frs", kgfin], w=["otf"])
            p.op("sp", lambda e: e.dma_start(out=C.out[r0:r0 + 128, :], in_=ot[i2]), r=["otf"], w=["outd"], dma="ost")

        C.ffn_phase("ffn2", seq, get_src, epi)
    A.words = words_save


_WNAMES = ["ffn1_w_gate", "ffn1_w_up", "ffn1_w_down", "ffn2_w_gate", "ffn2_w_up", "ffn2_w_down", "w_in", "w_out"]
_VNAMES = ["ffn1_pre_g", "ffn1_post_g", "mix_pre_g", "mix_post_g", "ffn2_pre_g", "ffn2_post_g", "final_g", "ssd_norm_g",
           "attn_subln_g", "lambda_q1", "lambda_k1", "lambda_q2", "lambda_k2", "a_log_fwd", "a_log_bwd",
           "dt_bias_fwd", "dt_bias_bwd", "d_skip"]


def make_in_map(inp, xs):
    m = {"x": np.ascontiguousarray(xs, dtype=np.float32), "consts": _const_pack()}
    for n in _WNAMES:
        m[n] = np.ascontiguousarray(np.asarray(inp[n])[0], dtype=np.float32)
    for n in _VNAMES:
        m[n] = np.ascontiguousarray(np.asarray(inp[n]).reshape(1, -1), dtype=np.float32)
    cwt = np.asarray(inp["conv_w"])[0].T.reshape(12, 128, 5).transpose(1, 0, 2).reshape(128, 60)
    m["conv_w"] = np.ascontiguousarray(cwt, dtype=np.float32)
    m["conv_b"] = np.ascontiguousarray(np.asarray(inp["conv_b"]).reshape(12, 128).T, dtype=np.float32)
    return m


_CACHE = {}


def kernel(**inputs):
    x = np.asarray(inputs["x"], dtype=np.float32)
    B = x.shape[0]
    per = B // NCORES
    if "nc" not in _CACHE:
        _CACHE["nc"] = build_program(per)
    nc = _CACHE["nc"]
    in_maps = [make_in_map(inputs, x[i * per:(i + 1) * per].reshape(per * S, D)) for i in range(NCORES)]
    res = run_bass_kernel_spmd(nc, in_maps, core_ids=list(range(NCORES)))
    outs = [np.asarray(r["out"]).reshape(per, S, D) for r in res.results]
    return np.concatenate(outs, axis=0).astype(np.float32)
```
